# Optimizing a Trainium2 kernel written in Bass

```python
import jax, jax.numpy as jnp
from jax import lax
import numpy as np

D_MODEL = 1024
BATCH = 4
SEQ = 8192
DEPTH = 1

POOL_WINDOWS = (2, 4, 8, 16)
POOL_GROUPS = len(POOL_WINDOWS)
POOL_GROUP_DIM = D_MODEL // 16
POOL_WIDTH = POOL_GROUPS * POOL_GROUP_DIM
ATTN_PATTERNS = ((128, 1), (512, 4), (2048, 16))
N_ATTN_GROUPS = len(ATTN_PATTERNS)
HEAD_DIM = 64
HEADS_PER_GROUP = 4
N_ATTN_HEADS = N_ATTN_GROUPS * HEADS_PER_GROUP
ATTN_WIDTH = N_ATTN_HEADS * HEAD_DIM
ATTN_OUT_WIDTH = HEADS_PER_GROUP * HEAD_DIM
ROPE_THETA = 10000.0
BLOCK = 128
N_BRANCHES = 2
IN_WIDTH = POOL_WIDTH + 3 * ATTN_WIDTH + N_BRANCHES * D_MODEL
D_FF = ((8 * D_MODEL // 3 + 255) // 256) * 256
N_MOD = 9
EPS = 1e-6

kernel_name = "hybrid_pool_dilated_attn_macaron_block"


def rmsnorm(x, g):
    xf = x.astype(jnp.float32)
    y = xf * lax.rsqrt(jnp.mean(xf * xf, axis=-1, keepdims=True) + EPS)
    return (y * g.astype(jnp.float32)).astype(x.dtype)


def modulate(x, shift, scale):
    return x * (1 + scale) + shift


def swiglu(u, w_in, w_out):
    a, b = jnp.split(u @ w_in, 2, axis=-1)
    return (jax.nn.silu(a) * b) @ w_out


def rope_tables(positions, dtype):
    inv_freq = ROPE_THETA ** (-jnp.arange(0, HEAD_DIM, 2, dtype=jnp.float32) / HEAD_DIM)
    ang = positions.astype(jnp.float32)[..., None] * inv_freq
    return jnp.cos(ang)[:, :, None, :].astype(dtype), jnp.sin(ang)[:, :, None, :].astype(dtype)


def apply_rope(t, cos, sin):
    t1, t2 = jnp.split(t, 2, axis=-1)
    return jnp.concatenate([t1 * cos - t2 * sin, t2 * cos + t1 * sin], axis=-1)


def multiscale_pool(p, w_pool, pool_scale):
    B, S, _ = p.shape
    pf = p.astype(jnp.float32)
    cs = jnp.pad(jnp.cumsum(pf, axis=1), ((0, 0), (1, 0), (0, 0)))
    t = jnp.arange(S)
    outs = []
    for gi, w in enumerate(POOL_WINDOWS):
        sl = slice(gi * POOL_GROUP_DIM, (gi + 1) * POOL_GROUP_DIM)
        csg = cs[..., sl]
        lagged = jnp.pad(csg[:, :S + 1 - w], ((0, 0), (w - 1, 0), (0, 0)))
        count = jnp.minimum(t + 1, w).astype(jnp.float32)[None, :, None]
        outs.append((csg[:, 1:] - lagged) / count - pf[..., sl])
    d = jnp.stack(outs, axis=2).astype(p.dtype)
    y = jnp.einsum('bsgc,gcd->bsgd', d, w_pool)
    return y.reshape(B, S, POOL_WIDTH) * pool_scale


def dilated_window_attention(q, k, v, steps, dilation):
    B, S, H, Dh = q.shape
    L = S // dilation
    nb = -(-L // BLOCK)
    Lp = nb * BLOCK

    def to_strided(t):
        t = t.reshape(B, L, dilation, H, Dh).transpose(0, 3, 2, 1, 4)
        return jnp.pad(t, ((0, 0), (0, 0), (0, 0), (0, Lp - L), (0, 0)))

    qs, ks, vs = to_strided(q), to_strided(k), to_strided(v)
    qb = qs.reshape(B, H, dilation, nb, BLOCK, Dh)

    def band(t):
        tp = jnp.pad(t, ((0, 0), (0, 0), (0, 0), (BLOCK, 0), (0, 0)))
        prev = tp[..., :Lp, :].reshape(B, H, dilation, nb, BLOCK, Dh)
        cur = t.reshape(B, H, dilation, nb, BLOCK, Dh)
        return jnp.concatenate([prev, cur], axis=-2)

    kb, vb = band(ks), band(vs)
    a = jnp.arange(BLOCK)[:, None]
    cidx = jnp.arange(2 * BLOCK)[None, :]
    rel = a + BLOCK - cidx
    in_band = (rel >= 0) & (rel <= steps)
    key_pos = jnp.arange(nb)[:, None, None] * BLOCK - BLOCK + cidx[None]
    mask = in_band[None] & (key_pos >= 0)

    s = jnp.einsum('bhrnqd,bhrnkd->bhrnqk', qb, kb).astype(jnp.float32) * (HEAD_DIM ** -0.5)
    s = jnp.where(mask, s, -jnp.inf)
    lse = jax.nn.logsumexp(s, axis=-1)
    pr = jnp.exp(s - lse[..., None]).astype(v.dtype)
    o = jnp.einsum('bhrnqk,bhrnkd->bhrnqd', pr, vb)
    o = o.reshape(B, H, dilation, Lp, Dh)[:, :, :, :L]
    o = o.transpose(0, 3, 2, 1, 4).reshape(B, S, H, Dh)
    lse = lse.reshape(B, H, dilation, Lp)[:, :, :, :L].transpose(0, 3, 2, 1).reshape(B, S, H)
    return o, lse


def token_mixing(u, cos, sin, w_in, w_pool, pool_scale, w_pool_branch, w_attn_branch, w_out):
    B, S, _ = u.shape
    proj = u @ w_in
    cuts = [POOL_WIDTH, POOL_WIDTH + ATTN_WIDTH, POOL_WIDTH + 2 * ATTN_WIDTH,
            POOL_WIDTH + 3 * ATTN_WIDTH]
    p, q, k, v, gate_logits = jnp.split(proj, cuts, axis=-1)

    y_pool = multiscale_pool(p, w_pool, pool_scale)

    q = apply_rope(q.reshape(B, S, N_ATTN_HEADS, HEAD_DIM), cos, sin)
    k = apply_rope(k.reshape(B, S, N_ATTN_HEADS, HEAD_DIM), cos, sin)
    v = v.reshape(B, S, N_ATTN_HEADS, HEAD_DIM)
    outs, lses = [], []
    for gi, (window, dilation) in enumerate(ATTN_PATTERNS):
        hs = slice(gi * HEADS_PER_GROUP, (gi + 1) * HEADS_PER_GROUP)
        o, lse = dilated_window_attention(q[:, :, hs], k[:, :, hs], v[:, :, hs],
                                          window // dilation, dilation)
        outs.append(o)
        lses.append(lse)
    wts = jax.nn.softmax(jnp.stack(lses, axis=0), axis=0)
    y_attn = jnp.einsum('gbsh,gbshd->bshd', wts.astype(v.dtype), jnp.stack(outs, axis=0))
    y_attn = y_attn.reshape(B, S, ATTN_OUT_WIDTH)

    gates = jax.nn.sigmoid(gate_logits.astype(jnp.float32)).astype(u.dtype)
    g_pool, g_attn = jnp.split(gates, N_BRANCHES, axis=-1)
    merged = g_pool * (y_pool @ w_pool_branch) + g_attn * (y_attn @ w_attn_branch)
    return merged @ w_out


def setup_inputs(seed: int = 0) -> dict:
    key = jax.random.key(seed)
    ks = jax.random.split(key, 20)
    f32 = jnp.float32

    def lin(k, shape, fan_in, scale=1.0):
        return jax.random.normal(k, shape, f32) * (scale * fan_in ** -0.5)

    def gain(k, shape):
        return 1.0 + 0.05 * jax.random.normal(k, shape, f32)

    x = jax.random.normal(ks[0], (BATCH, SEQ, D_MODEL), f32)
    c = jax.random.normal(ks[1], (BATCH, D_MODEL), f32)
    offset = jax.random.randint(ks[2], (BATCH, 1), 0, 1024, dtype=jnp.int32)
    positions = (jnp.arange(SEQ, dtype=jnp.int32)[None, :] + offset).astype(jnp.int32)
    return {
        "x": x,
        "c": c,
        "positions": positions,
        "w_ada": lin(ks[3], (DEPTH, D_MODEL, N_MOD * D_MODEL), D_MODEL, 0.5),
        "b_ada": 0.02 * jax.random.normal(ks[4], (DEPTH, N_MOD * D_MODEL), f32),
        "g_norm_ffn1": gain(ks[5], (DEPTH, D_MODEL)),
        "w_ffn1_in": lin(ks[6], (DEPTH, D_MODEL, 2 * D_FF), D_MODEL),
        "w_ffn1_out": lin(ks[7], (DEPTH, D_FF, D_MODEL), D_FF),
        "g_norm_mix": gain(ks[8], (DEPTH, D_MODEL)),
        "w_in": lin(ks[9], (DEPTH, D_MODEL, IN_WIDTH), D_MODEL),
        "w_pool": lin(ks[10], (DEPTH, POOL_GROUPS, POOL_GROUP_DIM, POOL_GROUP_DIM), POOL_GROUP_DIM),
        "pool_scale": 1.0 + 0.1 * jax.random.normal(ks[11], (DEPTH, POOL_WIDTH), f32),
        "w_pool_branch": lin(ks[12], (DEPTH, POOL_WIDTH, D_MODEL), POOL_WIDTH),
        "w_attn_branch": lin(ks[13], (DEPTH, ATTN_OUT_WIDTH, D_MODEL), ATTN_OUT_WIDTH),
        "w_out": lin(ks[14], (DEPTH, D_MODEL, D_MODEL), D_MODEL),
        "g_norm_ffn2": gain(ks[15], (DEPTH, D_MODEL)),
        "w_ffn2_in": lin(ks[16], (DEPTH, D_MODEL, 2 * D_FF), D_MODEL),
        "w_ffn2_out": lin(ks[17], (DEPTH, D_FF, D_MODEL), D_FF),
        "g_final": gain(ks[18], (D_MODEL,)),
    }


def reference(x, c, positions, w_ada, b_ada, g_norm_ffn1, w_ffn1_in, w_ffn1_out,
              g_norm_mix, w_in, w_pool, pool_scale, w_pool_branch, w_attn_branch, w_out,
              g_norm_ffn2, w_ffn2_in, w_ffn2_out, g_final):
    cos, sin = rope_tables(positions, x.dtype)
    cond = jax.nn.silu(c)
    h = x
    for l in range(DEPTH):
        mod = cond @ w_ada[l] + b_ada[l]
        sh1, sc1, gt1, sh2, sc2, gt2, sh3, sc3, gt3 = [
            m[:, None, :] for m in jnp.split(mod, N_MOD, axis=-1)]
        u = modulate(rmsnorm(h, g_norm_ffn1[l]), sh1, sc1)
        h = h + 0.5 * gt1 * swiglu(u, w_ffn1_in[l], w_ffn1_out[l])
        u = modulate(rmsnorm(h, g_norm_mix[l]), sh2, sc2)
        h = h + gt2 * token_mixing(u, cos, sin, w_in[l], w_pool[l], pool_scale[l],
                                   w_pool_branch[l], w_attn_branch[l], w_out[l])
        u = modulate(rmsnorm(h, g_norm_ffn2[l]), sh3, sc3)
        h = h + 0.5 * gt3 * swiglu(u, w_ffn2_in[l], w_ffn2_out[l])
    return rmsnorm(h, g_final)
```

```python
import os
from contextlib import ExitStack
import numpy as np
import concourse.bass as bass
import concourse.mybir as mybir
from concourse.bass_utils import run_bass_kernel_spmd

F32 = mybir.dt.float32
BF16 = mybir.dt.bfloat16
I32 = mybir.dt.int32
AF = mybir.ActivationFunctionType
ALU = mybir.AluOpType

D = 1024
DFF = 2816
NF = 22
OWN = 4096
HALO = 2048
NT = OWN + HALO
ST = 1024
RING = (1536, 1536, 3072)
DIL = (1, 4, 16)
NVB = (12, 12, 24)
EPS = 1e-6
NEG = -30000.0
TWO_PI_HI = 6.28125
TWO_PI_LO = 2.0 * np.pi - 6.28125


class _Stop(Exception):
    pass


class Sched:
    def __init__(self, nc, stack, n_dma=12):
        self.nc = nc
        self.eng = {"pe": nc.tensor, "act": nc.scalar, "dve": nc.vector, "pool": nc.gpsimd, "sp": nc.sync}
        self.semh = {}
        self.cnt = {}
        for e in ("pe", "act", "dve", "pool"):
            self.semh[e] = stack.enter_context(nc.semaphore("s_" + e))
            self.cnt[e] = 0
        self.dma_slots = {}
        for q in ("sp", "pool", "act"):
            sl = []
            for i in range(n_dma):
                k = "d_%s%d" % (q, i)
                self.semh[k] = stack.enter_context(nc.semaphore(k))
                sl.append([k, 0])
            self.dma_slots[q] = sl
        self.rr = {"sp": 0, "pool": 0, "act": 0}
        self.waited = {e: {} for e in self.eng}
        self.last_w = {}
        self.readers = {}
        self.off = False
        self.deferred = None
        self.pending = []

    def _wait(self, e, semk, val):
        w = self.waited[e]
        if w.get(semk, 0) >= val:
            return
        self.eng[e].wait_ge(self.semh[semk], val)
        w[semk] = val

    def _deps(self, e, reads, writes):
        need = {}

        def add(t):
            if t is not None and need.get(t[0], 0) < t[1]:
                need[t[0]] = t[1]

        for r in reads:
            add(self.last_w.get(r))
        for w in writes:
            add(self.last_w.get(w))
            for k, v in self.readers.get(w, {}).items():
                add((k, v))
        for k, v in need.items():
            if e == "pe" and k == "pe":
                continue
            self._wait(e, k, v)

    def _record(self, tok, reads, writes):
        for r in reads:
            d = self.readers.setdefault(r, {})
            if d.get(tok[0], 0) < tok[1]:
                d[tok[0]] = tok[1]
        for w in writes:
            self.last_w[w] = tok
            self.readers[w] = {}

    def flush(self, n=None):
        q, self.deferred = self.deferred, None
        k = 0
        while q and (n is None or k < n):
            kind, args = q.pop(0)
            (self.op if kind == "op" else self.dma)(*args)
            k += 1
        self.deferred = None
        self.pending = q
        return q

    def op(self, e, fn, reads=(), writes=()):
        if self.off:
            return None
        if self.deferred is not None:
            self.deferred.append(("op", (e, fn, tuple(reads), tuple(writes))))
            return None
        self._deps(e, reads, writes)
        inst = fn(self.eng[e])
        self.cnt[e] += 1
        inst.then_inc(self.semh[e], 1)
        tok = (e, self.cnt[e])
        self._record(tok, reads, writes)
        return tok

    def dma(self, q, out, in_, reads=(), writes=()):
        if self.off:
            return None
        if self.deferred is not None:
            self.deferred.append(("dma", (q, out, in_, tuple(reads), tuple(writes))))
            return None
        self._deps(q, reads, writes)
        sl = self.dma_slots[q]
        slot = sl[self.rr[q] % len(sl)]
        self.rr[q] += 1
        if slot[1] > 0:
            self._wait(q, slot[0], 16 * slot[1])
        self.eng[q].dma_start(out=out, in_=in_).then_inc(self.semh[slot[0]], 16)
        slot[1] += 1
        tok = (slot[0], 16 * slot[1])
        self._record(tok, reads, writes)
        return tok

    def drain(self, keys):
        if self.off:
            return
        need = {}
        for k in keys:
            t = self.last_w.get(k)
            if t is not None and need.get(t[0], 0) < t[1]:
                need[t[0]] = t[1]
            for sk, v in self.readers.get(k, {}).items():
                if need.get(sk, 0) < v:
                    need[sk] = v
        for e in self.eng:
            for sk, v in need.items():
                self._wait(e, sk, v)

    def barrier(self):
        if self.off:
            return
        tot = {e: self.cnt[e] for e in self.cnt}
        for q in self.dma_slots:
            for k, u in self.dma_slots[q]:
                tot[k] = 16 * u
        for e in self.eng:
            for k, v in tot.items():
                if v > 0:
                    self._wait(e, k, v)
        self.last_w = {}
        self.readers = {}


def build_program(dbg=False):
    nc = bass.Bass("TRN2", target_bir_lowering=False)

    def din(name, shape, dt=F32):
        return nc.dram_tensor(name, list(shape), dt, kind="ExternalInput").ap()

    x = din("x", [NT, D])
    pos = din("pos", [1, NT], I32)
    cT = din("cT", [128, 8])
    w_ada = din("w_ada", [D, 9 * D])
    b_ada = din("b_ada", [1, 9 * D])
    b_adaT = din("b_adaT", [128, 72])
    gT_in = din("gT", [128, 24])
    g_final = din("g_final", [1, D])
    w1a = din("w_ffn1_in", [D, 2 * DFF])
    w1b = din("w_ffn1_out", [DFF, D])
    w3a = din("w_ffn2_in", [D, 2 * DFF])
    w3b = din("w_ffn2_out", [DFF, D])
    w_in = din("w_in", [D, 4608])
    perm_in = din("permT", [128, 128])
    w_pool_bd = din("w_pool_bd", [128, 2, 128])
    pool_scaleT = din("pool_scaleT", [128, 2])
    w_pb = din("w_pb", [256, D])
    w_ab = din("w_ab", [256, D])
    w_o = din("w_o", [D, D])
    ident_in = din("ident", [128, 128])
    masks_in = din("masks", [128, 5 * 4 * 128])
    ropef = din("ropef", [128, 2])
    flags = din("flags", [128, 2])
    rcnt_in = din("rcnt", [128, 32])
    NITEM = 26
    wbf = nc.dram_tensor("wbf", [NITEM, 128, 2048], BF16, kind="Internal").ap()
    wada_bf = nc.dram_tensor("wada_bf", [6, 128, 8 * D], BF16, kind="Internal").ap()
    w3a_bf = nc.dram_tensor("w3a_bf", [22, 128, 2048], BF16, kind="Internal").ap()
    w3b_bf = nc.dram_tensor("w3b_bf", [11, 128, 2048], BF16, kind="Internal").ap()
    h1s = nc.dram_tensor("h1s", [NT, D], F32, kind="Internal").ap()
    h2s = nc.dram_tensor("h2s", [OWN, D], F32, kind="Internal").ap()
    y = nc.dram_tensor("y", [OWN, D], F32, kind="ExternalOutput").ap()

    with ExitStack() as top:
        S = Sched(nc, top)
        uid = [0]

        def sb(st, name, shape, dt):
            uid[0] += 1
            return st.enter_context(nc.sbuf_tensor("sb%d_%s" % (uid[0], name), list(shape), dt))
        TB = [top.enter_context(nc.psum_tensor("tb%d" % i, [128, 1024], BF16)) for i in range(2)]
        PB = [top.enter_context(nc.psum_tensor("pb%d" % i, [128, 512], F32)) for i in range(6)]
        rot = {"t": 0, "p": 0}

        def tbank():
            i = rot["t"] % 2
            rot["t"] += 1
            return TB[i], ("tb", i)

        def pbank(lo=0, hi=6):
            k = "p%d_%d" % (lo, hi)
            i = lo + rot.get(k, 0) % (hi - lo)
            rot[k] = rot.get(k, 0) + 1
            return PB[i], ("pb", i)

        ident = sb(top, "ident", [128, 128], BF16)
        onesb = sb(top, "onesb", [128, 128], BF16)
        condT = sb(top, "condT", [128, 8], BF16)
        cf = sb(top, "cf", [128, 8], F32)
        badT = sb(top, "badT", [128, 72], F32)
        gT = sb(top, "gT", [128, 24], F32)
        modT = sb(top, "modT", [128, 16], F32)
        modA = sb(top, "modA", [128, 8], F32)
        gtb = sb(top, "gtb", [128, D], F32)
        flg = sb(top, "flg", [128, 2], F32)

        S.dma("pool", ident[:], ident_in[:, :], writes=["ident"])
        S.dma("sp", cf[:], cT[:, :], writes=["cf"])
        S.dma("sp", badT[:], b_adaT[:, :], writes=["badT"])
        S.dma("sp", gT[:], gT_in[:, :], writes=["gT"])
        S.dma("sp", flg[:], flags[:, :], writes=["flg"])
        S.op("dve", lambda e: e.memset(onesb[:], 1.0), writes=["onesb"])
        S.op("act", lambda e: e.activation(out=condT[:], in_=cf[:], func=AF.Silu), reads=["cf"], writes=["condT"])

        def adaln(m0, gidx, gate_scale, after_loads=None):
            with ExitStack() as st:
                nslot = 2 if (m0 >= 3 and (mode == "full" or mode.startswith("mix"))) else 3
                wa_ = [sb(st, "wa%d" % i, [128, 8, D], BF16) for i in range(nslot)]
                wa = [wa_[i % nslot] for i in range(3)]
                bb = sb(st, "bb", [128, D], F32)
                condB = sb(st, "condB", [128, 8, 128], BF16)
                for dc in range(8):
                    S.op("dve", lambda e: e.tensor_scalar(out=condB[:, dc, :], in0=onesb[:], scalar1=condT[:, dc:dc + 1],
                                                          scalar2=None, op0=ALU.mult),
                         reads=["onesb", "condT"], writes=[("condB", dc)])
                S.dma("sp", bb[:], b_ada[0:1, (m0 + 2) * D:(m0 + 3) * D].partition_broadcast(128), writes=["bb"])
                def load_slice(mi):
                    pre = (m0 + mi >= 3) and mode in ("full",) or (m0 + mi >= 3 and mode.startswith("mix"))
                    if pre:
                        src = wada_bf[m0 + mi - 3].rearrange("p (dc f) -> p dc f", dc=8)
                    else:
                        src = w_ada[:, (m0 + mi) * D:(m0 + mi + 1) * D].rearrange("(dc p) f -> p dc f", p=128)
                    for hf in range(2):
                        S.dma("sp" if pre else "pool", wa[mi][:, :, hf * 512:(hf + 1) * 512], src[:, :, hf * 512:(hf + 1) * 512],
                              writes=[(("wa", mi % nslot), hf)])

                for mi in range(nslot):
                    load_slice(mi)
                if after_loads is not None and nslot == 3:
                    after_loads()
                for mi in range(3):
                    if mi >= 1 and mi - 1 + nslot < 3:
                        load_slice(mi - 1 + nslot)
                        if after_loads is not None:
                            after_loads()
                    m = m0 + mi
                    w = wa[mi]
                    wk = ("wa", mi % nslot)
                    if mi < 2:
                        ps, pk = pbank()

                        def f(e):
                            last = None
                            for fc in range(8):
                                for dc in range(8):
                                    last = e.matmul(ps[:, fc:fc + 1], lhsT=w[:, dc, fc * 128:(fc + 1) * 128],
                                                    rhs=condT[:, dc:dc + 1], start=(dc == 0), stop=(dc == 7))
                            return last

                        S.op("pe", f, reads=[(wk, 0), (wk, 1), "condT"], writes=[pk])
                        S.op("dve", lambda e: e.tensor_tensor(out=modT[:, mi * 8:(mi + 1) * 8], in0=ps[:, 0:8],
                                                              in1=badT[:, m * 8:(m + 1) * 8], op=ALU.add),
                             reads=[pk, "badT"], writes=[("modT", mi)])
                    else:
                        for hf in range(2):
                            ps, pk = pbank()

                            def f(e):
                                last = None
                                for dc in range(8):
                                    last = e.matmul(ps[:, :], lhsT=condB[:, dc, :], rhs=w[:, dc, hf * 512:(hf + 1) * 512],
                                                    start=(dc == 0), stop=(dc == 7))
                                return last

                            S.op("pe", f, reads=[(wk, hf)] + [("condB", dc) for dc in range(8)], writes=[pk])
                            S.op("dve", lambda e: e.tensor_tensor(out=gtb[:, hf * 512:(hf + 1) * 512], in0=ps[:, :],
                                                                  in1=bb[:, hf * 512:(hf + 1) * 512], op=ALU.add),
                                 reads=[pk, "bb"], writes=[("gtb", hf)])
                            if gate_scale != 1.0:
                                S.op("dve", lambda e: e.tensor_scalar(out=gtb[:, hf * 512:(hf + 1) * 512],
                                                                      in0=gtb[:, hf * 512:(hf + 1) * 512],
                                                                      scalar1=gate_scale, scalar2=None, op0=ALU.mult),
                                     reads=[("gtb", hf)], writes=[("gtb", hf)])
                S.op("dve", lambda e: e.scalar_tensor_tensor(out=modA[:], in0=modT[:, 8:16], scalar=1.0,
                                                             in1=gT[:, gidx * 8:(gidx + 1) * 8], op0=ALU.add, op1=ALU.mult),
                     reads=[("modT", 1), "gT"], writes=["modA"])
                S.drain(["bb"] + [("condB", dc) for dc in range(8)] + [(("wa", mi), hf) for mi in range(nslot) for hf in range(2)])

        def rsqrt_op(ss_ap, out_ap, rkeys, wkeys, n):
            S.op("pool", lambda e: e.tensor_scalar(out=ss_ap, in0=ss_ap, scalar1=1.0 / D, scalar2=EPS, op0=ALU.mult, op1=ALU.add),
                 reads=rkeys, writes=rkeys)
            S.op("pool", lambda e: e.tensor_tensor(out=out_ap, in0=ss_ap, in1=mhalf[:, 0:n], op=ALU.pow),
                 reads=rkeys + ["mhalf"], writes=wkeys)

        def norm_stats(hts, keys, ss, rstd, junk, n, junk_keys=()):
            for j in range(n):
                S.op("act", lambda e: e.activation(out=junk[:], in_=hts[j], func=AF.Square, accum_out=ss[:, j:j + 1]),
                     reads=[keys[j]], writes=["junk", ("ss", j)] + list(junk_keys))
            rsqrt_op(ss[:, 0:n], rstd[:, 0:n], [("ss", j) for j in range(n)], [("rstd", j) for j in range(n)], n)

        def norm_to_uT(ht, hkey, rstd_col, rkey, xh, xkey, uT, ukey_fn, col0):
            S.op("act", lambda e: e.activation(out=xh[:], in_=ht, func=AF.Copy, scale=rstd_col),
                 reads=[hkey, rkey], writes=[xkey])
            tb, tk = tbank()

            def f(e):
                last = None
                for dc in range(8):
                    last = e.transpose(tb[:, dc * 128:(dc + 1) * 128], xh[:, dc * 128:(dc + 1) * 128], ident[:])
                return last

            S.op("pe", f, reads=[xkey, "ident"], writes=[tk])
            for dc in range(8):
                S.op("dve", lambda e: e.tensor_scalar(out=uT[:, dc, col0:col0 + 128], in0=tb[:, dc * 128:(dc + 1) * 128],
                                                      scalar1=modA[:, dc:dc + 1], scalar2=modT[:, dc:dc + 1],
                                                      op0=ALU.mult, op1=ALU.add),
                     reads=[tk, "modA", ("modT", 0)], writes=[ukey_fn(dc)])

        mhalf = sb(top, "mhalf", [128, 4], F32)
        S.op("dve", lambda e: e.memset(mhalf[:], -0.5), writes=["mhalf"])

        def wview(i):
            return wbf[i].rearrange("p (dc f) -> p dc f", dc=8)

        IT_P, IT_Q, IT_K, IT_V, IT_GP, IT_GA, IT_BR = 0, 1, 4, 7, 10, 14, 18
        cvt_specs = []

        def std_item(i, c0):
            cvt_specs.append((wview(i), w_in[:, c0:c0 + 256].rearrange("(dc p) f -> p dc f", p=128)))

        std_item(IT_P, 0)
        for g_ in range(3):
            std_item(IT_Q + g_, 256 + g_ * 256)
            std_item(IT_K + g_, 1024 + g_ * 256)
            std_item(IT_V + g_, 1792 + g_ * 256)
        for dp_ in range(4):
            std_item(IT_GP + dp_, 2560 + dp_ * 256)
            std_item(IT_GA + dp_, 3584 + dp_ * 256)
            cvt_specs.append((wview(IT_BR + dp_)[:, 0:2, :], w_pb[:, dp_ * 256:(dp_ + 1) * 256].rearrange("(c p) f -> p c f", p=128)))
            cvt_specs.append((wview(IT_BR + dp_)[:, 2:4, :], w_ab[:, dp_ * 256:(dp_ + 1) * 256].rearrange("(c p) f -> p c f", p=128)))

        IT_WO = 22
        for q4_ in range(4):
            cvt_specs.append((wview(IT_WO + q4_), w_o[:, q4_ * 256:(q4_ + 1) * 256].rearrange("(dc p) f -> p dc f", p=128)))
        for m_ in range(3, 9):
            src_ = w_ada[:, m_ * D:(m_ + 1) * D].rearrange("(dc p) f -> p dc f", p=128)
            dst_ = wada_bf[m_ - 3].rearrange("p (dc f) -> p dc f", dc=8)
            for hf_ in range(2):
                cvt_specs.append((dst_[:, :, hf_ * 512:(hf_ + 1) * 512], src_[:, :, hf_ * 512:(hf_ + 1) * 512]))

        def ffn_phase(src, dst, ntok, wa_d, wb_d, m0, gidx, final, cvt=False, pre_bf=None):
            with ExitStack() as st:
                W1 = sb(st, "W1", [128, 8, 2 * DFF], BF16)
                srcw = wa_d.rearrange("(dc p) f -> p dc f", p=128)
                srcw2 = wb_d.rearrange("(f p) d -> p f d", p=128)
                def wloads():
                    q_ = "sp" if pre_bf is not None else "pool"
                    sw1 = pre_bf[0] if pre_bf is not None else srcw
                    sw2 = pre_bf[1] if pre_bf is not None else srcw2
                    for fg in range(11):
                        for ab in range(2):
                            c0 = ab * DFF + fg * 256
                            if pre_bf is not None:
                                src_ = sw1[ab * 11 + fg].rearrange("p (dc f) -> p dc f", dc=8)
                            else:
                                src_ = sw1[:, :, c0:c0 + 256]
                            S.dma(q_, W1[:, :, c0:c0 + 256], src_, writes=[("W1", ab, fg)])

                def wloads2():
                    q_ = "sp" if pre_bf is not None else "pool"
                    sw2 = pre_bf[1] if pre_bf is not None else srcw2
                    for fg in range(11):
                        if pre_bf is not None:
                            src_ = sw2[fg].rearrange("p (a d) -> p a d", a=2)
                        else:
                            src_ = sw2[:, 2 * fg:2 * fg + 2, :]
                        S.dma(q_, W2[:, 2 * fg:2 * fg + 2, :], src_, writes=[("W2", fg)])

                early = True
                if early:
                    hb = [sb(st, "hb%d" % i, [128, D], F32) for i in range(4)]
                    ss = sb(st, "ss", [128, 4], F32)
                    rstd = sb(st, "rstd", [128, 4], F32)
                    junk = sb(st, "junk", [128, D], BF16)
                    for j in range(4):
                        S.dma("sp", hb[j][:], src[j * 128:(j + 1) * 128, :], writes=[("hb", j)])
                    norm_stats([hb[j][:] for j in range(4)], [("hb", j) for j in range(4)], ss, rstd, junk, 4)
                adaln(m0, gidx, 0.5, wloads)
                W2 = sb(st, "W2", [128, NF, D], BF16)
                wloads2()
                if not early:
                    hb = [sb(st, "hb%d" % i, [128, D], F32) for i in range(4)]
                NHR = 3 if final else 2
                hr = [sb(st, "hr%d" % i, [128, D], F32) for i in range(NHR)]
                xh = [sb(st, "xh0", [128, D], BF16)] * 2
                uT = sb(st, "uT", [128, 8, 512], BF16)
                gTt = sb(st, "gTt", [128, NF, 512], BF16)
                sa = [sb(st, "sa%d" % i, [128, 512], F32) for i in range(2)]
                if not early:
                    ss = sb(st, "ss", [128, 4], F32)
                    rstd = sb(st, "rstd", [128, 4], F32)
                    junk = sb(st, "junk", [128, D], BF16)
                if final:
                    gfb = sb(st, "gfb", [128, D], F32)
                    S.dma("sp", gfb[:], g_final[0:1, :].partition_broadcast(128), writes=["gfb"])
                    ss2 = sb(st, "ss2", [128, 1], F32)
                    rs2 = sb(st, "rs2", [128, 1], F32)
                ntile = ntok // 512

                def pre_stats(t):
                    for j in range(4):
                        S.dma("sp", hb[j][:], src[t * 512 + j * 128:t * 512 + (j + 1) * 128, :], writes=[("hb", j)])
                    norm_stats([hb[j][:] for j in range(4)], [("hb", j) for j in range(4)], ss, rstd, junk, 4)

                def pre_sub(j):
                    norm_to_uT(hb[j][:], ("hb", j), rstd[:, j:j + 1], ("rstd", j), xh[0], ("xh", 0),
                               uT, lambda dc: ("uT", dc, j), j * 128)

                if not early:
                    pre_stats(0)
                for j in range(4):
                    pre_sub(j)
                for t in range(ntile):
                    t0 = t * 512
                    if cvt and t >= 1:
                        for _ in range(4):
                            if cvt_specs:
                                d_, s_src = cvt_specs.pop(0)
                                S.dma("pool", d_, s_src, writes=["wbf"])
                    ukeys = [("uT", dc, j) for dc in range(8) for j in range(4)]
                    if STOP[0] == "a":
                        S.barrier()
                        return
                    for f_ in range(NF):
                        pa, pak = pbank(0, 4)
                        pbk_, pbk = pbank(0, 4)
                        fg, fo = f_ // 2, (f_ % 2) * 128

                        def fa(e):
                            last = None
                            for dc in range(8):
                                last = e.matmul(pa[:, :], lhsT=W1[:, dc, f_ * 128:(f_ + 1) * 128], rhs=uT[:, dc, :],
                                                start=(dc == 0), stop=(dc == 7))
                            return last

                        def fb(e):
                            last = None
                            for dc in range(8):
                                last = e.matmul(pbk_[:, :], lhsT=W1[:, dc, DFF + f_ * 128:DFF + (f_ + 1) * 128],
                                                rhs=uT[:, dc, :], start=(dc == 0), stop=(dc == 7))
                            return last

                        S.op("pe", fa, reads=ukeys + [("W1", 0, fg)], writes=[pak])
                        S.op("pe", fb, reads=ukeys + [("W1", 1, fg)], writes=[pbk])
                        s_ = sa[f_ % 2]
                        S.op("act", lambda e: e.activation(out=s_[:], in_=pa[:, :], func=AF.Silu),
                             reads=[pak], writes=[("sa", f_ % 2)])
                        S.op("dve", lambda e: e.tensor_tensor(out=gTt[:, f_, :], in0=s_[:], in1=pbk_[:, :], op=ALU.mult),
                             reads=[("sa", f_ % 2), pbk], writes=[("gTt", f_)])
                        if f_ == 8 and t + 1 < ntile and STOP[0] == "":
                            pre_stats(t + 1)
                    if STOP[0] == "b":
                        S.barrier()
                        return
                    if t == 0:
                        for f_ in range(NF):
                            S.op("pool", lambda e: e.tensor_tensor(out=W2[:, f_, :], in0=W2[:, f_, :], in1=gtb[:], op=ALU.mult),
                                 reads=[("W2", f_ // 2), ("gtb", 0), ("gtb", 1)], writes=[("W2", f_ // 2)])
                    for j in range(4):
                        hk = (4 * t + j) % NHR
                        r_ = hr[hk]
                        S.dma("sp", r_[:], src[t0 + j * 128:t0 + (j + 1) * 128, :], writes=[("hr", hk)])
                        for hf in range(2):
                            po, pok = pbank(4, 6)

                            def fo_(e):
                                last = None
                                for f_ in range(NF):
                                    last = e.matmul(po[:, :], lhsT=gTt[:, f_, j * 128:(j + 1) * 128],
                                                    rhs=W2[:, f_, hf * 512:(hf + 1) * 512], start=(f_ == 0), stop=(f_ == NF - 1))
                                return last

                            S.op("pe", fo_, reads=[("gTt", f_) for f_ in range(NF)] + [("W2", fg) for fg in range(11)],
                                 writes=[pok])
                            S.op("dve", lambda e: e.tensor_tensor(out=r_[:, hf * 512:(hf + 1) * 512], in0=po[:, :],
                                                                  in1=r_[:, hf * 512:(hf + 1) * 512], op=ALU.add),
                                 reads=[pok, ("hr", hk)], writes=[("hr", hk)])
                        if final:
                            S.op("dve", lambda e: e.scalar_tensor_tensor(out=junk[:], in0=r_[:], scalar=1.0, in1=r_[:],
                                                                         op0=ALU.mult, op1=ALU.mult, accum_out=ss2[:, 0:1]),
                                 reads=[("hr", hk)], writes=["junk", "ss2"])
                            rsqrt_op(ss2[:], rs2[:], ["ss2"], ["rs2"], 1)
                            S.op("dve", lambda e: e.scalar_tensor_tensor(out=r_[:], in0=r_[:], scalar=rs2[:, 0:1], in1=gfb[:],
                                                                         op0=ALU.mult, op1=ALU.mult),
                                 reads=[("hr", hk), "rs2", "gfb"], writes=[("hr", hk)])
                        S.dma("sp", dst[t0 + j * 128:t0 + (j + 1) * 128, :], r_[:], reads=[("hr", hk)], writes=[("dst", t, j)])
                        if t + 1 < ntile:
                            pre_sub(j)
                S.barrier()

        mode = dbg or "full"
        STOP = [""]

        def chk(tag):
            if mode == tag and not S.off:
                S.barrier()
                S.off = True
        if mode in ("ffn1a", "ffn1b"):
            STOP[0] = mode[-1]
            ffn_phase(x, y, 512, w1a, w1b, 0, 0, False)
            return nc
        if mode == "ada":
            adaln(0, 0, 0.5)
            S.dma("sp", y[0:128, :], gtb[:], reads=[("gtb", 0), ("gtb", 1)], writes=["yy"])
            S.barrier()
            return nc
        if mode == "ffn1":
            ffn_phase(x, y, 512, w1a, w1b, 0, 0, False)
            return nc
        if mode == "ffn3":
            ffn_phase(x, y, 512, w3a, w3b, 6, 2, True)
            return nc
        if mode.startswith("mix"):
            h1s = x
            h2s = y
            for d_, s_src in cvt_specs:
                S.dma("pool", d_, s_src, writes=["wbf"])
            S.barrier()
        if mode == "full":
            ffn_phase(x, h1s, NT, w1a, w1b, 0, 0, False, cvt=True)

        def mix_phase():
            with ExitStack() as st:
                hb = [sb(st, "mhb%d" % i, [128, D], F32) for i in range(4)]
                xh = [sb(st, "mxh0", [128, D], BF16)] * 2
                ss = sb(st, "mss", [128, 4], F32)
                rstd = sb(st, "mrstd", [128, 4], F32)
                junk = sb(st, "mjunk", [128, D], BF16)
                for j in range(4):
                    S.dma("sp", hb[j][:], h1s[j * 128:(j + 1) * 128, :], writes=[("hb", j)])
                norm_stats([hb[j][:] for j in range(4)], [("hb", j) for j in range(4)], ss, rstd, junk, 4)
                adaln(3, 1, 1.0)
                masks = sb(st, "masks", [128, 5, 512], BF16)
                for mi_ in range(5):
                    S.dma("pool", masks[:, mi_, :], masks_in[:, mi_ * 512:(mi_ + 1) * 512], writes=["masks"])
                rf = sb(st, "rf", [128, 2], F32)
                S.dma("sp", rf[:], ropef[:, :], writes=["rf"])
                rcn = sb(st, "rcn", [128, 32], F32)
                S.dma("sp", rcn[:], rcnt_in[:, :], writes=["rcn"])
                psc = sb(st, "psc", [128, 2], F32)
                S.dma("sp", psc[:], pool_scaleT[:, :], writes=["psc"])
                wpool = sb(st, "wpool", [128, 2, 128], BF16)
                S.dma("pool", wpool[:], w_pool_bd[:, :, :], writes=["wpool"])
                uT = sb(st, "uT2", [128, 8, ST], BF16)
                QT = sb(st, "QT", [128, 4, ST], BF16)
                for h_ in range(4):
                    z0 = 64 if h_ % 2 == 0 else 0
                    S.op("pool", lambda e: e.memset(QT[z0:z0 + 64, h_, :], 0.0), writes=[("QTz", h_)])
                KT = [sb(st, "KT%d" % g, [128, 2, RING[g]], BF16) for g in range(3)]
                VT = [sb(st, "VT%d" % g, [128, NVB[g], 256], BF16) for g in range(3)]
                acc = sb(st, "acc", [128, 4, ST], F32)
                merged = acc[:].bitcast(BF16).rearrange("p a (b c) -> p (a b) c", b=2)
                yat = sb(st, "yat", [128, 2, ST], BF16)
                ypl = sb(st, "ypl", [128, 2, ST], BF16)
                PT = [sb(st, "PT%d" % i, [128, 512], BF16) for i in range(2)]
                NWS = 6
                WS = [sb(st, "WS%d" % i, [128, 8, 256], BF16) for i in range(NWS)]
                wsr = [0]
                qbuf = [sb(st, "qbuf%d" % i, [128, 512], BF16) for i in range(2)]
                permT = sb(st, "permT", [128, 128], BF16)
                S.dma("pool", permT[:], perm_in[:, :], writes=["permT"])
                Ctab = sb(st, "Ctab", [128, ST], F32)
                Stab = sb(st, "Stab", [128, ST], F32)
                posi = sb(st, "posi", [128, 512], I32)
                scr = sb(st, "scr", [128, 1024], F32)
                ang0 = ang = scr[:, 0:512]
                kf0 = kf = scr[:, 512:1024]
                rD = scr[:].rearrange("p (a b) -> p a b", a=2)
                pT = sb(st, "pT", [128, 2, 16 + 512], F32)
                w2_ = sb(st, "pw2", [128, 2, 16 + 512], F32)
                w4_ = sb(st, "pw4", [128, 2, 16 + 512], F32)
                dT = sb(st, "dT", [128, 2, 512], BF16)
                tmp = sb(st, "mtmp", [128, 512], F32)
                tmp2 = sb(st, "mtmp2", [128, 512], F32)
                sg = [sb(st, "sg%d" % i, [128, 512], F32) for i in range(2)]
                S.op("pool", lambda e: e.memset(pT[:, :, 0:16], 0.0), writes=["pThist"])

                ukeys = [("uT", dc, jj) for dc in range(8) for jj in range(8)]
                ukeys1 = [("uT1", dc, jj) for dc in range(8) for jj in range(8)] + ["acc"]
                UT = [uT]
                UK = [ukeys]
                PRE = []

                def pre_next(n=1):
                    for _ in range(n):
                        if PRE:
                            PRE.pop(0)()
                wplan, wiss, dry = [], [0], [True]
                LA = 3

                def wload(item):
                    idx = wsr[0]
                    wsr[0] += 1
                    if dry[0]:
                        wplan.append(item)
                    else:
                        while wiss[0] < min(len(wplan), idx + LA + 1):
                            k = wiss[0]
                            if IT_BR <= wplan[k] < IT_BR + 4:
                                S.dma("pool", WS[k % NWS][:, 0:4, :], wview(wplan[k])[:, 0:4, :], writes=[("WS", k % NWS)])
                            else:
                                S.dma("pool", WS[k % NWS][:], wview(wplan[k]), writes=[("WS", k % NWS)])
                            wiss[0] += 1
                    return WS[idx % NWS], ("WS", idx % NWS)

                def wsrc(mat, c0):
                    return mat[:, c0:c0 + 256].rearrange("(dc p) f -> p dc f", p=128)

                def cols(buf, ps_, c2, g, jbase, ring, qb):
                    if g == 0:
                        r0 = (jbase + 128 * qb) % ring
                        return buf[ps_, c2, r0:r0 + 128]
                    if g == 1:
                        u, rho = qb // 4, qb % 4
                        r0 = (jbase + 512 * u) % ring + rho
                        return buf[ps_, c2, r0:r0 + 4 * 127 + 1:4]
                    r0 = jbase % ring
                    if buf is QT:
                        return buf[ps_, c2, 128 * qb:128 * qb + 128]
                    return buf[ps_, c2, r0:r0 + ST].rearrange("p (m e) -> p e m", e=16)[:, 2 * qb:2 * qb + 2, :]

                def vblk(g, s, qb):
                    if g == 0:
                        return (8 * s + qb) % 12
                    if g == 1:
                        u, rho = qb // 4, qb % 4
                        return ((2 * s + u) % 3) * 4 + rho
                    return (s % 3) * 8 + qb

                def proj_fm(wt, wk, ukeys, tt):
                    ps, pk = pbank()

                    def f(e):
                        last = None
                        for dc in range(8):
                            last = e.matmul(ps[:, :], lhsT=wt[dc], rhs=UT[0][:, dc, tt * 512:(tt + 1) * 512],
                                            start=(dc == 0), stop=(dc == 7))
                        return last

                    S.op("pe", f, reads=ukeys + [wk], writes=[pk])
                    return ps, pk

                def qk_proj(g, isk, tt, j0):
                        wn, wnk = wload((IT_K if isk else IT_Q) + g)
                        pas = []
                        for c2 in range(2):
                            pa, pak = proj_fm([wn[:, dc, c2 * 128:(c2 + 1) * 128] for dc in range(8)], wnk, UK[0], tt)
                            qb_ = qbuf[c2]
                            S.op("act", lambda e: e.activation(out=qb_[:], in_=pa[:, :], func=AF.Copy), reads=[pak], writes=[("qb", c2)])
                            pas.append((pa, pak))
                        for c2 in range(2):
                            pa, pak = pas[c2]
                            qb_ = qbuf[c2]
                            pb_, pbk = pbank()
                            S.op("pe", lambda e: e.matmul(pb_[:, :], lhsT=permT[:], rhs=qb_[:], start=True, stop=True),
                                 reads=[("qb", c2), "permT"], writes=[pbk])
                            tA, tB, k1, k2 = (tmp, tmp2, "tmp", "tmp2") if c2 == 0 else (sg[0], sg[1], ("sg", 0), ("sg", 1))
                            S.op("dve", lambda e: e.tensor_tensor(out=tA[:], in0=pa[:, :], in1=Ctab[:, tt * 512:(tt + 1) * 512], op=ALU.mult),
                                 reads=[pak, ("tab", 1, tt), ("qb", c2)], writes=[k1])
                            S.op("dve", lambda e: e.tensor_tensor(out=tB[:], in0=pb_[:, :], in1=Stab[:, tt * 512:(tt + 1) * 512], op=ALU.mult),
                                 reads=[pbk, ("tab", 0, tt)], writes=[k2])
                            r0 = (j0 + tt * 512) % RING[g]
                            if isk:
                                dkey = ("KT", g, c2, r0 // 512)
                                if g == 2:
                                    b0 = (j0 % RING[2])
                                    dst_ap = KT[2][:, c2, b0:b0 + ST].rearrange("p (e m) -> p m e", e=16)[:, 32 * tt:32 * tt + 32, :]
                                else:
                                    dst_ap = KT[g][:, c2, r0:r0 + 512]
                            else:
                                dkey = ("QT", c2, tt)
                            for hh in ((None,) if isk else (0, 1)):
                                pr = slice(0, 128) if isk else slice(64 * hh, 64 * hh + 64)
                                if not isk:
                                    if g == 2:
                                        dst_ap = QT[pr, 2 * c2 + hh, :].rearrange("p (e m) -> p m e", e=16)[:, 32 * tt:32 * tt + 32, :]
                                    else:
                                        dst_ap = QT[pr, 2 * c2 + hh, tt * 512:(tt + 1) * 512]
                                if g == 2:
                                    i0 = tA[pr, :].rearrange("p (m e) -> p m e", e=16)
                                    i1 = tB[pr, :].rearrange("p (m e) -> p m e", e=16)
                                else:
                                    i0, i1 = tA[pr, :], tB[pr, :]
                                S.op("dve", lambda e: e.tensor_tensor(out=dst_ap, in0=i0, in1=i1, op=ALU.add),
                                     reads=[k1, k2, ("QTz", 0)], writes=[dkey if isk else ("QT", c2, tt, hh)])

                def rope_tables(s_, tt):
                    c0 = s_ * ST + tt * 512
                    S.dma("sp", posi[:], pos[0:1, c0:c0 + 512].partition_broadcast(128), writes=["posi"])
                    bufs = [(ang0, kf0, "ang", "kf"), (tmp[:], tmp2[:], "tmp", "tmp2")]
                    for which in (0, 1):
                        ang, kf, ka, kk = bufs[which]
                        S.op("dve", lambda e, ang=ang: e.tensor_copy(out=ang, in_=posi[:]), reads=["posi"], writes=[ka])
                        S.op("dve", lambda e, which=which, ang=ang: e.tensor_scalar(out=ang, in0=ang, scalar1=rf[:, which:which + 1],
                                                              scalar2=(np.pi / 2 if which else 0.0), op0=ALU.mult, op1=ALU.add),
                             reads=[ka, "rf"], writes=[ka])
                    for which, tab in ((0, Stab), (1, Ctab)):
                        ang, kf, ka, kk = bufs[which]
                        S.op("dve", lambda e, ang=ang, kf=kf: e.tensor_scalar(out=kf, in0=ang, scalar1=float(1.0 / (2 * np.pi)),
                                                              scalar2=None, op0=ALU.mult), reads=[ka], writes=[kk])
                        S.op("dve", lambda e, kf=kf: e.tensor_copy(out=posi[:], in_=kf), reads=[kk], writes=["posi"])
                        S.op("dve", lambda e, kf=kf: e.tensor_copy(out=kf, in_=posi[:]), reads=["posi"], writes=[kk])
                        S.op("dve", lambda e, ang=ang, kf=kf: e.scalar_tensor_tensor(out=ang, in0=kf, scalar=-TWO_PI_HI, in1=ang,
                                                                     op0=ALU.mult, op1=ALU.add),
                             reads=[kk, ka], writes=[ka])
                        S.op("dve", lambda e, ang=ang, kf=kf: e.scalar_tensor_tensor(out=ang, in0=kf, scalar=-float(TWO_PI_LO), in1=ang,
                                                                     op0=ALU.mult, op1=ALU.add),
                             reads=[kk, ka], writes=[ka])
                        S.op("dve", lambda e, ang=ang: e.tensor_scalar(out=ang, in0=ang, scalar1=-3.14159, scalar2=3.14159,
                                                              op0=ALU.max, op1=ALU.min), reads=[ka], writes=[ka])
                        S.op("act", lambda e, tab=tab, tt=tt, ang=ang: e.activation(out=tab[:, tt * 512:(tt + 1) * 512], in_=ang, func=AF.Sin),
                             reads=[ka], writes=[("tab", which, tt)])

                DT1 = [("dT", 1, a_, b_) for a_ in range(2) for b_ in range(2)]

                def pre_stats(s_, q4):
                    for j in range(4):
                        r0 = s_ * ST + q4 * 512 + j * 128
                        S.dma("sp", hb[j][:], h1s[r0:r0 + 128, :], writes=[("hb", j)])
                    norm_stats([hb[j][:] for j in range(4)], [("hb", j) for j in range(4)], ss, rstd, junk, 4, junk_keys=DT1)

                def pre_sub(q4, j, alt=False):
                    norm_to_uT(hb[j][:], ("hb", j), rstd[:, j:j + 1], ("rstd", j), xh[0], ("xh", 0),
                               merged if alt else uT, lambda dc: (("uT1" if alt else "uT"), dc, q4 * 4 + j), q4 * 512 + j * 128)

                def pre_pieces(s_, alt):
                    out = []
                    for q4 in range(2):
                        out.append(lambda q4=q4: pre_stats(s_, q4))
                        for j in range(4):
                            out.append(lambda q4=q4, j=j: pre_sub(q4, j, alt))
                    return out

                cvt3 = []
                if mode == "full":
                    s3a = w3a.rearrange("(dc p) f -> p dc f", p=128)
                    s3b = w3b.rearrange("(f p) d -> p f d", p=128)
                    for fg in range(11):
                        for ab in range(2):
                            c0 = ab * DFF + fg * 256
                            cvt3.append((w3a_bf[ab * 11 + fg].rearrange("p (dc f) -> p dc f", dc=8), s3a[:, :, c0:c0 + 256]))
                    for fg in range(11):
                        cvt3.append((w3b_bf[fg].rearrange("p (a d) -> p a d", a=2), s3b[:, 2 * fg:2 * fg + 2, :]))

                def run_tiles():
                    for s in range(NT // ST):
                        halo = s < 2
                        j0 = s * ST
                        groups = [2] if s == 0 else [0, 1, 2]

                        if s == 0:
                            for pc in pre_pieces(0, False)[1:]:
                                pc()
                        UT[0], UK[0] = (merged, ukeys1) if s == 1 else (uT, ukeys)
                        if halo:
                            PRE[:] = pre_pieces(s + 1, s + 1 == 1)
                        if s == 0 and not S.off:
                            for hf_ in range(2):
                                for q2 in range(2):
                                    S.dma("sp", WOq[hf_][0][:, :, q2 * 256:(q2 + 1) * 256], wview(IT_WO + hf_ * 2 + q2), writes=[("WO", hf_)])
                            for hf_ in range(2):
                                for dc_ in range(8):
                                    S.op("dve", lambda e: e.tensor_tensor(out=WOq[hf_][0][:, dc_, :], in0=WOq[hf_][0][:, dc_, :],
                                                                          in1=gtb[:, hf_ * 512:(hf_ + 1) * 512], op=ALU.mult),
                                         reads=[("WO", hf_), ("gtb", hf_)], writes=[("WO", hf_)])
                        chk("mixA%d" % s)
                        for tt in range(2):
                            c0 = j0 + tt * 512
                            if s < 3:
                                rope_tables(s, tt)
                            def p_and_adds():
                                if s >= 1:
                                    wt, wk = wload(IT_P)
                                    for c2 in range(2):
                                        ps, pk = proj_fm([wt[:, dc, c2 * 128:(c2 + 1) * 128] for dc in range(8)], wk, UK[0], tt)
                                        if halo:
                                            S.op("dve", lambda e: e.tensor_scalar(out=pT[:, c2, 16:528], in0=ps[:, :], scalar1=flg[:, 0:1],
                                                                                  scalar2=None, op0=ALU.mult),
                                                 reads=[pk, "flg"], writes=[("pTc", c2)])
                                        else:
                                            S.op("act", lambda e: e.activation(out=pT[:, c2, 16:528], in_=ps[:, :], func=AF.Copy),
                                                 reads=[pk], writes=[("pTc", c2)])
                                if s >= 1:
                                    if not halo:
                                        dTt = dT if tt == 0 else junk[:].rearrange("p (a b) -> p a b", a=2)
                                        pk_all = [("pTc", 0), ("pTc", 1), "pThist"]
                                        S.op("pool", lambda e: e.tensor_tensor(out=w2_[:, :, 1:528], in0=pT[:, :, 1:528], in1=pT[:, :, 0:527], op=ALU.add),
                                             reads=pk_all, writes=["w2", "w8"])
                                        S.op("pool", lambda e: e.tensor_tensor(out=w4_[:, :, 3:528], in0=w2_[:, :, 3:528], in1=w2_[:, :, 1:526], op=ALU.add),
                                             reads=["w2"], writes=["w4", "w16"])
                                        S.op("pool", lambda e: e.tensor_tensor(out=w2_[:, 1, 7:528], in0=w4_[:, 1, 7:528], in1=w4_[:, 1, 3:524], op=ALU.add),
                                             reads=["w4"], writes=["w8"])
                                        S.op("pool", lambda e: e.tensor_tensor(out=w4_[:, 1, 15:528], in0=w2_[:, 1, 15:528], in1=w2_[:, 1, 7:520], op=ALU.add),
                                             reads=["w8"], writes=["w16"])
                            def k_block():
                                for g in groups:
                                    qk_proj(g, 1, tt, j0)
                                if halo:
                                    pre_next(2)
                            if tt == 0:
                                p_and_adds()
                                k_block()
                            else:
                                k_block()
                                p_and_adds()
                            if s >= 1:
                                if not halo:
                                    dTt = dT if tt == 0 else junk[:].rearrange("p (a b) -> p a b", a=2)
                                    pk_all = [("pTc", 0), ("pTc", 1), "pThist"]
                                    S.op("dve", lambda e: e.scalar_tensor_tensor(out=dTt[0:64, 0, :], in0=w2_[0:64, 0, 16:528], scalar=0.5,
                                                                                 in1=pT[0:64, 0, 16:528], op0=ALU.mult, op1=ALU.subtract),
                                         reads=["w2"] + pk_all, writes=[("dT", tt, 0, 0)])
                                    S.op("dve", lambda e: e.scalar_tensor_tensor(out=dTt[64:128, 0, :], in0=w4_[64:128, 0, 16:528], scalar=0.25,
                                                                                 in1=pT[64:128, 0, 16:528], op0=ALU.mult, op1=ALU.subtract),
                                         reads=["w4"] + pk_all, writes=[("dT", tt, 0, 1)])
                                    S.op("dve", lambda e: e.scalar_tensor_tensor(out=dTt[0:64, 1, :], in0=w2_[0:64, 1, 16:528], scalar=0.125,
                                                                                 in1=pT[0:64, 1, 16:528], op0=ALU.mult, op1=ALU.subtract),
                                         reads=["w8", "w16"] + pk_all, writes=[("dT", tt, 1, 0)])
                                    S.op("dve", lambda e: e.scalar_tensor_tensor(out=dTt[64:128, 1, :], in0=w4_[64:128, 1, 16:528], scalar=0.0625,
                                                                                 in1=pT[64:128, 1, 16:528], op0=ALU.mult, op1=ALU.subtract),
                                         reads=["w16"] + pk_all, writes=[("dT", tt, 1, 1)])
                                    if s == 2 and tt == 0:
                                        for c2, hfp, wb_, wkey in ((0, 0, w2_, "w2"), (0, 1, w4_, "w4"), (1, 0, w2_, "w8"), (1, 1, w4_, "w16")):
                                            pr = slice(hfp * 64, hfp * 64 + 64)
                                            S.op("dve", lambda e: e.tensor_tensor(out=tmp[pr, 0:16], in0=wb_[pr, c2, 16:32],
                                                                                  in1=rcn[pr, c2 * 16:(c2 + 1) * 16], op=ALU.mult),
                                                 reads=["w2", "w4", "w8", "w16", "rcn"], writes=["tmp"])
                                            S.op("dve", lambda e: e.tensor_tensor(out=dTt[pr, c2, 0:16], in0=tmp[pr, 0:16],
                                                                                  in1=pT[pr, c2, 16:32], op=ALU.subtract),
                                                 reads=["tmp"] + pk_all, writes=[("dT", tt, c2, hfp)])
                            if s >= 1:
                                S.op("pool", lambda e: e.tensor_copy(out=pT[:, :, 0:16], in_=pT[:, :, 512:528]),
                                     reads=[("pTc", 0), ("pTc", 1)] + [("dT", tt, a_, b_) for a_ in range(2) for b_ in range(2)] + ["w2"],
                                     writes=["pThist"])
                        chk("mixB%d" % s)
                        for g in groups:
                            wv, wvk = wload(IT_V + g)
                            for qb in range(8):
                                ps, pk = pbank(4, 6)
                                blk = vblk(g, s, qb)

                                def f(e):
                                    last = None
                                    for dc in range(8):
                                        if g == 2:
                                            for e2 in range(2):
                                                c_ = 2 * qb + e2
                                                last = e.matmul(ps[e2 * 64:e2 * 64 + 64, 0:256], lhsT=UT[0][:, dc, c_:c_ + 16 * 63 + 1:16],
                                                                rhs=wv[:, dc, :], start=(dc == 0), stop=(dc == 7))
                                        else:
                                            last = e.matmul(ps[:, 0:256], lhsT=cols(UT[0], slice(0, 128), dc, g, 0, ST, qb), rhs=wv[:, dc, :],
                                                            start=(dc == 0), stop=(dc == 7))
                                    return last

                                S.op("pe", f, reads=UK[0] + [wvk], writes=[pk])
                                S.op("act", lambda e: e.activation(out=VT[g][:, blk, :], in_=ps[:, 0:256], func=AF.Copy),
                                     reads=[pk], writes=[("VT", g, blk)])
                                if halo:
                                    pre_next(1)
                                elif cvt3 and not S.off:
                                    d_, s_src = cvt3.pop(0)
                                    S.dma("pool", d_, s_src, writes=["w3bf"])
                            chk("mixV%d%d" % (s, g))
                            if halo:
                                continue
                            for tt in range(2):
                                qk_proj(g, 0, tt, j0)
                            micro = []
                            if g == 2 and s + 1 < NT // ST and s >= 2 and not S.off:
                                micro = []
                                for tt in range(2):
                                    S.deferred = []
                                    rope_tables(s + 1, tt)
                                    m2, S.deferred = S.deferred, None
                                    pos_ = [k_ for k_, it in enumerate(m2) if it[0] == "op" and it[1][0] == "act"]
                                    for k_ in reversed(pos_):
                                        it = m2.pop(k_)
                                        m2.insert(min(len(m2), k_ + 6), it)
                                    micro += m2
                            kring = [("KT", g, c2, i) for c2 in range(2) for i in range(RING[g] // 512)]
                            qkeys = [("QT", c2, tt, hh) for c2 in range(2) for tt in range(2) for hh in range(2)] + [("QTz", h_) for h_ in range(4)]
                            units = []
                            for qb in range(8):
                                if g == 0:
                                    kbs = [(j0 + 128 * qb - 128, 0), (j0 + 128 * qb, 1)]
                                elif g == 1:
                                    kbs = [(j0 + 512 * (qb // 4) - 512, 0), (j0 + 512 * (qb // 4), 1)]
                                else:
                                    kbs = [(j0 - 2048, 2), (j0 - 1024, 3), (j0, 4)]
                                for ki_, (jb, mid) in enumerate(kbs):
                                    units.append((qb, ki_, len(kbs), jb, mid))
                            st_ = {}

                            def emit_qk(i):
                                qb, ki_, nk, jb, mid = units[i]
                                sp_, spk = pbank(0, 2)
                                if g == 0:
                                    kcol = lambda c2: KT[0][:, c2, jb % RING[0]:jb % RING[0] + 128]
                                    kb_blk = (jb // 128) % 12
                                elif g == 1:
                                    rho = qb % 4
                                    kcol = lambda c2: KT[1][:, c2, jb % RING[1] + rho:jb % RING[1] + rho + 509:4]
                                    kb_blk = ((jb // 512) % 3) * 4 + rho
                                else:
                                    kcol = lambda c2: KT[2][:, c2, jb % RING[2] + 128 * qb:jb % RING[2] + 128 * qb + 128]
                                    kb_blk = ((jb // ST) % 3) * 8 + qb

                                def fs(e):
                                    e.matmul(sp_[:, :], lhsT=ident[:], rhs=masks[:, mid, :], start=True, stop=False, skip_group_check=True)
                                    last = None
                                    for h in range(4):
                                        last = e.matmul(sp_[:, h * 128:(h + 1) * 128], lhsT=kcol(h // 2),
                                                        rhs=cols(QT, slice(0, 128), h, g, 0, ST, qb), start=False, stop=(h == 3),
                                                        skip_group_check=True)
                                    return last

                                S.op("pe", fs, reads=kring + qkeys + ["masks", "ident"], writes=[spk])
                                pt = PT[i % 2]
                                ptk = ("PT", i % 2)
                                if jb < HALO:
                                    S.op("act", lambda e: e.activation(out=pt[:], in_=sp_[:, :], func=AF.Exp, scale=0.125, bias=flg[:, 1:2]),
                                         reads=[spk, "flg"], writes=[ptk])
                                else:
                                    S.op("act", lambda e: e.activation(out=pt[:], in_=sp_[:, :], func=AF.Exp, scale=0.125),
                                         reads=[spk], writes=[ptk])
                                st_[i] = (pt, ptk, kb_blk)

                            def emit_pv(i):
                                qb, ki_, nk, jb, mid = units[i]
                                pt, ptk, kb_blk = st_.pop(i)
                                if ki_ == 0:
                                    st_["nd"] = pbank(2, 4)
                                nd, ndk = st_["nd"]
                                first, lastkb = ki_ == 0, ki_ == nk - 1

                                def fpv(e):
                                    last = None
                                    for h in range(4):
                                        po_ = slice((h % 2) * 64, (h % 2) * 64 + 64)
                                        last = e.matmul(nd[po_, (h // 2) * 128:(h // 2) * 128 + 128],
                                                        lhsT=VT[g][:, kb_blk, h * 64:(h + 1) * 64], rhs=pt[:, h * 128:(h + 1) * 128],
                                                        start=(first and h < 2), stop=False, skip_group_check=True)
                                    for h in range(4):
                                        po_ = slice((h % 2) * 64, (h % 2) * 64 + 64)
                                        last = e.matmul(nd[po_, 256 + (h // 2) * 128:256 + (h // 2) * 128 + 128], lhsT=onesb[:, 0:64],
                                                        rhs=pt[:, h * 128:(h + 1) * 128],
                                                        start=False, stop=(lastkb and h == 3), skip_group_check=True)
                                    return last

                                S.op("pe", fpv, reads=[ptk, ("VT", g, kb_blk), "onesb"], writes=[ndk])
                                if lastkb:
                                    if g < 2:
                                        if g == 0:
                                            dst_ap = acc[:, :, 128 * qb:128 * qb + 128]
                                        else:
                                            r0_ = 512 * (qb // 4) + qb % 4
                                            dst_ap = acc[:, :, r0_:r0_ + 509:4]
                                        src_ap = nd[:, :].rearrange("p (k q) -> p k q", k=4)
                                        if g == 0:
                                            S.op("dve", lambda e: e.tensor_copy(out=dst_ap, in_=src_ap), reads=[ndk], writes=["acc"])
                                        else:
                                            S.op("dve", lambda e: e.tensor_tensor(out=dst_ap, in0=dst_ap, in1=src_ap, op=ALU.add),
                                                 reads=[ndk, "acc"], writes=["acc"])
                                    else:
                                        for k4 in range(4):
                                            dst_ap = cols(acc, slice(0, 128), k4, g, 0, ST, qb)
                                            src_ap = nd[:, k4 * 128:(k4 + 1) * 128].rearrange("p (e m) -> p e m", e=2)
                                            S.op("dve", lambda e: e.tensor_tensor(out=dst_ap, in0=dst_ap, in1=src_ap, op=ALU.add),
                                                 reads=[ndk, "acc"], writes=["acc"])

                            emit_qk(0)
                            for i in range(len(units)):
                                if i + 1 < len(units):
                                    emit_qk(i + 1)
                                emit_pv(i)
                                for _ in range(2):
                                    if micro:
                                        kind, args = micro.pop(0)
                                        (S.op if kind == "op" else S.dma)(*args)
                            while micro:
                                kind, args = micro.pop(0)
                                (S.op if kind == "op" else S.dma)(*args)
                            if mode == "mixT%d%d" % (s, g) and not S.off:
                                for k_ in range(4):
                                    S.dma("sp", y[256 + k_ * 128:256 + (k_ + 1) * 128, :], acc[:, k_, :], reads=["acc"], writes=["ydbg"])
                            chk("mixT%d%d" % (s, g))
                        if halo:
                            pre_next(len(PRE))
                            continue
                        for tt in range(2):
                            dTt = dT if tt == 0 else junk[:].rearrange("p (a b) -> p a b", a=2)
                            for c2 in range(2):
                                ps, pk = pbank()
                                S.op("pe", lambda e: e.matmul(ps[:, :], lhsT=wpool[:, c2, :], rhs=dTt[:, c2, :], start=True, stop=True),
                                     reads=[("dT", tt, c2, 0), ("dT", tt, c2, 1), "wpool"], writes=[pk])
                                S.op("dve", lambda e: e.tensor_scalar(out=ypl[:, c2, tt * 512:(tt + 1) * 512], in0=ps[:, :],
                                                                      scalar1=psc[:, c2:c2 + 1], scalar2=None, op0=ALU.mult),
                                     reads=[pk, "psc"], writes=[("ypl", c2, tt)])
                        for tt in range(2):
                            tsl = slice(tt * 512, (tt + 1) * 512)
                            S.op("act", lambda e: e.activation(out=rD, in_=acc[:, 2:4, tsl], func=AF.Ln), reads=["acc"], writes=["rD", "ang", "kf"])
                            S.op("act", lambda e: e.activation(out=rD, in_=rD, func=AF.Exp, scale=-1.0), reads=["rD"], writes=["rD", "ang", "kf"])
                            S.op("dve", lambda e: e.tensor_tensor(out=yat[:, :, tsl], in0=acc[:, 0:2, tsl], in1=rD, op=ALU.mult),
                                 reads=["acc", "rD"], writes=[("yat", tt)])
                        if mode == "mixY%d" % s and not S.off:
                            for sl_ in range(2):
                                S.dma("pool", y[sl_ * 128:(sl_ + 1) * 128, :], yat[:, sl_, :], reads=[("yat", 0), ("yat", 1)], writes=["ydbg"])
                            for k_ in range(4):
                                S.dma("sp", y[256 + k_ * 128:256 + (k_ + 1) * 128, :], acc[:, k_, :], reads=["acc"], writes=["ydbg"])
                        chk("mixY%d" % s)
                        wpb, wpbk = None, None
                        for dp in range(4):
                            wgp, wgpk = wload(IT_GP + dp)
                            wga, wgak = wload(IT_GA + dp)
                            wbr, wbrk = wload(IT_BR + dp)
                            for di in range(2):
                                dc_o = dp * 2 + di
                                fs_ = slice(di * 128, (di + 1) * 128)
                                for tt in range(2):
                                    tsl = slice(tt * 512, (tt + 1) * 512)
                                    for br, (wg, wgk, src_, skey) in enumerate(((wgp, wgpk, ypl, "ypl"), (wga, wgak, yat, "yat"))):
                                        pg, pgk = proj_fm([wg[:, dc, fs_] for dc in range(8)], wgk, UK[0], tt)
                                        pbr, pbrk = pbank()

                                        def fbr(e):
                                            last = None
                                            for c2 in range(2):
                                                last = e.matmul(pbr[:, :], lhsT=wbr[:, br * 2 + c2, fs_], rhs=src_[:, c2, tsl],
                                                                start=(c2 == 0), stop=(c2 == 1))
                                            return last

                                        rk = [("ypl", c2, tt) for c2 in range(2)] if br == 0 else [("yat", tt)]
                                        S.op("pe", fbr, reads=rk + [wbrk], writes=[pbrk])
                                        S.op("act", lambda e: e.activation(out=sg[br][:], in_=pg[:, :], func=AF.Sigmoid),
                                             reads=[pgk], writes=[("sg", br)])
                                        if br == 0:
                                            S.op("dve", lambda e: e.tensor_tensor(out=tmp[:], in0=sg[0][:], in1=pbr[:, :], op=ALU.mult),
                                                 reads=[("sg", 0), pbrk], writes=["tmp"])
                                        else:
                                            S.op("dve", lambda e: e.tensor_tensor(out=tmp2[:], in0=sg[1][:], in1=pbr[:, :], op=ALU.mult),
                                                 reads=[("sg", 1), pbrk], writes=["tmp2"])
                                    S.op("dve", lambda e: e.tensor_tensor(out=merged[:, dc_o, tsl], in0=tmp[:], in1=tmp2[:], op=ALU.add),
                                         reads=["tmp", "tmp2", ("yat", 0), ("yat", 1)], writes=["acc"])
                        chk("mixF%d" % s)
                        nxt = s + 1 < NT // ST
                        if nxt:
                            pre_stats(s + 1, 0)
                        for j in range(8):
                            r0 = j0 - HALO + j * 128
                            for hf in range(2):
                                cs = slice(hf * 512, (hf + 1) * 512)
                                rb, rbk = ((sg[0], ("sg", 0)), (sg[1], ("sg", 1)), (tmp, "tmp"), (tmp2, "tmp2"))[(2 * j + hf) % 4]
                                S.dma("sp", rb[:], h1s[j0 + j * 128:j0 + (j + 1) * 128, cs], writes=[rbk])
                                wo, wok = WOq[hf]
                                ps, pk = pbank()

                                def f(e):
                                    last = None
                                    for dc in range(8):
                                        last = e.matmul(ps[:, :], lhsT=merged[:, dc, j * 128:(j + 1) * 128], rhs=wo[:, dc, :],
                                                        start=(dc == 0), stop=(dc == 7))
                                    return last

                                S.op("pe", f, reads=["acc", wok], writes=[pk])
                                S.op("dve", lambda e: e.tensor_tensor(out=rb[:], in0=ps[:, :], in1=rb[:], op=ALU.add),
                                     reads=[pk, rbk], writes=[rbk])
                                S.dma("sp", h2s[r0:r0 + 128, cs], rb[:], reads=[rbk], writes=[("dst2", j, hf)])
                            if nxt:
                                pre_sub(j // 4, j % 4)
                                if j == 3:
                                    pre_stats(s + 1, 1)

                        chk("mixG%d" % s)

                was_off = S.off
                rot_save = dict(rot)
                S.off = True
                run_tiles()
                S.off = was_off
                dry[0] = False
                wsr[0] = 0
                rot.clear()
                rot.update(rot_save)
                run_tiles()
                S.barrier()

        WOq = []
        try:
          with ExitStack() as st2:
            for hf in range(2):
                t_ = sb(st2, "WO%d" % hf, [128, 8, 512], BF16)
                WOq.append((t_, ("WO", hf)))
            mix_phase()
        except _Stop:
            return nc

        if mode == "full":
            ffn_phase(h2s, y, OWN, w3a, w3b, 6, 2, True, pre_bf=(w3a_bf, w3b_bf))
    return nc


def _host_consts():
    ident = np.eye(128, dtype=np.float32)
    c = np.arange(128)[:, None]
    a = np.arange(128)[None, :]
    m0 = np.where(c >= a, 0.0, NEG)
    m1 = np.where(c <= a, 0.0, NEG)
    ek, mk = c // 64, c % 64
    eq, mq = a // 64, a % 64
    same = ek == eq
    m2 = np.where(same & (mk >= mq), 0.0, NEG)
    m3 = np.where(same, 0.0, NEG)
    m4 = np.where(same & (mk <= mq), 0.0, NEG)
    masks = np.stack([np.tile(m, (1, 4)) for m in (m0, m1, m2, m3, m4)], axis=1).astype(np.float32)
    masks = masks.reshape(128, 5 * 512)
    inv = (np.float32(10000.0) ** (-(np.arange(0, 64, 2, dtype=np.float32)) / np.float32(64))).astype(np.float32)
    i = np.arange(128) % 64
    f = inv[i % 32]
    sgn = np.where(i < 32, -1.0, 1.0).astype(np.float32)
    ropef = np.stack([f * sgn, f], axis=1).astype(np.float32)
    return ident, masks, ropef


_NC_CACHE = {}


def kernel(**inputs):
    f32 = lambda a: np.ascontiguousarray(np.asarray(a, dtype=np.float32))
    x = f32(inputs["x"])
    c = f32(inputs["c"])
    positions = np.asarray(inputs["positions"]).astype(np.int32)
    ident, masks, ropef = _host_consts()
    w_in = f32(inputs["w_in"])[0]
    perm = np.concatenate([np.arange(h * 64 + 32, h * 64 + 64).tolist() + np.arange(h * 64, h * 64 + 32).tolist()
                           for h in range(24)]).astype(np.int64)
    pidx = np.arange(128)
    partner = (pidx // 64) * 64 + (pidx % 64 + 32) % 64
    permT = np.zeros((128, 128), np.float32)
    permT[partner, pidx] = 1.0
    w_pool = f32(inputs["w_pool"])[0]
    w_pool_bd = np.zeros((128, 2, 128), np.float32)
    for c2 in range(2):
        w_pool_bd[0:64, c2, 0:64] = w_pool[2 * c2]
        w_pool_bd[64:128, c2, 64:128] = w_pool[2 * c2 + 1]
    gTm = np.concatenate([f32(inputs[k])[0].reshape(8, 128).T for k in ("g_norm_ffn1", "g_norm_mix", "g_norm_ffn2")], axis=1)
    b_ada = f32(inputs["b_ada"])
    shared = {
        "w_ada": f32(inputs["w_ada"])[0], "b_ada": b_ada, "b_adaT": np.ascontiguousarray(b_ada[0].reshape(72, 128).T),
        "gT": np.ascontiguousarray(gTm), "g_final": f32(inputs["g_final"]).reshape(1, D),
        "w_ffn1_in": f32(inputs["w_ffn1_in"])[0], "w_ffn1_out": f32(inputs["w_ffn1_out"])[0],
        "w_ffn2_in": f32(inputs["w_ffn2_in"])[0], "w_ffn2_out": f32(inputs["w_ffn2_out"])[0],
        "w_in": w_in, "permT": permT, "w_pool_bd": w_pool_bd,
        "pool_scaleT": np.ascontiguousarray(f32(inputs["pool_scale"])[0].reshape(2, 128).T),
        "w_pb": f32(inputs["w_pool_branch"])[0], "w_ab": f32(inputs["w_attn_branch"])[0], "w_o": f32(inputs["w_out"])[0],
        "ident": ident, "masks": masks, "ropef": ropef,
    }
    wins = np.array([2, 4, 8, 16], np.float32)
    in_maps = []
    for core in range(8):
        b, half = core // 2, core % 2
        s0 = half * OWN
        xc = np.zeros((NT, D), np.float32)
        pc = np.zeros((1, NT), np.int32)
        xc[HALO:] = x[b, s0:s0 + OWN]
        pc[0, HALO:] = positions[b, s0:s0 + OWN]
        if half == 1:
            xc[:HALO] = x[b, s0 - HALO:s0]
            pc[0, :HALO] = positions[b, s0 - HALO:s0]
        flags = np.zeros((128, 2), np.float32)
        flags[:, 0] = float(half)
        flags[:, 1] = 0.0 if half else NEG
        rc = np.zeros((128, 32), np.float32)
        for p in range(128):
            for c2 in range(2):
                w = wins[2 * c2 + p // 64]
                t = np.arange(16, dtype=np.float32)
                rc[p, c2 * 16:(c2 + 1) * 16] = 1.0 / (np.minimum(t + 1, w) if half == 0 else w)
        m = dict(shared)
        m.update({"x": xc, "pos": pc, "cT": np.ascontiguousarray(c[b].reshape(8, 128).T), "flags": flags, "rcnt": rc})
        in_maps.append(m)
    dbg = os.environ.get("MK_DBG", "")
    if dbg:
        return in_maps
    if "nc" not in _NC_CACHE:
        _NC_CACHE["nc"] = build_program()
    res = run_bass_kernel_spmd(_NC_CACHE["nc"], in_maps, core_ids=list(range(8)))
    out = np.zeros((4, 2 * OWN, D), np.float32)
    for core in range(8):
        b, half = core // 2, core % 2
        out[b, half * OWN:(half + 1) * OWN] = res.results[core]["y"]
    return out
```

```python
import os
from contextlib import ExitStack
import numpy as np
import concourse.bass as bass
import concourse.mybir as mybir
from concourse.bass_utils import run_bass_kernel_spmd

F32 = mybir.dt.float32
BF16 = mybir.dt.bfloat16
I32 = mybir.dt.int32
AF = mybir.ActivationFunctionType
ALU = mybir.AluOpType

D = 1024
DFF = 2816
NF = 22
OWN = 4096
HALO = 2048
NT = OWN + HALO
ST = 1024
RING = (1536, 1536, 3072)
DIL = (1, 4, 16)
NVB = (12, 12, 24)
EPS = 1e-6
NEG = -30000.0
TWO_PI_HI = 6.28125
TWO_PI_LO = 2.0 * np.pi - 6.28125


class _Stop(Exception):
    pass


class Sched:
    def __init__(self, nc, stack, n_dma=12):
        self.nc = nc
        self.eng = {"pe": nc.tensor, "act": nc.scalar, "dve": nc.vector, "pool": nc.gpsimd, "sp": nc.sync}
        self.semh = {}
        self.cnt = {}
        for e in ("pe", "act", "dve", "pool"):
            self.semh[e] = stack.enter_context(nc.semaphore("s_" + e))
            self.cnt[e] = 0
        self.dma_slots = {}
        for q in ("sp", "pool", "act"):
            sl = []
            for i in range(n_dma):
                k = "d_%s%d" % (q, i)
                self.semh[k] = stack.enter_context(nc.semaphore(k))
                sl.append([k, 0])
            self.dma_slots[q] = sl
        self.rr = {"sp": 0, "pool": 0, "act": 0}
        self.waited = {e: {} for e in self.eng}
        self.last_w = {}
        self.readers = {}
        self.off = False
        self.deferred = None
        self.pending = []

    def _wait(self, e, semk, val):
        w = self.waited[e]
        if w.get(semk, 0) >= val:
            return
        self.eng[e].wait_ge(self.semh[semk], val)
        w[semk] = val

    def _deps(self, e, reads, writes):
        need = {}

        def add(t):
            if t is not None and need.get(t[0], 0) < t[1]:
                need[t[0]] = t[1]

        for r in reads:
            add(self.last_w.get(r))
        for w in writes:
            add(self.last_w.get(w))
            for k, v in self.readers.get(w, {}).items():
                add((k, v))
        for k, v in need.items():
            if e == "pe" and k == "pe":
                continue
            self._wait(e, k, v)

    def _record(self, tok, reads, writes):
        for r in reads:
            d = self.readers.setdefault(r, {})
            if d.get(tok[0], 0) < tok[1]:
                d[tok[0]] = tok[1]
        for w in writes:
            self.last_w[w] = tok
            self.readers[w] = {}

    def flush(self, n=None):
        q, self.deferred = self.deferred, None
        k = 0
        while q and (n is None or k < n):
            kind, args = q.pop(0)
            (self.op if kind == "op" else self.dma)(*args)
            k += 1
        self.deferred = None
        self.pending = q
        return q

    def op(self, e, fn, reads=(), writes=()):
        if self.off:
            return None
        if self.deferred is not None:
            self.deferred.append(("op", (e, fn, tuple(reads), tuple(writes))))
            return None
        self._deps(e, reads, writes)
        inst = fn(self.eng[e])
        self.cnt[e] += 1
        inst.then_inc(self.semh[e], 1)
        tok = (e, self.cnt[e])
        self._record(tok, reads, writes)
        return tok

    def dma(self, q, out, in_, reads=(), writes=()):
        if self.off:
            return None
        if self.deferred is not None:
            self.deferred.append(("dma", (q, out, in_, tuple(reads), tuple(writes))))
            return None
        self._deps(q, reads, writes)
        sl = self.dma_slots[q]
        slot = sl[self.rr[q] % len(sl)]
        self.rr[q] += 1
        if slot[1] > 0:
            self._wait(q, slot[0], 16 * slot[1])
        self.eng[q].dma_start(out=out, in_=in_).then_inc(self.semh[slot[0]], 16)
        slot[1] += 1
        tok = (slot[0], 16 * slot[1])
        self._record(tok, reads, writes)
        return tok

    def drain(self, keys):
        if self.off:
            return
        need = {}
        for k in keys:
            t = self.last_w.get(k)
            if t is not None and need.get(t[0], 0) < t[1]:
                need[t[0]] = t[1]
            for sk, v in self.readers.get(k, {}).items():
                if need.get(sk, 0) < v:
                    need[sk] = v
        for e in self.eng:
            for sk, v in need.items():
                self._wait(e, sk, v)

    def barrier(self):
        if self.off:
            return
        tot = {e: self.cnt[e] for e in self.cnt}
        for q in self.dma_slots:
            for k, u in self.dma_slots[q]:
                tot[k] = 16 * u
        for e in self.eng:
            for k, v in tot.items():
                if v > 0:
                    self._wait(e, k, v)
        self.last_w = {}
        self.readers = {}


def build_program(dbg=False):
    nc = bass.Bass("TRN2", target_bir_lowering=False)

    def din(name, shape, dt=F32):
        return nc.dram_tensor(name, list(shape), dt, kind="ExternalInput").ap()

    x = din("x", [NT, D])
    pos = din("pos", [1, NT], I32)
    cT = din("cT", [128, 8])
    w_ada = din("w_ada", [D, 9 * D])
    b_ada = din("b_ada", [1, 9 * D])
    b_adaT = din("b_adaT", [128, 72])
    gT_in = din("gT", [128, 24])
    g_final = din("g_final", [1, D])
    w1a = din("w_ffn1_in", [D, 2 * DFF])
    w1b = din("w_ffn1_out", [DFF, D])
    w3a = din("w_ffn2_in", [D, 2 * DFF])
    w3b = din("w_ffn2_out", [DFF, D])
    w_in = din("w_in", [D, 4608])
    perm_in = din("permT", [128, 128])
    w_pool_bd = din("w_pool_bd", [128, 2, 128])
    pool_scaleT = din("pool_scaleT", [128, 2])
    w_pb = din("w_pb", [256, D])
    w_ab = din("w_ab", [256, D])
    w_o = din("w_o", [D, D])
    ident_in = din("ident", [128, 128])
    masks_in = din("masks", [128, 5 * 4 * 128])
    ropef = din("ropef", [128, 2])
    flags = din("flags", [128, 2])
    rcnt_in = din("rcnt", [128, 32])
    NITEM = 26
    wbf = nc.dram_tensor("wbf", [NITEM, 128, 2048], BF16, kind="Internal").ap()
    wada_bf = nc.dram_tensor("wada_bf", [6, 128, 8 * D], BF16, kind="Internal").ap()
    w3a_bf = nc.dram_tensor("w3a_bf", [22, 128, 2048], BF16, kind="Internal").ap()
    w3b_bf = nc.dram_tensor("w3b_bf", [11, 128, 2048], BF16, kind="Internal").ap()
    h1s = nc.dram_tensor("h1s", [NT, D], F32, kind="Internal").ap()
    h2s = nc.dram_tensor("h2s", [OWN, D], F32, kind="Internal").ap()
    y = nc.dram_tensor("y", [OWN, D], F32, kind="ExternalOutput").ap()

    with ExitStack() as top:
        S = Sched(nc, top)
        uid = [0]

        def sb(st, name, shape, dt):
            uid[0] += 1
            return st.enter_context(nc.sbuf_tensor("sb%d_%s" % (uid[0], name), list(shape), dt))
        TB = [top.enter_context(nc.psum_tensor("tb%d" % i, [128, 1024], BF16)) for i in range(2)]
        PB = [top.enter_context(nc.psum_tensor("pb%d" % i, [128, 512], F32)) for i in range(6)]
        rot = {"t": 0, "p": 0}

        def tbank():
            i = rot["t"] % 2
            rot["t"] += 1
            return TB[i], ("tb", i)

        def pbank(lo=0, hi=6):
            k = "p%d_%d" % (lo, hi)
            i = lo + rot.get(k, 0) % (hi - lo)
            rot[k] = rot.get(k, 0) + 1
            return PB[i], ("pb", i)

        ident = sb(top, "ident", [128, 128], BF16)
        onesb = sb(top, "onesb", [128, 128], BF16)
        condT = sb(top, "condT", [128, 8], BF16)
        cf = sb(top, "cf", [128, 8], F32)
        badT = sb(top, "badT", [128, 72], F32)
        gT = sb(top, "gT", [128, 24], F32)
        modT = sb(top, "modT", [128, 16], F32)
        modA = sb(top, "modA", [128, 8], F32)
        gtb = sb(top, "gtb", [128, D], F32)
        flg = sb(top, "flg", [128, 2], F32)

        S.dma("pool", ident[:], ident_in[:, :], writes=["ident"])
        S.dma("sp", cf[:], cT[:, :], writes=["cf"])
        S.dma("sp", badT[:], b_adaT[:, :], writes=["badT"])
        S.dma("sp", gT[:], gT_in[:, :], writes=["gT"])
        S.dma("sp", flg[:], flags[:, :], writes=["flg"])
        S.op("dve", lambda e: e.memset(onesb[:], 1.0), writes=["onesb"])
        S.op("act", lambda e: e.activation(out=condT[:], in_=cf[:], func=AF.Silu), reads=["cf"], writes=["condT"])

        def adaln(m0, gidx, gate_scale, after_loads=None):
            with ExitStack() as st:
                nslot = 2 if (m0 >= 3 and (mode == "full" or mode.startswith("mix"))) else 3
                wa_ = [sb(st, "wa%d" % i, [128, 8, D], BF16) for i in range(nslot)]
                wa = [wa_[i % nslot] for i in range(3)]
                bb = sb(st, "bb", [128, D], F32)
                condB = sb(st, "condB", [128, 8, 128], BF16)
                for dc in range(8):
                    S.op("dve", lambda e: e.tensor_scalar(out=condB[:, dc, :], in0=onesb[:], scalar1=condT[:, dc:dc + 1],
                                                          scalar2=None, op0=ALU.mult),
                         reads=["onesb", "condT"], writes=[("condB", dc)])
                S.dma("sp", bb[:], b_ada[0:1, (m0 + 2) * D:(m0 + 3) * D].partition_broadcast(128), writes=["bb"])
                def load_slice(mi):
                    pre = (m0 + mi >= 3) and mode in ("full",) or (m0 + mi >= 3 and mode.startswith("mix"))
                    if pre:
                        src = wada_bf[m0 + mi - 3].rearrange("p (dc f) -> p dc f", dc=8)
                    else:
                        src = w_ada[:, (m0 + mi) * D:(m0 + mi + 1) * D].rearrange("(dc p) f -> p dc f", p=128)
                    for hf in range(2):
                        S.dma("sp" if pre else "pool", wa[mi][:, :, hf * 512:(hf + 1) * 512], src[:, :, hf * 512:(hf + 1) * 512],
                              writes=[(("wa", mi % nslot), hf)])

                for mi in range(nslot):
                    load_slice(mi)
                if after_loads is not None and nslot == 3:
                    after_loads()
                for mi in range(3):
                    if mi >= 1 and mi - 1 + nslot < 3:
                        load_slice(mi - 1 + nslot)
                        if after_loads is not None:
                            after_loads()
                    m = m0 + mi
                    w = wa[mi]
                    wk = ("wa", mi % nslot)
                    if mi < 2:
                        ps, pk = pbank()

                        def f(e):
                            last = None
                            for fc in range(8):
                                for dc in range(8):
                                    last = e.matmul(ps[:, fc:fc + 1], lhsT=w[:, dc, fc * 128:(fc + 1) * 128],
                                                    rhs=condT[:, dc:dc + 1], start=(dc == 0), stop=(dc == 7))
                            return last

                        S.op("pe", f, reads=[(wk, 0), (wk, 1), "condT"], writes=[pk])
                        S.op("dve", lambda e: e.tensor_tensor(out=modT[:, mi * 8:(mi + 1) * 8], in0=ps[:, 0:8],
                                                              in1=badT[:, m * 8:(m + 1) * 8], op=ALU.add),
                             reads=[pk, "badT"], writes=[("modT", mi)])
                    else:
                        for hf in range(2):
                            ps, pk = pbank()

                            def f(e):
                                last = None
                                for dc in range(8):
                                    last = e.matmul(ps[:, :], lhsT=condB[:, dc, :], rhs=w[:, dc, hf * 512:(hf + 1) * 512],
                                                    start=(dc == 0), stop=(dc == 7))
                                return last

                            S.op("pe", f, reads=[(wk, hf)] + [("condB", dc) for dc in range(8)], writes=[pk])
                            S.op("dve", lambda e: e.tensor_tensor(out=gtb[:, hf * 512:(hf + 1) * 512], in0=ps[:, :],
                                                                  in1=bb[:, hf * 512:(hf + 1) * 512], op=ALU.add),
                                 reads=[pk, "bb"], writes=[("gtb", hf)])
                            if gate_scale != 1.0:
                                S.op("dve", lambda e: e.tensor_scalar(out=gtb[:, hf * 512:(hf + 1) * 512],
                                                                      in0=gtb[:, hf * 512:(hf + 1) * 512],
                                                                      scalar1=gate_scale, scalar2=None, op0=ALU.mult),
                                     reads=[("gtb", hf)], writes=[("gtb", hf)])
                S.op("dve", lambda e: e.scalar_tensor_tensor(out=modA[:], in0=modT[:, 8:16], scalar=1.0,
                                                             in1=gT[:, gidx * 8:(gidx + 1) * 8], op0=ALU.add, op1=ALU.mult),
                     reads=[("modT", 1), "gT"], writes=["modA"])
                S.drain(["bb"] + [("condB", dc) for dc in range(8)] + [(("wa", mi), hf) for mi in range(nslot) for hf in range(2)])

        def rsqrt_op(ss_ap, out_ap, rkeys, wkeys, n):
            S.op("pool", lambda e: e.tensor_scalar(out=ss_ap, in0=ss_ap, scalar1=1.0 / D, scalar2=EPS, op0=ALU.mult, op1=ALU.add),
                 reads=rkeys, writes=rkeys)
            S.op("pool", lambda e: e.tensor_tensor(out=out_ap, in0=ss_ap, in1=mhalf[:, 0:n], op=ALU.pow),
                 reads=rkeys + ["mhalf"], writes=wkeys)

        def norm_stats(hts, keys, ss, rstd, junk, n, junk_keys=()):
            for j in range(n):
                S.op("act", lambda e: e.activation(out=junk[:], in_=hts[j], func=AF.Square, accum_out=ss[:, j:j + 1]),
                     reads=[keys[j]], writes=["junk", ("ss", j)] + list(junk_keys))
            rsqrt_op(ss[:, 0:n], rstd[:, 0:n], [("ss", j) for j in range(n)], [("rstd", j) for j in range(n)], n)

        def norm_to_uT(ht, hkey, rstd_col, rkey, xh, xkey, uT, ukey_fn, col0):
            S.op("act", lambda e: e.activation(out=xh[:], in_=ht, func=AF.Copy, scale=rstd_col),
                 reads=[hkey, rkey], writes=[xkey])
            tb, tk = tbank()

            def f(e):
                last = None
                for dc in range(8):
                    last = e.transpose(tb[:, dc * 128:(dc + 1) * 128], xh[:, dc * 128:(dc + 1) * 128], ident[:])
                return last

            S.op("pe", f, reads=[xkey, "ident"], writes=[tk])
            for dc in range(8):
                S.op("dve", lambda e: e.tensor_scalar(out=uT[:, dc, col0:col0 + 128], in0=tb[:, dc * 128:(dc + 1) * 128],
                                                      scalar1=modA[:, dc:dc + 1], scalar2=modT[:, dc:dc + 1],
                                                      op0=ALU.mult, op1=ALU.add),
                     reads=[tk, "modA", ("modT", 0)], writes=[ukey_fn(dc)])

        mhalf = sb(top, "mhalf", [128, 4], F32)
        S.op("dve", lambda e: e.memset(mhalf[:], -0.5), writes=["mhalf"])

        def wview(i):
            return wbf[i].rearrange("p (dc f) -> p dc f", dc=8)

        IT_P, IT_Q, IT_K, IT_V, IT_GP, IT_GA, IT_BR = 0, 1, 4, 7, 10, 14, 18
        cvt_specs = []

        def std_item(i, c0):
            cvt_specs.append((wview(i), w_in[:, c0:c0 + 256].rearrange("(dc p) f -> p dc f", p=128)))

        std_item(IT_P, 0)
        for g_ in range(3):
            std_item(IT_Q + g_, 256 + g_ * 256)
            std_item(IT_K + g_, 1024 + g_ * 256)
            std_item(IT_V + g_, 1792 + g_ * 256)
        for dp_ in range(4):
            std_item(IT_GP + dp_, 2560 + dp_ * 256)
            std_item(IT_GA + dp_, 3584 + dp_ * 256)
            cvt_specs.append((wview(IT_BR + dp_)[:, 0:2, :], w_pb[:, dp_ * 256:(dp_ + 1) * 256].rearrange("(c p) f -> p c f", p=128)))
            cvt_specs.append((wview(IT_BR + dp_)[:, 2:4, :], w_ab[:, dp_ * 256:(dp_ + 1) * 256].rearrange("(c p) f -> p c f", p=128)))

        IT_WO = 22
        for q4_ in range(4):
            cvt_specs.append((wview(IT_WO + q4_), w_o[:, q4_ * 256:(q4_ + 1) * 256].rearrange("(dc p) f -> p dc f", p=128)))
        for m_ in range(3, 9):
            src_ = w_ada[:, m_ * D:(m_ + 1) * D].rearrange("(dc p) f -> p dc f", p=128)
            dst_ = wada_bf[m_ - 3].rearrange("p (dc f) -> p dc f", dc=8)
            for hf_ in range(2):
                cvt_specs.append((dst_[:, :, hf_ * 512:(hf_ + 1) * 512], src_[:, :, hf_ * 512:(hf_ + 1) * 512]))

        def ffn_phase(src, dst, ntok, wa_d, wb_d, m0, gidx, final, cvt=False, pre_bf=None):
            with ExitStack() as st:
                W1 = sb(st, "W1", [128, 8, 2 * DFF], BF16)
                srcw = wa_d.rearrange("(dc p) f -> p dc f", p=128)
                srcw2 = wb_d.rearrange("(f p) d -> p f d", p=128)
                def wloads():
                    q_ = "sp" if pre_bf is not None else "pool"
                    sw1 = pre_bf[0] if pre_bf is not None else srcw
                    sw2 = pre_bf[1] if pre_bf is not None else srcw2
                    for fg in range(11):
                        for ab in range(2):
                            c0 = ab * DFF + fg * 256
                            if pre_bf is not None:
                                src_ = sw1[ab * 11 + fg].rearrange("p (dc f) -> p dc f", dc=8)
                            else:
                                src_ = sw1[:, :, c0:c0 + 256]
                            S.dma(q_, W1[:, :, c0:c0 + 256], src_, writes=[("W1", ab, fg)])

                def wloads2():
                    q_ = "sp" if pre_bf is not None else "pool"
                    sw2 = pre_bf[1] if pre_bf is not None else srcw2
                    for fg in range(11):
                        if pre_bf is not None:
                            src_ = sw2[fg].rearrange("p (a d) -> p a d", a=2)
                        else:
                            src_ = sw2[:, 2 * fg:2 * fg + 2, :]
                        S.dma(q_, W2[:, 2 * fg:2 * fg + 2, :], src_, writes=[("W2", fg)])

                early = True
                if early:
                    hb = [sb(st, "hb%d" % i, [128, D], F32) for i in range(4)]
                    ss = sb(st, "ss", [128, 4], F32)
                    rstd = sb(st, "rstd", [128, 4], F32)
                    junk = sb(st, "junk", [128, D], BF16)
                    for j in range(4):
                        S.dma("sp", hb[j][:], src[j * 128:(j + 1) * 128, :], writes=[("hb", j)])
                    norm_stats([hb[j][:] for j in range(4)], [("hb", j) for j in range(4)], ss, rstd, junk, 4)
                adaln(m0, gidx, 0.5, wloads)
                W2 = sb(st, "W2", [128, NF, D], BF16)
                wloads2()
                if not early:
                    hb = [sb(st, "hb%d" % i, [128, D], F32) for i in range(4)]
                NHR = 3 if final else 2
                hr = [sb(st, "hr%d" % i, [128, D], F32) for i in range(NHR)]
                xh = [sb(st, "xh0", [128, D], BF16)] * 2
                uT = sb(st, "uT", [128, 8, 512], BF16)
                gTt = sb(st, "gTt", [128, NF, 512], BF16)
                sa = [sb(st, "sa%d" % i, [128, 512], F32) for i in range(2)]
                if not early:
                    ss = sb(st, "ss", [128, 4], F32)
                    rstd = sb(st, "rstd", [128, 4], F32)
                    junk = sb(st, "junk", [128, D], BF16)
                if final:
                    gfb = sb(st, "gfb", [128, D], F32)
                    S.dma("sp", gfb[:], g_final[0:1, :].partition_broadcast(128), writes=["gfb"])
                    ss2 = sb(st, "ss2", [128, 1], F32)
                    rs2 = sb(st, "rs2", [128, 1], F32)
                ntile = ntok // 512

                def pre_stats(t):
                    for j in range(4):
                        S.dma("sp", hb[j][:], src[t * 512 + j * 128:t * 512 + (j + 1) * 128, :], writes=[("hb", j)])
                    norm_stats([hb[j][:] for j in range(4)], [("hb", j) for j in range(4)], ss, rstd, junk, 4)

                def pre_sub(j):
                    norm_to_uT(hb[j][:], ("hb", j), rstd[:, j:j + 1], ("rstd", j), xh[0], ("xh", 0),
                               uT, lambda dc: ("uT", dc, j), j * 128)

                if not early:
                    pre_stats(0)
                for j in range(4):
                    pre_sub(j)
                for t in range(ntile):
                    t0 = t * 512
                    if cvt and t >= 1:
                        for _ in range(4):
                            if cvt_specs:
                                d_, s_src = cvt_specs.pop(0)
                                S.dma("pool", d_, s_src, writes=["wbf"])
                    ukeys = [("uT", dc, j) for dc in range(8) for j in range(4)]
                    if STOP[0] == "a":
                        S.barrier()
                        return
                    for f_ in range(NF):
                        pa, pak = pbank(0, 4)
                        pbk_, pbk = pbank(0, 4)
                        fg, fo = f_ // 2, (f_ % 2) * 128

                        def fa(e):
                            last = None
                            for dc in range(8):
                                last = e.matmul(pa[:, :], lhsT=W1[:, dc, f_ * 128:(f_ + 1) * 128], rhs=uT[:, dc, :],
                                                start=(dc == 0), stop=(dc == 7))
                            return last

                        def fb(e):
                            last = None
                            for dc in range(8):
                                last = e.matmul(pbk_[:, :], lhsT=W1[:, dc, DFF + f_ * 128:DFF + (f_ + 1) * 128],
                                                rhs=uT[:, dc, :], start=(dc == 0), stop=(dc == 7))
                            return last

                        S.op("pe", fa, reads=ukeys + [("W1", 0, fg)], writes=[pak])
                        S.op("pe", fb, reads=ukeys + [("W1", 1, fg)], writes=[pbk])
                        s_ = sa[f_ % 2]
                        S.op("act", lambda e: e.activation(out=s_[:], in_=pa[:, :], func=AF.Silu),
                             reads=[pak], writes=[("sa", f_ % 2)])
                        S.op("dve", lambda e: e.tensor_tensor(out=gTt[:, f_, :], in0=s_[:], in1=pbk_[:, :], op=ALU.mult),
                             reads=[("sa", f_ % 2), pbk], writes=[("gTt", f_)])
                        if f_ == 8 and t + 1 < ntile and STOP[0] == "":
                            pre_stats(t + 1)
                    if STOP[0] == "b":
                        S.barrier()
                        return
                    if t == 0:
                        for f_ in range(NF):
                            S.op("pool", lambda e: e.tensor_tensor(out=W2[:, f_, :], in0=W2[:, f_, :], in1=gtb[:], op=ALU.mult),
                                 reads=[("W2", f_ // 2), ("gtb", 0), ("gtb", 1)], writes=[("W2", f_ // 2)])
                    for j in range(4):
                        hk = (4 * t + j) % NHR
                        r_ = hr[hk]
                        S.dma("sp", r_[:], src[t0 + j * 128:t0 + (j + 1) * 128, :], writes=[("hr", hk)])
                        for hf in range(2):
                            po, pok = pbank(4, 6)

                            def fo_(e):
                                last = None
                                for f_ in range(NF):
                                    last = e.matmul(po[:, :], lhsT=gTt[:, f_, j * 128:(j + 1) * 128],
                                                    rhs=W2[:, f_, hf * 512:(hf + 1) * 512], start=(f_ == 0), stop=(f_ == NF - 1))
                                return last

                            S.op("pe", fo_, reads=[("gTt", f_) for f_ in range(NF)] + [("W2", fg) for fg in range(11)],
                                 writes=[pok])
                            S.op("dve", lambda e: e.tensor_tensor(out=r_[:, hf * 512:(hf + 1) * 512], in0=po[:, :],
                                                                  in1=r_[:, hf * 512:(hf + 1) * 512], op=ALU.add),
                                 reads=[pok, ("hr", hk)], writes=[("hr", hk)])
                        if final:
                            S.op("dve", lambda e: e.scalar_tensor_tensor(out=junk[:], in0=r_[:], scalar=1.0, in1=r_[:],
                                                                         op0=ALU.mult, op1=ALU.mult, accum_out=ss2[:, 0:1]),
                                 reads=[("hr", hk)], writes=["junk", "ss2"])
                            rsqrt_op(ss2[:], rs2[:], ["ss2"], ["rs2"], 1)
                            S.op("dve", lambda e: e.scalar_tensor_tensor(out=r_[:], in0=r_[:], scalar=rs2[:, 0:1], in1=gfb[:],
                                                                         op0=ALU.mult, op1=ALU.mult),
                                 reads=[("hr", hk), "rs2", "gfb"], writes=[("hr", hk)])
                        S.dma("sp", dst[t0 + j * 128:t0 + (j + 1) * 128, :], r_[:], reads=[("hr", hk)], writes=[("dst", t, j)])
                        if t + 1 < ntile:
                            pre_sub(j)
                S.barrier()

        mode = dbg or "full"
        STOP = [""]

        def chk(tag):
            if mode == tag and not S.off:
                S.barrier()
                S.off = True
        if mode in ("ffn1a", "ffn1b"):
            STOP[0] = mode[-1]
            ffn_phase(x, y, 512, w1a, w1b, 0, 0, False)
            return nc
        if mode == "ada":
            adaln(0, 0, 0.5)
            S.dma("sp", y[0:128, :], gtb[:], reads=[("gtb", 0), ("gtb", 1)], writes=["yy"])
            S.barrier()
            return nc
        if mode == "ffn1":
            ffn_phase(x, y, 512, w1a, w1b, 0, 0, False)
            return nc
        if mode == "ffn3":
            ffn_phase(x, y, 512, w3a, w3b, 6, 2, True)
            return nc
        if mode.startswith("mix"):
            h1s = x
            h2s = y
            for d_, s_src in cvt_specs:
                S.dma("pool", d_, s_src, writes=["wbf"])
            S.barrier()
        if mode == "full":
            ffn_phase(x, h1s, NT, w1a, w1b, 0, 0, False, cvt=True)

        def mix_phase():
            with ExitStack() as st:
                hb = [sb(st, "mhb%d" % i, [128, D], F32) for i in range(4)]
                xh = [sb(st, "mxh0", [128, D], BF16)] * 2
                ss = sb(st, "mss", [128, 4], F32)
                rstd = sb(st, "mrstd", [128, 4], F32)
                junk = sb(st, "mjunk", [128, D], BF16)
                for j in range(4):
                    S.dma("sp", hb[j][:], h1s[j * 128:(j + 1) * 128, :], writes=[("hb", j)])
                norm_stats([hb[j][:] for j in range(4)], [("hb", j) for j in range(4)], ss, rstd, junk, 4)
                adaln(3, 1, 1.0)
                masks = sb(st, "masks", [128, 5, 512], BF16)
                for mi_ in range(5):
                    S.dma("pool", masks[:, mi_, :], masks_in[:, mi_ * 512:(mi_ + 1) * 512], writes=["masks"])
                rf = sb(st, "rf", [128, 2], F32)
                S.dma("sp", rf[:], ropef[:, :], writes=["rf"])
                rcn = sb(st, "rcn", [128, 32], F32)
                S.dma("sp", rcn[:], rcnt_in[:, :], writes=["rcn"])
                psc = sb(st, "psc", [128, 2], F32)
                S.dma("sp", psc[:], pool_scaleT[:, :], writes=["psc"])
                wpool = sb(st, "wpool", [128, 2, 128], BF16)
                S.dma("pool", wpool[:], w_pool_bd[:, :, :], writes=["wpool"])
                uT = sb(st, "uT2", [128, 8, ST], BF16)
                QT = sb(st, "QT", [128, 4, ST], BF16)
                for h_ in range(4):
                    z0 = 64 if h_ % 2 == 0 else 0
                    S.op("pool", lambda e: e.memset(QT[z0:z0 + 64, h_, :], 0.0), writes=[("QTz", h_)])
                KT = [sb(st, "KT%d" % g, [128, 2, RING[g]], BF16) for g in range(3)]
                VT = [sb(st, "VT%d" % g, [128, NVB[g], 256], BF16) for g in range(3)]
                acc = sb(st, "acc", [128, 4, ST], F32)
                merged = acc[:].bitcast(BF16).rearrange("p a (b c) -> p (a b) c", b=2)
                yat = sb(st, "yat", [128, 2, ST], BF16)
                ypl = sb(st, "ypl", [128, 2, ST], BF16)
                PT = [sb(st, "PT%d" % i, [128, 512], BF16) for i in range(2)]
                NWS = 6
                WS = [sb(st, "WS%d" % i, [128, 8, 256], BF16) for i in range(NWS)]
                wsr = [0]
                qbuf = [sb(st, "qbuf%d" % i, [128, 512], BF16) for i in range(2)]
                permT = sb(st, "permT", [128, 128], BF16)
                S.dma("pool", permT[:], perm_in[:, :], writes=["permT"])
                Ctab = sb(st, "Ctab", [128, ST], F32)
                Stab = sb(st, "Stab", [128, ST], F32)
                posi = sb(st, "posi", [128, 512], I32)
                scr = sb(st, "scr", [128, 1024], F32)
                ang0 = ang = scr[:, 0:512]
                kf0 = kf = scr[:, 512:1024]
                rD = scr[:].rearrange("p (a b) -> p a b", a=2)
                pT = sb(st, "pT", [128, 2, 16 + 512], F32)
                w2_ = sb(st, "pw2", [128, 2, 16 + 512], F32)
                w4_ = sb(st, "pw4", [128, 2, 16 + 512], F32)
                dT = sb(st, "dT", [128, 2, 512], BF16)
                tmp = sb(st, "mtmp", [128, 512], F32)
                tmp2 = sb(st, "mtmp2", [128, 512], F32)
                sg = [sb(st, "sg%d" % i, [128, 512], F32) for i in range(2)]
                S.op("pool", lambda e: e.memset(pT[:, :, 0:16], 0.0), writes=["pThist"])

                ukeys = [("uT", dc, jj) for dc in range(8) for jj in range(8)]
                ukeys1 = [("uT1", dc, jj) for dc in range(8) for jj in range(8)] + ["acc"]
                UT = [uT]
                UK = [ukeys]
                PRE = []

                def pre_next(n=1):
                    for _ in range(n):
                        if PRE:
                            PRE.pop(0)()
                wplan, wiss, dry = [], [0], [True]
                LA = 3

                def wload(item):
                    idx = wsr[0]
                    wsr[0] += 1
                    if dry[0]:
                        wplan.append(item)
                    else:
                        while wiss[0] < min(len(wplan), idx + LA + 1):
                            k = wiss[0]
                            if IT_BR <= wplan[k] < IT_BR + 4:
                                S.dma("pool", WS[k % NWS][:, 0:4, :], wview(wplan[k])[:, 0:4, :], writes=[("WS", k % NWS)])
                            else:
                                S.dma("pool", WS[k % NWS][:], wview(wplan[k]), writes=[("WS", k % NWS)])
                            wiss[0] += 1
                    return WS[idx % NWS], ("WS", idx % NWS)

                def wsrc(mat, c0):
                    return mat[:, c0:c0 + 256].rearrange("(dc p) f -> p dc f", p=128)

                def cols(buf, ps_, c2, g, jbase, ring, qb):
                    if g == 0:
                        r0 = (jbase + 128 * qb) % ring
                        return buf[ps_, c2, r0:r0 + 128]
                    if g == 1:
                        u, rho = qb // 4, qb % 4
                        r0 = (jbase + 512 * u) % ring + rho
                        return buf[ps_, c2, r0:r0 + 4 * 127 + 1:4]
                    r0 = jbase % ring
                    if buf is QT:
                        return buf[ps_, c2, 128 * qb:128 * qb + 128]
                    return buf[ps_, c2, r0:r0 + ST].rearrange("p (m e) -> p e m", e=16)[:, 2 * qb:2 * qb + 2, :]

                def vblk(g, s, qb):
                    if g == 0:
                        return (8 * s + qb) % 12
                    if g == 1:
                        u, rho = qb // 4, qb % 4
                        return ((2 * s + u) % 3) * 4 + rho
                    return (s % 3) * 8 + qb

                def proj_fm(wt, wk, ukeys, tt):
                    ps, pk = pbank()

                    def f(e):
                        last = None
                        for dc in range(8):
                            last = e.matmul(ps[:, :], lhsT=wt[dc], rhs=UT[0][:, dc, tt * 512:(tt + 1) * 512],
                                            start=(dc == 0), stop=(dc == 7))
                        return last

                    S.op("pe", f, reads=ukeys + [wk], writes=[pk])
                    return ps, pk

                def qk_proj(g, isk, tt, j0):
                        wn, wnk = wload((IT_K if isk else IT_Q) + g)
                        pas = []
                        for c2 in range(2):
                            pa, pak = proj_fm([wn[:, dc, c2 * 128:(c2 + 1) * 128] for dc in range(8)], wnk, UK[0], tt)
                            qb_ = qbuf[c2]
                            S.op("act", lambda e: e.activation(out=qb_[:], in_=pa[:, :], func=AF.Copy), reads=[pak], writes=[("qb", c2)])
                            pas.append((pa, pak))
                        for c2 in range(2):
                            pa, pak = pas[c2]
                            qb_ = qbuf[c2]
                            pb_, pbk = pbank()
                            S.op("pe", lambda e: e.matmul(pb_[:, :], lhsT=permT[:], rhs=qb_[:], start=True, stop=True),
                                 reads=[("qb", c2), "permT"], writes=[pbk])
                            tA, tB, k1, k2 = (tmp, tmp2, "tmp", "tmp2") if c2 == 0 else (sg[0], sg[1], ("sg", 0), ("sg", 1))
                            S.op("dve", lambda e: e.tensor_tensor(out=tA[:], in0=pa[:, :], in1=Ctab[:, tt * 512:(tt + 1) * 512], op=ALU.mult),
                                 reads=[pak, ("tab", 1, tt), ("qb", c2)], writes=[k1])
                            S.op("dve", lambda e: e.tensor_tensor(out=tB[:], in0=pb_[:, :], in1=Stab[:, tt * 512:(tt + 1) * 512], op=ALU.mult),
                                 reads=[pbk, ("tab", 0, tt)], writes=[k2])
                            r0 = (j0 + tt * 512) % RING[g]
                            if isk:
                                dkey = ("KT", g, c2, r0 // 512)
                                if g == 2:
                                    b0 = (j0 % RING[2])
                                    dst_ap = KT[2][:, c2, b0:b0 + ST].rearrange("p (e m) -> p m e", e=16)[:, 32 * tt:32 * tt + 32, :]
                                else:
                                    dst_ap = KT[g][:, c2, r0:r0 + 512]
                            else:
                                dkey = ("QT", c2, tt)
                            for hh in ((None,) if isk else (0, 1)):
                                pr = slice(0, 128) if isk else slice(64 * hh, 64 * hh + 64)
                                if not isk:
                                    if g == 2:
                                        dst_ap = QT[pr, 2 * c2 + hh, :].rearrange("p (e m) -> p m e", e=16)[:, 32 * tt:32 * tt + 32, :]
                                    else:
                                        dst_ap = QT[pr, 2 * c2 + hh, tt * 512:(tt + 1) * 512]
                                if g == 2:
                                    i0 = tA[pr, :].rearrange("p (m e) -> p m e", e=16)
                                    i1 = tB[pr, :].rearrange("p (m e) -> p m e", e=16)
                                else:
                                    i0, i1 = tA[pr, :], tB[pr, :]
                                S.op("dve", lambda e: e.tensor_tensor(out=dst_ap, in0=i0, in1=i1, op=ALU.add),
                                     reads=[k1, k2, ("QTz", 0)], writes=[dkey if isk else ("QT", c2, tt, hh)])

                def rope_tables(s_, tt):
                    c0 = s_ * ST + tt * 512
                    S.dma("sp", posi[:], pos[0:1, c0:c0 + 512].partition_broadcast(128), writes=["posi"])
                    bufs = [(ang0, kf0, "ang", "kf"), (tmp[:], tmp2[:], "tmp", "tmp2")]
                    for which in (0, 1):
                        ang, kf, ka, kk = bufs[which]
                        S.op("dve", lambda e, ang=ang: e.tensor_copy(out=ang, in_=posi[:]), reads=["posi"], writes=[ka])
                        S.op("dve", lambda e, which=which, ang=ang: e.tensor_scalar(out=ang, in0=ang, scalar1=rf[:, which:which + 1],
                                                              scalar2=(np.pi / 2 if which else 0.0), op0=ALU.mult, op1=ALU.add),
                             reads=[ka, "rf"], writes=[ka])
                    for which, tab in ((0, Stab), (1, Ctab)):
                        ang, kf, ka, kk = bufs[which]
                        S.op("dve", lambda e, ang=ang, kf=kf: e.tensor_scalar(out=kf, in0=ang, scalar1=float(1.0 / (2 * np.pi)),
                                                              scalar2=None, op0=ALU.mult), reads=[ka], writes=[kk])
                        S.op("dve", lambda e, kf=kf: e.tensor_copy(out=posi[:], in_=kf), reads=[kk], writes=["posi"])
                        S.op("dve", lambda e, kf=kf: e.tensor_copy(out=kf, in_=posi[:]), reads=["posi"], writes=[kk])
                        S.op("dve", lambda e, ang=ang, kf=kf: e.scalar_tensor_tensor(out=ang, in0=kf, scalar=-TWO_PI_HI, in1=ang,
                                                                     op0=ALU.mult, op1=ALU.add),
                             reads=[kk, ka], writes=[ka])
                        S.op("dve", lambda e, ang=ang, kf=kf: e.scalar_tensor_tensor(out=ang, in0=kf, scalar=-float(TWO_PI_LO), in1=ang,
                                                                     op0=ALU.mult, op1=ALU.add),
                             reads=[kk, ka], writes=[ka])
                        S.op("dve", lambda e, ang=ang: e.tensor_scalar(out=ang, in0=ang, scalar1=-3.14159, scalar2=3.14159,
                                                              op0=ALU.max, op1=ALU.min), reads=[ka], writes=[ka])
                        S.op("act", lambda e, tab=tab, tt=tt, ang=ang: e.activation(out=tab[:, tt * 512:(tt + 1) * 512], in_=ang, func=AF.Sin),
                             reads=[ka], writes=[("tab", which, tt)])

                DT1 = [("dT", 1, a_, b_) for a_ in range(2) for b_ in range(2)]

                def pre_stats(s_, q4):
                    for j in range(4):
                        r0 = s_ * ST + q4 * 512 + j * 128
                        S.dma("sp", hb[j][:], h1s[r0:r0 + 128, :], writes=[("hb", j)])
                    norm_stats([hb[j][:] for j in range(4)], [("hb", j) for j in range(4)], ss, rstd, junk, 4, junk_keys=DT1)

                def pre_sub(q4, j, alt=False):
                    norm_to_uT(hb[j][:], ("hb", j), rstd[:, j:j + 1], ("rstd", j), xh[0], ("xh", 0),
                               merged if alt else uT, lambda dc: (("uT1" if alt else "uT"), dc, q4 * 4 + j), q4 * 512 + j * 128)

                def pre_pieces(s_, alt):
                    out = []
                    for q4 in range(2):
                        out.append(lambda q4=q4: pre_stats(s_, q4))
                        for j in range(4):
                            out.append(lambda q4=q4, j=j: pre_sub(q4, j, alt))
                    return out

                cvt3 = []
                if mode == "full":
                    s3a = w3a.rearrange("(dc p) f -> p dc f", p=128)
                    s3b = w3b.rearrange("(f p) d -> p f d", p=128)
                    for fg in range(11):
                        for ab in range(2):
                            c0 = ab * DFF + fg * 256
                            cvt3.append((w3a_bf[ab * 11 + fg].rearrange("p (dc f) -> p dc f", dc=8), s3a[:, :, c0:c0 + 256]))
                    for fg in range(11):
                        cvt3.append((w3b_bf[fg].rearrange("p (a d) -> p a d", a=2), s3b[:, 2 * fg:2 * fg + 2, :]))

                def run_tiles():
                    for s in range(NT // ST):
                        halo = s < 2
                        j0 = s * ST
                        groups = [2] if s == 0 else [0, 1, 2]

                        if s == 0:
                            for pc in pre_pieces(0, False)[1:]:
                                pc()
                        UT[0], UK[0] = (merged, ukeys1) if s == 1 else (uT, ukeys)
                        if halo:
                            PRE[:] = pre_pieces(s + 1, s + 1 == 1)
                        if s == 0 and not S.off:
                            for hf_ in range(2):
                                for q2 in range(2):
                                    S.dma("sp", WOq[hf_][0][:, :, q2 * 256:(q2 + 1) * 256], wview(IT_WO + hf_ * 2 + q2), writes=[("WO", hf_)])
                            for hf_ in range(2):
                                for dc_ in range(8):
                                    S.op("dve", lambda e: e.tensor_tensor(out=WOq[hf_][0][:, dc_, :], in0=WOq[hf_][0][:, dc_, :],
                                                                          in1=gtb[:, hf_ * 512:(hf_ + 1) * 512], op=ALU.mult),
                                         reads=[("WO", hf_), ("gtb", hf_)], writes=[("WO", hf_)])
                        chk("mixA%d" % s)
                        for tt in range(2):
                            c0 = j0 + tt * 512
                            if s < 3:
                                rope_tables(s, tt)
                            def p_and_adds():
                                if s >= 1:
                                    wt, wk = wload(IT_P)
                                    for c2 in range(2):
                                        ps, pk = proj_fm([wt[:, dc, c2 * 128:(c2 + 1) * 128] for dc in range(8)], wk, UK[0], tt)
                                        if halo:
                                            S.op("dve", lambda e: e.tensor_scalar(out=pT[:, c2, 16:528], in0=ps[:, :], scalar1=flg[:, 0:1],
                                                                                  scalar2=None, op0=ALU.mult),
                                                 reads=[pk, "flg"], writes=[("pTc", c2)])
                                        else:
                                            S.op("act", lambda e: e.activation(out=pT[:, c2, 16:528], in_=ps[:, :], func=AF.Copy),
                                                 reads=[pk], writes=[("pTc", c2)])
                                if s >= 1:
                                    if not halo:
                                        dTt = dT if tt == 0 else junk[:].rearrange("p (a b) -> p a b", a=2)
                                        pk_all = [("pTc", 0), ("pTc", 1), "pThist"]
                                        S.op("pool", lambda e: e.tensor_tensor(out=w2_[:, :, 1:528], in0=pT[:, :, 1:528], in1=pT[:, :, 0:527], op=ALU.add),
                                             reads=pk_all, writes=["w2", "w8"])
                                        S.op("pool", lambda e: e.tensor_tensor(out=w4_[:, :, 3:528], in0=w2_[:, :, 3:528], in1=w2_[:, :, 1:526], op=ALU.add),
                                             reads=["w2"], writes=["w4", "w16"])
                                        S.op("pool", lambda e: e.tensor_tensor(out=w2_[:, 1, 7:528], in0=w4_[:, 1, 7:528], in1=w4_[:, 1, 3:524], op=ALU.add),
                                             reads=["w4"], writes=["w8"])
                                        S.op("pool", lambda e: e.tensor_tensor(out=w4_[:, 1, 15:528], in0=w2_[:, 1, 15:528], in1=w2_[:, 1, 7:520], op=ALU.add),
                                             reads=["w8"], writes=["w16"])
                            def k_block():
                                for g in groups:
                                    qk_proj(g, 1, tt, j0)
                                if halo:
                                    pre_next(2)
                            if tt == 0:
                                p_and_adds()
                                k_block()
                            else:
                                k_block()
                                p_and_adds()
                            if s >= 1:
                                if not halo:
                                    dTt = dT if tt == 0 else junk[:].rearrange("p (a b) -> p a b", a=2)
                                    pk_all = [("pTc", 0), ("pTc", 1), "pThist"]
                                    S.op("dve", lambda e: e.scalar_tensor_tensor(out=dTt[0:64, 0, :], in0=w2_[0:64, 0, 16:528], scalar=0.5,
                                                                                 in1=pT[0:64, 0, 16:528], op0=ALU.mult, op1=ALU.subtract),
                                         reads=["w2"] + pk_all, writes=[("dT", tt, 0, 0)])
                                    S.op("dve", lambda e: e.scalar_tensor_tensor(out=dTt[64:128, 0, :], in0=w4_[64:128, 0, 16:528], scalar=0.25,
                                                                                 in1=pT[64:128, 0, 16:528], op0=ALU.mult, op1=ALU.subtract),
                                         reads=["w4"] + pk_all, writes=[("dT", tt, 0, 1)])
                                    S.op("dve", lambda e: e.scalar_tensor_tensor(out=dTt[0:64, 1, :], in0=w2_[0:64, 1, 16:528], scalar=0.125,
                                                                                 in1=pT[0:64, 1, 16:528], op0=ALU.mult, op1=ALU.subtract),
                                         reads=["w8", "w16"] + pk_all, writes=[("dT", tt, 1, 0)])
                                    S.op("dve", lambda e: e.scalar_tensor_tensor(out=dTt[64:128, 1, :], in0=w4_[64:128, 1, 16:528], scalar=0.0625,
                                                                                 in1=pT[64:128, 1, 16:528], op0=ALU.mult, op1=ALU.subtract),
                                         reads=["w16"] + pk_all, writes=[("dT", tt, 1, 1)])
                                    if s == 2 and tt == 0:
                                        for c2, hfp, wb_, wkey in ((0, 0, w2_, "w2"), (0, 1, w4_, "w4"), (1, 0, w2_, "w8"), (1, 1, w4_, "w16")):
                                            pr = slice(hfp * 64, hfp * 64 + 64)
                                            S.op("dve", lambda e: e.tensor_tensor(out=tmp[pr, 0:16], in0=wb_[pr, c2, 16:32],
                                                                                  in1=rcn[pr, c2 * 16:(c2 + 1) * 16], op=ALU.mult),
                                                 reads=["w2", "w4", "w8", "w16", "rcn"], writes=["tmp"])
                                            S.op("dve", lambda e: e.tensor_tensor(out=dTt[pr, c2, 0:16], in0=tmp[pr, 0:16],
                                                                                  in1=pT[pr, c2, 16:32], op=ALU.subtract),
                                                 reads=["tmp"] + pk_all, writes=[("dT", tt, c2, hfp)])
                            if s >= 1:
                                S.op("pool", lambda e: e.tensor_copy(out=pT[:, :, 0:16], in_=pT[:, :, 512:528]),
                                     reads=[("pTc", 0), ("pTc", 1)] + [("dT", tt, a_, b_) for a_ in range(2) for b_ in range(2)] + ["w2"],
                                     writes=["pThist"])
                        chk("mixB%d" % s)
                        for g in groups:
                            wv, wvk = wload(IT_V + g)
                            for qb in range(8):
                                ps, pk = pbank()
                                blk = vblk(g, s, qb)

                                def f(e):
                                    last = None
                                    for dc in range(8):
                                        if g == 2:
                                            for e2 in range(2):
                                                c_ = 2 * qb + e2
                                                last = e.matmul(ps[e2 * 64:e2 * 64 + 64, 0:256], lhsT=UT[0][:, dc, c_:c_ + 16 * 63 + 1:16],
                                                                rhs=wv[:, dc, :], start=(dc == 0), stop=(dc == 7))
                                        else:
                                            last = e.matmul(ps[:, 0:256], lhsT=cols(UT[0], slice(0, 128), dc, g, 0, ST, qb), rhs=wv[:, dc, :],
                                                            start=(dc == 0), stop=(dc == 7))
                                    return last

                                S.op("pe", f, reads=UK[0] + [wvk], writes=[pk])
                                S.op("act", lambda e: e.activation(out=VT[g][:, blk, :], in_=ps[:, 0:256], func=AF.Copy),
                                     reads=[pk], writes=[("VT", g, blk)])
                                if halo:
                                    pre_next(1)
                                elif cvt3 and not S.off:
                                    d_, s_src = cvt3.pop(0)
                                    S.dma("pool", d_, s_src, writes=["w3bf"])
                            chk("mixV%d%d" % (s, g))
                            if halo:
                                continue
                            for tt in range(2):
                                qk_proj(g, 0, tt, j0)
                            micro = []
                            if g == 2 and s + 1 < NT // ST and s >= 2 and not S.off:
                                micro = []
                                for tt in range(2):
                                    S.deferred = []
                                    rope_tables(s + 1, tt)
                                    m2, S.deferred = S.deferred, None
                                    pos_ = [k_ for k_, it in enumerate(m2) if it[0] == "op" and it[1][0] == "act"]
                                    for k_ in reversed(pos_):
                                        it = m2.pop(k_)
                                        m2.insert(min(len(m2), k_ + 6), it)
                                    micro += m2
                            kring = [("KT", g, c2, i) for c2 in range(2) for i in range(RING[g] // 512)]
                            qkeys = [("QT", c2, tt, hh) for c2 in range(2) for tt in range(2) for hh in range(2)] + [("QTz", h_) for h_ in range(4)]
                            units = []
                            for qb in range(8):
                                if g == 0:
                                    kbs = [(j0 + 128 * qb - 128, 0), (j0 + 128 * qb, 1)]
                                elif g == 1:
                                    kbs = [(j0 + 512 * (qb // 4) - 512, 0), (j0 + 512 * (qb // 4), 1)]
                                else:
                                    kbs = [(j0 - 2048, 2), (j0 - 1024, 3), (j0, 4)]
                                for ki_, (jb, mid) in enumerate(kbs):
                                    units.append((qb, ki_, len(kbs), jb, mid))
                            st_ = {}

                            def emit_qk(i):
                                qb, ki_, nk, jb, mid = units[i]
                                sp_, spk = pbank(0, 2)
                                if g == 0:
                                    kcol = lambda c2: KT[0][:, c2, jb % RING[0]:jb % RING[0] + 128]
                                    kb_blk = (jb // 128) % 12
                                elif g == 1:
                                    rho = qb % 4
                                    kcol = lambda c2: KT[1][:, c2, jb % RING[1] + rho:jb % RING[1] + rho + 509:4]
                                    kb_blk = ((jb // 512) % 3) * 4 + rho
                                else:
                                    kcol = lambda c2: KT[2][:, c2, jb % RING[2] + 128 * qb:jb % RING[2] + 128 * qb + 128]
                                    kb_blk = ((jb // ST) % 3) * 8 + qb

                                def fs(e):
                                    e.matmul(sp_[:, :], lhsT=ident[:], rhs=masks[:, mid, :], start=True, stop=False, skip_group_check=True)
                                    last = None
                                    for h in range(4):
                                        last = e.matmul(sp_[:, h * 128:(h + 1) * 128], lhsT=kcol(h // 2),
                                                        rhs=cols(QT, slice(0, 128), h, g, 0, ST, qb), start=False, stop=(h == 3),
                                                        skip_group_check=True)
                                    return last

                                S.op("pe", fs, reads=kring + qkeys + ["masks", "ident"], writes=[spk])
                                pt = PT[i % 2]
                                ptk = ("PT", i % 2)
                                if jb < HALO:
                                    S.op("act", lambda e: e.activation(out=pt[:], in_=sp_[:, :], func=AF.Exp, scale=0.125, bias=flg[:, 1:2]),
                                         reads=[spk, "flg"], writes=[ptk])
                                else:
                                    S.op("act", lambda e: e.activation(out=pt[:], in_=sp_[:, :], func=AF.Exp, scale=0.125),
                                         reads=[spk], writes=[ptk])
                                st_[i] = (pt, ptk, kb_blk)

                            def emit_pv(i):
                                qb, ki_, nk, jb, mid = units[i]
                                pt, ptk, kb_blk = st_.pop(i)
                                if ki_ == 0:
                                    st_["nd"] = pbank(2, 4)
                                nd, ndk = st_["nd"]
                                first, lastkb = ki_ == 0, ki_ == nk - 1

                                def fpv(e):
                                    last = None
                                    for h in range(4):
                                        po_ = slice((h % 2) * 64, (h % 2) * 64 + 64)
                                        last = e.matmul(nd[po_, (h // 2) * 128:(h // 2) * 128 + 128],
                                                        lhsT=VT[g][:, kb_blk, h * 64:(h + 1) * 64], rhs=pt[:, h * 128:(h + 1) * 128],
                                                        start=(first and h < 2), stop=False, skip_group_check=True)
                                    for h in range(4):
                                        po_ = slice((h % 2) * 64, (h % 2) * 64 + 64)
                                        last = e.matmul(nd[po_, 256 + (h // 2) * 128:256 + (h // 2) * 128 + 128], lhsT=onesb[:, 0:64],
                                                        rhs=pt[:, h * 128:(h + 1) * 128],
                                                        start=False, stop=(lastkb and h == 3), skip_group_check=True)
                                    return last

                                S.op("pe", fpv, reads=[ptk, ("VT", g, kb_blk), "onesb"], writes=[ndk])
                                if lastkb:
                                    if g < 2:
                                        if g == 0:
                                            dst_ap = acc[:, :, 128 * qb:128 * qb + 128]
                                        else:
                                            r0_ = 512 * (qb // 4) + qb % 4
                                            dst_ap = acc[:, :, r0_:r0_ + 509:4]
                                        src_ap = nd[:, :].rearrange("p (k q) -> p k q", k=4)
                                        if g == 0:
                                            S.op("dve", lambda e: e.tensor_copy(out=dst_ap, in_=src_ap), reads=[ndk], writes=["acc"])
                                        else:
                                            S.op("dve", lambda e: e.tensor_tensor(out=dst_ap, in0=dst_ap, in1=src_ap, op=ALU.add),
                                                 reads=[ndk, "acc"], writes=["acc"])
                                    else:
                                        for k4 in range(4):
                                            dst_ap = cols(acc, slice(0, 128), k4, g, 0, ST, qb)
                                            src_ap = nd[:, k4 * 128:(k4 + 1) * 128].rearrange("p (e m) -> p e m", e=2)
                                            S.op("dve", lambda e: e.tensor_tensor(out=dst_ap, in0=dst_ap, in1=src_ap, op=ALU.add),
                                                 reads=[ndk, "acc"], writes=["acc"])

                            emit_qk(0)
                            for i in range(len(units)):
                                if i + 1 < len(units):
                                    emit_qk(i + 1)
                                emit_pv(i)
                                for _ in range(2):
                                    if micro:
                                        kind, args = micro.pop(0)
                                        (S.op if kind == "op" else S.dma)(*args)
                            while micro:
                                kind, args = micro.pop(0)
                                (S.op if kind == "op" else S.dma)(*args)
                            if mode == "mixT%d%d" % (s, g) and not S.off:
                                for k_ in range(4):
                                    S.dma("sp", y[256 + k_ * 128:256 + (k_ + 1) * 128, :], acc[:, k_, :], reads=["acc"], writes=["ydbg"])
                            chk("mixT%d%d" % (s, g))
                        if halo:
                            pre_next(len(PRE))
                            continue
                        for tt in range(2):
                            dTt = dT if tt == 0 else junk[:].rearrange("p (a b) -> p a b", a=2)
                            for c2 in range(2):
                                ps, pk = pbank()
                                S.op("pe", lambda e: e.matmul(ps[:, :], lhsT=wpool[:, c2, :], rhs=dTt[:, c2, :], start=True, stop=True),
                                     reads=[("dT", tt, c2, 0), ("dT", tt, c2, 1), "wpool"], writes=[pk])
                                S.op("dve", lambda e: e.tensor_scalar(out=ypl[:, c2, tt * 512:(tt + 1) * 512], in0=ps[:, :],
                                                                      scalar1=psc[:, c2:c2 + 1], scalar2=None, op0=ALU.mult),
                                     reads=[pk, "psc"], writes=[("ypl", c2, tt)])
                        for tt in range(2):
                            tsl = slice(tt * 512, (tt + 1) * 512)
                            S.op("act", lambda e: e.activation(out=rD, in_=acc[:, 2:4, tsl], func=AF.Ln), reads=["acc"], writes=["rD", "ang", "kf"])
                            S.op("act", lambda e: e.activation(out=rD, in_=rD, func=AF.Exp, scale=-1.0), reads=["rD"], writes=["rD", "ang", "kf"])
                            S.op("dve", lambda e: e.tensor_tensor(out=yat[:, :, tsl], in0=acc[:, 0:2, tsl], in1=rD, op=ALU.mult),
                                 reads=["acc", "rD"], writes=[("yat", tt)])
                        if mode == "mixY%d" % s and not S.off:
                            for sl_ in range(2):
                                S.dma("pool", y[sl_ * 128:(sl_ + 1) * 128, :], yat[:, sl_, :], reads=[("yat", 0), ("yat", 1)], writes=["ydbg"])
                            for k_ in range(4):
                                S.dma("sp", y[256 + k_ * 128:256 + (k_ + 1) * 128, :], acc[:, k_, :], reads=["acc"], writes=["ydbg"])
                        chk("mixY%d" % s)
                        wpb, wpbk = None, None
                        for dp in range(4):
                            wgp, wgpk = wload(IT_GP + dp)
                            wga, wgak = wload(IT_GA + dp)
                            wbr, wbrk = wload(IT_BR + dp)
                            for di in range(2):
                                dc_o = dp * 2 + di
                                fs_ = slice(di * 128, (di + 1) * 128)
                                for tt in range(2):
                                    tsl = slice(tt * 512, (tt + 1) * 512)
                                    for br, (wg, wgk, src_, skey) in enumerate(((wgp, wgpk, ypl, "ypl"), (wga, wgak, yat, "yat"))):
                                        pg, pgk = proj_fm([wg[:, dc, fs_] for dc in range(8)], wgk, UK[0], tt)
                                        pbr, pbrk = pbank()

                                        def fbr(e):
                                            last = None
                                            for c2 in range(2):
                                                last = e.matmul(pbr[:, :], lhsT=wbr[:, br * 2 + c2, fs_], rhs=src_[:, c2, tsl],
                                                                start=(c2 == 0), stop=(c2 == 1))
                                            return last

                                        rk = [("ypl", c2, tt) for c2 in range(2)] if br == 0 else [("yat", tt)]
                                        S.op("pe", fbr, reads=rk + [wbrk], writes=[pbrk])
                                        S.op("act", lambda e: e.activation(out=sg[br][:], in_=pg[:, :], func=AF.Sigmoid),
                                             reads=[pgk], writes=[("sg", br)])
                                        if br == 0:
                                            S.op("dve", lambda e: e.tensor_tensor(out=tmp[:], in0=sg[0][:], in1=pbr[:, :], op=ALU.mult),
                                                 reads=[("sg", 0), pbrk], writes=["tmp"])
                                        else:
                                            S.op("dve", lambda e: e.tensor_tensor(out=tmp2[:], in0=sg[1][:], in1=pbr[:, :], op=ALU.mult),
                                                 reads=[("sg", 1), pbrk], writes=["tmp2"])
                                    S.op("dve", lambda e: e.tensor_tensor(out=merged[:, dc_o, tsl], in0=tmp[:], in1=tmp2[:], op=ALU.add),
                                         reads=["tmp", "tmp2", ("yat", 0), ("yat", 1)], writes=["acc"])
                        chk("mixF%d" % s)
                        nxt = s + 1 < NT // ST
                        if nxt:
                            pre_stats(s + 1, 0)
                        for j in range(8):
                            r0 = j0 - HALO + j * 128
                            for hf in range(2):
                                cs = slice(hf * 512, (hf + 1) * 512)
                                rb, rbk = ((sg[0], ("sg", 0)), (sg[1], ("sg", 1)), (tmp, "tmp"), (tmp2, "tmp2"))[(2 * j + hf) % 4]
                                S.dma("sp", rb[:], h1s[j0 + j * 128:j0 + (j + 1) * 128, cs], writes=[rbk])
                                wo, wok = WOq[hf]
                                ps, pk = pbank()

                                def f(e):
                                    last = None
                                    for dc in range(8):
                                        last = e.matmul(ps[:, :], lhsT=merged[:, dc, j * 128:(j + 1) * 128], rhs=wo[:, dc, :],
                                                        start=(dc == 0), stop=(dc == 7))
                                    return last

                                S.op("pe", f, reads=["acc", wok], writes=[pk])
                                S.op("dve", lambda e: e.tensor_tensor(out=rb[:], in0=ps[:, :], in1=rb[:], op=ALU.add),
                                     reads=[pk, rbk], writes=[rbk])
                                S.dma("pool", h2s[r0:r0 + 128, cs], rb[:], reads=[rbk], writes=[("dst2", j, hf)])
                            if nxt:
                                pre_sub(j // 4, j % 4)
                                if j == 3:
                                    pre_stats(s + 1, 1)

                        chk("mixG%d" % s)

                was_off = S.off
                rot_save = dict(rot)
                S.off = True
                run_tiles()
                S.off = was_off
                dry[0] = False
                wsr[0] = 0
                rot.clear()
                rot.update(rot_save)
                run_tiles()
                S.barrier()

        WOq = []
        try:
          with ExitStack() as st2:
            for hf in range(2):
                t_ = sb(st2, "WO%d" % hf, [128, 8, 512], BF16)
                WOq.append((t_, ("WO", hf)))
            mix_phase()
        except _Stop:
            return nc

        if mode == "full":
            ffn_phase(h2s, y, OWN, w3a, w3b, 6, 2, True, pre_bf=(w3a_bf, w3b_bf))
    return nc


def _host_consts():
    ident = np.eye(128, dtype=np.float32)
    c = np.arange(128)[:, None]
    a = np.arange(128)[None, :]
    m0 = np.where(c >= a, 0.0, NEG)
    m1 = np.where(c <= a, 0.0, NEG)
    ek, mk = c // 64, c % 64
    eq, mq = a // 64, a % 64
    same = ek == eq
    m2 = np.where(same & (mk >= mq), 0.0, NEG)
    m3 = np.where(same, 0.0, NEG)
    m4 = np.where(same & (mk <= mq), 0.0, NEG)
    masks = np.stack([np.tile(m, (1, 4)) for m in (m0, m1, m2, m3, m4)], axis=1).astype(np.float32)
    masks = masks.reshape(128, 5 * 512)
    inv = (np.float32(10000.0) ** (-(np.arange(0, 64, 2, dtype=np.float32)) / np.float32(64))).astype(np.float32)
    i = np.arange(128) % 64
    f = inv[i % 32]
    sgn = np.where(i < 32, -1.0, 1.0).astype(np.float32)
    ropef = np.stack([f * sgn, f], axis=1).astype(np.float32)
    return ident, masks, ropef


_NC_CACHE = {}


def kernel(**inputs):
    f32 = lambda a: np.ascontiguousarray(np.asarray(a, dtype=np.float32))
    x = f32(inputs["x"])
    c = f32(inputs["c"])
    positions = np.asarray(inputs["positions"]).astype(np.int32)
    ident, masks, ropef = _host_consts()
    w_in = f32(inputs["w_in"])[0]
    perm = np.concatenate([np.arange(h * 64 + 32, h * 64 + 64).tolist() + np.arange(h * 64, h * 64 + 32).tolist()
                           for h in range(24)]).astype(np.int64)
    pidx = np.arange(128)
    partner = (pidx // 64) * 64 + (pidx % 64 + 32) % 64
    permT = np.zeros((128, 128), np.float32)
    permT[partner, pidx] = 1.0
    w_pool = f32(inputs["w_pool"])[0]
    w_pool_bd = np.zeros((128, 2, 128), np.float32)
    for c2 in range(2):
        w_pool_bd[0:64, c2, 0:64] = w_pool[2 * c2]
        w_pool_bd[64:128, c2, 64:128] = w_pool[2 * c2 + 1]
    gTm = np.concatenate([f32(inputs[k])[0].reshape(8, 128).T for k in ("g_norm_ffn1", "g_norm_mix", "g_norm_ffn2")], axis=1)
    b_ada = f32(inputs["b_ada"])
    shared = {
        "w_ada": f32(inputs["w_ada"])[0], "b_ada": b_ada, "b_adaT": np.ascontiguousarray(b_ada[0].reshape(72, 128).T),
        "gT": np.ascontiguousarray(gTm), "g_final": f32(inputs["g_final"]).reshape(1, D),
        "w_ffn1_in": f32(inputs["w_ffn1_in"])[0], "w_ffn1_out": f32(inputs["w_ffn1_out"])[0],
        "w_ffn2_in": f32(inputs["w_ffn2_in"])[0], "w_ffn2_out": f32(inputs["w_ffn2_out"])[0],
        "w_in": w_in, "permT": permT, "w_pool_bd": w_pool_bd,
        "pool_scaleT": np.ascontiguousarray(f32(inputs["pool_scale"])[0].reshape(2, 128).T),
        "w_pb": f32(inputs["w_pool_branch"])[0], "w_ab": f32(inputs["w_attn_branch"])[0], "w_o": f32(inputs["w_out"])[0],
        "ident": ident, "masks": masks, "ropef": ropef,
    }
    wins = np.array([2, 4, 8, 16], np.float32)
    in_maps = []
    for core in range(8):
        b, half = core // 2, core % 2
        s0 = half * OWN
        xc = np.zeros((NT, D), np.float32)
        pc = np.zeros((1, NT), np.int32)
        xc[HALO:] = x[b, s0:s0 + OWN]
        pc[0, HALO:] = positions[b, s0:s0 + OWN]
        if half == 1:
            xc[:HALO] = x[b, s0 - HALO:s0]
            pc[0, :HALO] = positions[b, s0 - HALO:s0]
        flags = np.zeros((128, 2), np.float32)
        flags[:, 0] = float(half)
        flags[:, 1] = 0.0 if half else NEG
        rc = np.zeros((128, 32), np.float32)
        for p in range(128):
            for c2 in range(2):
                w = wins[2 * c2 + p // 64]
                t = np.arange(16, dtype=np.float32)
                rc[p, c2 * 16:(c2 + 1) * 16] = 1.0 / (np.minimum(t + 1, w) if half == 0 else w)
        m = dict(shared)
        m.update({"x": xc, "pos": pc, "cT": np.ascontiguousarray(c[b].reshape(8, 128).T), "flags": flags, "rcnt": rc})
        in_maps.append(m)
    dbg = os.environ.get("MK_DBG", "")
    if dbg:
        return in_maps
    if "nc" not in _NC_CACHE:
        _NC_CACHE["nc"] = build_program()
    res = run_bass_kernel_spmd(_NC_CACHE["nc"], in_maps, core_ids=list(range(8)))
    out = np.zeros((4, 2 * OWN, D), np.float32)
    for core in range(8):
        b, half = core // 2, core % 2
        out[b, half * OWN:(half + 1) * OWN] = res.results[core]["y"]
    return out
```

```python
import os
from contextlib import ExitStack
import numpy as np
import concourse.bass as bass
import concourse.mybir as mybir
from concourse.bass_utils import run_bass_kernel_spmd

F32 = mybir.dt.float32
BF16 = mybir.dt.bfloat16
I32 = mybir.dt.int32
AF = mybir.ActivationFunctionType
ALU = mybir.AluOpType

D = 1024
DFF = 2816
NF = 22
OWN = 4096
HALO = 2048
NT = OWN + HALO
ST = 1024
RING = (1536, 1536, 3072)
DIL = (1, 4, 16)
NVB = (12, 12, 24)
EPS = 1e-6
NEG = -30000.0
TWO_PI_HI = 6.28125
TWO_PI_LO = 2.0 * np.pi - 6.28125


class _Stop(Exception):
    pass


class Sched:
    def __init__(self, nc, stack, n_dma=12):
        self.nc = nc
        self.eng = {"pe": nc.tensor, "act": nc.scalar, "dve": nc.vector, "pool": nc.gpsimd, "sp": nc.sync}
        self.semh = {}
        self.cnt = {}
        for e in ("pe", "act", "dve", "pool"):
            self.semh[e] = stack.enter_context(nc.semaphore("s_" + e))
            self.cnt[e] = 0
        self.dma_slots = {}
        for q in ("sp", "pool", "act"):
            sl = []
            for i in range(n_dma):
                k = "d_%s%d" % (q, i)
                self.semh[k] = stack.enter_context(nc.semaphore(k))
                sl.append([k, 0])
            self.dma_slots[q] = sl
        self.rr = {"sp": 0, "pool": 0, "act": 0}
        self.waited = {e: {} for e in self.eng}
        self.last_w = {}
        self.readers = {}
        self.off = False
        self.deferred = None
        self.pending = []

    def _wait(self, e, semk, val):
        w = self.waited[e]
        if w.get(semk, 0) >= val:
            return
        self.eng[e].wait_ge(self.semh[semk], val)
        w[semk] = val

    def _deps(self, e, reads, writes):
        need = {}

        def add(t):
            if t is not None and need.get(t[0], 0) < t[1]:
                need[t[0]] = t[1]

        for r in reads:
            add(self.last_w.get(r))
        for w in writes:
            add(self.last_w.get(w))
            for k, v in self.readers.get(w, {}).items():
                add((k, v))
        for k, v in need.items():
            if e == "pe" and k == "pe":
                continue
            self._wait(e, k, v)

    def _record(self, tok, reads, writes):
        for r in reads:
            d = self.readers.setdefault(r, {})
            if d.get(tok[0], 0) < tok[1]:
                d[tok[0]] = tok[1]
        for w in writes:
            self.last_w[w] = tok
            self.readers[w] = {}

    def flush(self, n=None):
        q, self.deferred = self.deferred, None
        k = 0
        while q and (n is None or k < n):
            kind, args = q.pop(0)
            (self.op if kind == "op" else self.dma)(*args)
            k += 1
        self.deferred = None
        self.pending = q
        return q

    def op(self, e, fn, reads=(), writes=()):
        if self.off:
            return None
        if self.deferred is not None:
            self.deferred.append(("op", (e, fn, tuple(reads), tuple(writes))))
            return None
        self._deps(e, reads, writes)
        inst = fn(self.eng[e])
        self.cnt[e] += 1
        inst.then_inc(self.semh[e], 1)
        tok = (e, self.cnt[e])
        self._record(tok, reads, writes)
        return tok

    def dma(self, q, out, in_, reads=(), writes=()):
        if self.off:
            return None
        if self.deferred is not None:
            self.deferred.append(("dma", (q, out, in_, tuple(reads), tuple(writes))))
            return None
        self._deps(q, reads, writes)
        sl = self.dma_slots[q]
        slot = sl[self.rr[q] % len(sl)]
        self.rr[q] += 1
        if slot[1] > 0:
            self._wait(q, slot[0], 16 * slot[1])
        self.eng[q].dma_start(out=out, in_=in_).then_inc(self.semh[slot[0]], 16)
        slot[1] += 1
        tok = (slot[0], 16 * slot[1])
        self._record(tok, reads, writes)
        return tok

    def drain(self, keys):
        if self.off:
            return
        need = {}
        for k in keys:
            t = self.last_w.get(k)
            if t is not None and need.get(t[0], 0) < t[1]:
                need[t[0]] = t[1]
            for sk, v in self.readers.get(k, {}).items():
                if need.get(sk, 0) < v:
                    need[sk] = v
        for e in self.eng:
            for sk, v in need.items():
                self._wait(e, sk, v)

    def barrier(self):
        if self.off:
            return
        tot = {e: self.cnt[e] for e in self.cnt}
        for q in self.dma_slots:
            for k, u in self.dma_slots[q]:
                tot[k] = 16 * u
        for e in self.eng:
            for k, v in tot.items():
                if v > 0:
                    self._wait(e, k, v)
        self.last_w = {}
        self.readers = {}


def build_program(dbg=False):
    nc = bass.Bass("TRN2", target_bir_lowering=False)

    def din(name, shape, dt=F32):
        return nc.dram_tensor(name, list(shape), dt, kind="ExternalInput").ap()

    x = din("x", [NT, D])
    pos = din("pos", [1, NT], I32)
    cT = din("cT", [128, 8])
    w_ada = din("w_ada", [D, 9 * D])
    b_ada = din("b_ada", [1, 9 * D])
    b_adaT = din("b_adaT", [128, 72])
    gT_in = din("gT", [128, 24])
    g_final = din("g_final", [1, D])
    w1a = din("w_ffn1_in", [D, 2 * DFF])
    w1b = din("w_ffn1_out", [DFF, D])
    w3a = din("w_ffn2_in", [D, 2 * DFF])
    w3b = din("w_ffn2_out", [DFF, D])
    w_in = din("w_in", [D, 4608])
    perm_in = din("permT", [128, 128])
    w_pool_bd = din("w_pool_bd", [128, 2, 128])
    pool_scaleT = din("pool_scaleT", [128, 2])
    w_pb = din("w_pb", [256, D])
    w_ab = din("w_ab", [256, D])
    w_o = din("w_o", [D, D])
    ident_in = din("ident", [128, 128])
    masks_in = din("masks", [128, 5 * 4 * 128])
    ropef = din("ropef", [128, 2])
    flags = din("flags", [128, 2])
    rcnt_in = din("rcnt", [128, 32])
    NITEM = 26
    wbf = nc.dram_tensor("wbf", [NITEM, 128, 2048], BF16, kind="Internal").ap()
    wada_bf = nc.dram_tensor("wada_bf", [6, 128, 8 * D], BF16, kind="Internal").ap()
    w3a_bf = nc.dram_tensor("w3a_bf", [22, 128, 2048], BF16, kind="Internal").ap()
    w3b_bf = nc.dram_tensor("w3b_bf", [11, 128, 2048], BF16, kind="Internal").ap()
    h1s = nc.dram_tensor("h1s", [NT, D], F32, kind="Internal").ap()
    h2s = nc.dram_tensor("h2s", [OWN, D], F32, kind="Internal").ap()
    y = nc.dram_tensor("y", [OWN, D], F32, kind="ExternalOutput").ap()

    with ExitStack() as top:
        S = Sched(nc, top)
        uid = [0]

        def sb(st, name, shape, dt):
            uid[0] += 1
            return st.enter_context(nc.sbuf_tensor("sb%d_%s" % (uid[0], name), list(shape), dt))
        TB = [top.enter_context(nc.psum_tensor("tb%d" % i, [128, 1024], BF16)) for i in range(2)]
        PB = [top.enter_context(nc.psum_tensor("pb%d" % i, [128, 512], F32)) for i in range(6)]
        rot = {"t": 0, "p": 0}

        def tbank():
            i = rot["t"] % 2
            rot["t"] += 1
            return TB[i], ("tb", i)

        def pbank(lo=0, hi=6):
            k = "p%d_%d" % (lo, hi)
            i = lo + rot.get(k, 0) % (hi - lo)
            rot[k] = rot.get(k, 0) + 1
            return PB[i], ("pb", i)

        ident = sb(top, "ident", [128, 128], BF16)
        onesb = sb(top, "onesb", [128, 128], BF16)
        condT = sb(top, "condT", [128, 8], BF16)
        cf = sb(top, "cf", [128, 8], F32)
        badT = sb(top, "badT", [128, 72], F32)
        gT = sb(top, "gT", [128, 24], F32)
        modT = sb(top, "modT", [128, 16], F32)
        modA = sb(top, "modA", [128, 8], F32)
        gtb = sb(top, "gtb", [128, D], F32)
        flg = sb(top, "flg", [128, 2], F32)

        S.dma("pool", ident[:], ident_in[:, :], writes=["ident"])
        S.dma("sp", cf[:], cT[:, :], writes=["cf"])
        S.dma("sp", badT[:], b_adaT[:, :], writes=["badT"])
        S.dma("sp", gT[:], gT_in[:, :], writes=["gT"])
        S.dma("sp", flg[:], flags[:, :], writes=["flg"])
        S.op("dve", lambda e: e.memset(onesb[:], 1.0), writes=["onesb"])
        S.op("act", lambda e: e.activation(out=condT[:], in_=cf[:], func=AF.Silu), reads=["cf"], writes=["condT"])

        def adaln(m0, gidx, gate_scale, after_loads=None):
            with ExitStack() as st:
                nslot = 2 if (m0 >= 3 and (mode == "full" or mode.startswith("mix"))) else 3
                wa_ = [sb(st, "wa%d" % i, [128, 8, D], BF16) for i in range(nslot)]
                wa = [wa_[i % nslot] for i in range(3)]
                bb = sb(st, "bb", [128, D], F32)
                condB = sb(st, "condB", [128, 8, 128], BF16)
                for dc in range(8):
                    S.op("dve", lambda e: e.tensor_scalar(out=condB[:, dc, :], in0=onesb[:], scalar1=condT[:, dc:dc + 1],
                                                          scalar2=None, op0=ALU.mult),
                         reads=["onesb", "condT"], writes=[("condB", dc)])
                S.dma("sp", bb[:], b_ada[0:1, (m0 + 2) * D:(m0 + 3) * D].partition_broadcast(128), writes=["bb"])
                def load_slice(mi):
                    pre = (m0 + mi >= 3) and mode in ("full",) or (m0 + mi >= 3 and mode.startswith("mix"))
                    if pre:
                        src = wada_bf[m0 + mi - 3].rearrange("p (dc f) -> p dc f", dc=8)
                    else:
                        src = w_ada[:, (m0 + mi) * D:(m0 + mi + 1) * D].rearrange("(dc p) f -> p dc f", p=128)
                    for hf in range(2):
                        S.dma("sp" if pre else "pool", wa[mi][:, :, hf * 512:(hf + 1) * 512], src[:, :, hf * 512:(hf + 1) * 512],
                              writes=[(("wa", mi % nslot), hf)])

                for mi in range(nslot):
                    load_slice(mi)
                if after_loads is not None and nslot == 3:
                    after_loads()
                for mi in range(3):
                    if mi >= 1 and mi - 1 + nslot < 3:
                        load_slice(mi - 1 + nslot)
                        if after_loads is not None:
                            after_loads()
                    m = m0 + mi
                    w = wa[mi]
                    wk = ("wa", mi % nslot)
                    if mi < 2:
                        ps, pk = pbank()

                        def f(e):
                            last = None
                            for fc in range(8):
                                for dc in range(8):
                                    last = e.matmul(ps[:, fc:fc + 1], lhsT=w[:, dc, fc * 128:(fc + 1) * 128],
                                                    rhs=condT[:, dc:dc + 1], start=(dc == 0), stop=(dc == 7))
                            return last

                        S.op("pe", f, reads=[(wk, 0), (wk, 1), "condT"], writes=[pk])
                        S.op("dve", lambda e: e.tensor_tensor(out=modT[:, mi * 8:(mi + 1) * 8], in0=ps[:, 0:8],
                                                              in1=badT[:, m * 8:(m + 1) * 8], op=ALU.add),
                             reads=[pk, "badT"], writes=[("modT", mi)])
                    else:
                        for hf in range(2):
                            ps, pk = pbank()

                            def f(e):
                                last = None
                                for dc in range(8):
                                    last = e.matmul(ps[:, :], lhsT=condB[:, dc, :], rhs=w[:, dc, hf * 512:(hf + 1) * 512],
                                                    start=(dc == 0), stop=(dc == 7))
                                return last

                            S.op("pe", f, reads=[(wk, hf)] + [("condB", dc) for dc in range(8)], writes=[pk])
                            S.op("dve", lambda e: e.tensor_tensor(out=gtb[:, hf * 512:(hf + 1) * 512], in0=ps[:, :],
                                                                  in1=bb[:, hf * 512:(hf + 1) * 512], op=ALU.add),
                                 reads=[pk, "bb"], writes=[("gtb", hf)])
                            if gate_scale != 1.0:
                                S.op("dve", lambda e: e.tensor_scalar(out=gtb[:, hf * 512:(hf + 1) * 512],
                                                                      in0=gtb[:, hf * 512:(hf + 1) * 512],
                                                                      scalar1=gate_scale, scalar2=None, op0=ALU.mult),
                                     reads=[("gtb", hf)], writes=[("gtb", hf)])
                S.op("dve", lambda e: e.scalar_tensor_tensor(out=modA[:], in0=modT[:, 8:16], scalar=1.0,
                                                             in1=gT[:, gidx * 8:(gidx + 1) * 8], op0=ALU.add, op1=ALU.mult),
                     reads=[("modT", 1), "gT"], writes=["modA"])
                S.drain(["bb"] + [("condB", dc) for dc in range(8)] + [(("wa", mi), hf) for mi in range(nslot) for hf in range(2)])

        def rsqrt_op(ss_ap, out_ap, rkeys, wkeys, n):
            S.op("pool", lambda e: e.tensor_scalar(out=ss_ap, in0=ss_ap, scalar1=1.0 / D, scalar2=EPS, op0=ALU.mult, op1=ALU.add),
                 reads=rkeys, writes=rkeys)
            S.op("pool", lambda e: e.tensor_tensor(out=out_ap, in0=ss_ap, in1=mhalf[:, 0:n], op=ALU.pow),
                 reads=rkeys + ["mhalf"], writes=wkeys)

        def norm_stats(hts, keys, ss, rstd, junk, n, junk_keys=()):
            for j in range(n):
                S.op("act", lambda e: e.activation(out=junk[:], in_=hts[j], func=AF.Square, accum_out=ss[:, j:j + 1]),
                     reads=[keys[j]], writes=["junk", ("ss", j)] + list(junk_keys))
            rsqrt_op(ss[:, 0:n], rstd[:, 0:n], [("ss", j) for j in range(n)], [("rstd", j) for j in range(n)], n)

        def norm_to_uT(ht, hkey, rstd_col, rkey, xh, xkey, uT, ukey_fn, col0):
            S.op("act", lambda e: e.activation(out=xh[:], in_=ht, func=AF.Copy, scale=rstd_col),
                 reads=[hkey, rkey], writes=[xkey])
            tb, tk = tbank()

            def f(e):
                last = None
                for dc in range(8):
                    last = e.transpose(tb[:, dc * 128:(dc + 1) * 128], xh[:, dc * 128:(dc + 1) * 128], ident[:])
                return last

            S.op("pe", f, reads=[xkey, "ident"], writes=[tk])
            for dc in range(8):
                S.op("dve", lambda e: e.tensor_scalar(out=uT[:, dc, col0:col0 + 128], in0=tb[:, dc * 128:(dc + 1) * 128],
                                                      scalar1=modA[:, dc:dc + 1], scalar2=modT[:, dc:dc + 1],
                                                      op0=ALU.mult, op1=ALU.add),
                     reads=[tk, "modA", ("modT", 0)], writes=[ukey_fn(dc)])

        mhalf = sb(top, "mhalf", [128, 4], F32)
        S.op("dve", lambda e: e.memset(mhalf[:], -0.5), writes=["mhalf"])

        def wview(i):
            return wbf[i].rearrange("p (dc f) -> p dc f", dc=8)

        IT_P, IT_Q, IT_K, IT_V, IT_GP, IT_GA, IT_BR = 0, 1, 4, 7, 10, 14, 18
        cvt_specs = []

        def std_item(i, c0):
            cvt_specs.append((wview(i), w_in[:, c0:c0 + 256].rearrange("(dc p) f -> p dc f", p=128)))

        std_item(IT_P, 0)
        for g_ in range(3):
            std_item(IT_Q + g_, 256 + g_ * 256)
            std_item(IT_K + g_, 1024 + g_ * 256)
            std_item(IT_V + g_, 1792 + g_ * 256)
        for dp_ in range(4):
            std_item(IT_GP + dp_, 2560 + dp_ * 256)
            std_item(IT_GA + dp_, 3584 + dp_ * 256)
            cvt_specs.append((wview(IT_BR + dp_)[:, 0:2, :], w_pb[:, dp_ * 256:(dp_ + 1) * 256].rearrange("(c p) f -> p c f", p=128)))
            cvt_specs.append((wview(IT_BR + dp_)[:, 2:4, :], w_ab[:, dp_ * 256:(dp_ + 1) * 256].rearrange("(c p) f -> p c f", p=128)))

        IT_WO = 22
        for q4_ in range(4):
            cvt_specs.append((wview(IT_WO + q4_), w_o[:, q4_ * 256:(q4_ + 1) * 256].rearrange("(dc p) f -> p dc f", p=128)))
        for m_ in range(3, 9):
            src_ = w_ada[:, m_ * D:(m_ + 1) * D].rearrange("(dc p) f -> p dc f", p=128)
            dst_ = wada_bf[m_ - 3].rearrange("p (dc f) -> p dc f", dc=8)
            for hf_ in range(2):
                cvt_specs.append((dst_[:, :, hf_ * 512:(hf_ + 1) * 512], src_[:, :, hf_ * 512:(hf_ + 1) * 512]))

        def ffn_phase(src, dst, ntok, wa_d, wb_d, m0, gidx, final, cvt=False, pre_bf=None):
            with ExitStack() as st:
                W1 = sb(st, "W1", [128, 8, 2 * DFF], BF16)
                srcw = wa_d.rearrange("(dc p) f -> p dc f", p=128)
                srcw2 = wb_d.rearrange("(f p) d -> p f d", p=128)
                def wloads():
                    q_ = "sp" if pre_bf is not None else "pool"
                    sw1 = pre_bf[0] if pre_bf is not None else srcw
                    sw2 = pre_bf[1] if pre_bf is not None else srcw2
                    for fg in range(11):
                        for ab in range(2):
                            c0 = ab * DFF + fg * 256
                            if pre_bf is not None:
                                src_ = sw1[ab * 11 + fg].rearrange("p (dc f) -> p dc f", dc=8)
                            else:
                                src_ = sw1[:, :, c0:c0 + 256]
                            S.dma(q_, W1[:, :, c0:c0 + 256], src_, writes=[("W1", ab, fg)])

                def wloads2():
                    q_ = "sp" if pre_bf is not None else "pool"
                    sw2 = pre_bf[1] if pre_bf is not None else srcw2
                    for fg in range(11):
                        if pre_bf is not None:
                            src_ = sw2[fg].rearrange("p (a d) -> p a d", a=2)
                        else:
                            src_ = sw2[:, 2 * fg:2 * fg + 2, :]
                        S.dma(q_, W2[:, 2 * fg:2 * fg + 2, :], src_, writes=[("W2", fg)])

                early = True
                if early:
                    hb = [sb(st, "hb%d" % i, [128, D], F32) for i in range(4)]
                    ss = sb(st, "ss", [128, 4], F32)
                    rstd = sb(st, "rstd", [128, 4], F32)
                    junk = sb(st, "junk", [128, D], BF16)
                    for j in range(4):
                        S.dma("sp", hb[j][:], src[j * 128:(j + 1) * 128, :], writes=[("hb", j)])
                    norm_stats([hb[j][:] for j in range(4)], [("hb", j) for j in range(4)], ss, rstd, junk, 4)
                adaln(m0, gidx, 0.5, wloads)
                W2 = sb(st, "W2", [128, NF, D], BF16)
                wloads2()
                if not early:
                    hb = [sb(st, "hb%d" % i, [128, D], F32) for i in range(4)]
                NHR = 3 if final else 2
                hr = [sb(st, "hr%d" % i, [128, D], F32) for i in range(NHR)]
                xh = [sb(st, "xh0", [128, D], BF16)] * 2
                uT = sb(st, "uT", [128, 8, 512], BF16)
                gTt = sb(st, "gTt", [128, NF, 512], BF16)
                sa = [sb(st, "sa%d" % i, [128, 512], F32) for i in range(2)]
                if not early:
                    ss = sb(st, "ss", [128, 4], F32)
                    rstd = sb(st, "rstd", [128, 4], F32)
                    junk = sb(st, "junk", [128, D], BF16)
                if final:
                    gfb = sb(st, "gfb", [128, D], F32)
                    S.dma("sp", gfb[:], g_final[0:1, :].partition_broadcast(128), writes=["gfb"])
                    ss2 = sb(st, "ss2", [128, 1], F32)
                    rs2 = sb(st, "rs2", [128, 1], F32)
                ntile = ntok // 512

                def pre_stats(t):
                    for j in range(4):
                        S.dma("sp", hb[j][:], src[t * 512 + j * 128:t * 512 + (j + 1) * 128, :], writes=[("hb", j)])
                    norm_stats([hb[j][:] for j in range(4)], [("hb", j) for j in range(4)], ss, rstd, junk, 4)

                def pre_sub(j):
                    norm_to_uT(hb[j][:], ("hb", j), rstd[:, j:j + 1], ("rstd", j), xh[0], ("xh", 0),
                               uT, lambda dc: ("uT", dc, j), j * 128)

                if not early:
                    pre_stats(0)
                for j in range(4):
                    pre_sub(j)
                for t in range(ntile):
                    t0 = t * 512
                    if cvt and t >= 1:
                        for _ in range(4):
                            if cvt_specs:
                                d_, s_src = cvt_specs.pop(0)
                                S.dma("pool", d_, s_src, writes=["wbf"])
                    ukeys = [("uT", dc, j) for dc in range(8) for j in range(4)]
                    if STOP[0] == "a":
                        S.barrier()
                        return
                    for f_ in range(NF):
                        pa, pak = pbank(0, 4)
                        pbk_, pbk = pbank(0, 4)
                        fg, fo = f_ // 2, (f_ % 2) * 128

                        def fa(e):
                            last = None
                            for dc in range(8):
                                last = e.matmul(pa[:, :], lhsT=W1[:, dc, f_ * 128:(f_ + 1) * 128], rhs=uT[:, dc, :],
                                                start=(dc == 0), stop=(dc == 7))
                            return last

                        def fb(e):
                            last = None
                            for dc in range(8):
                                last = e.matmul(pbk_[:, :], lhsT=W1[:, dc, DFF + f_ * 128:DFF + (f_ + 1) * 128],
                                                rhs=uT[:, dc, :], start=(dc == 0), stop=(dc == 7))
                            return last

                        S.op("pe", fa, reads=ukeys + [("W1", 0, fg)], writes=[pak])
                        S.op("pe", fb, reads=ukeys + [("W1", 1, fg)], writes=[pbk])
                        s_ = sa[f_ % 2]
                        S.op("act", lambda e: e.activation(out=s_[:], in_=pa[:, :], func=AF.Silu),
                             reads=[pak], writes=[("sa", f_ % 2)])
                        S.op("dve", lambda e: e.tensor_tensor(out=gTt[:, f_, :], in0=s_[:], in1=pbk_[:, :], op=ALU.mult),
                             reads=[("sa", f_ % 2), pbk], writes=[("gTt", f_)])
                        if f_ == 8 and t + 1 < ntile and STOP[0] == "":
                            pre_stats(t + 1)
                    if STOP[0] == "b":
                        S.barrier()
                        return
                    if t == 0:
                        for f_ in range(NF):
                            S.op("pool", lambda e: e.tensor_tensor(out=W2[:, f_, :], in0=W2[:, f_, :], in1=gtb[:], op=ALU.mult),
                                 reads=[("W2", f_ // 2), ("gtb", 0), ("gtb", 1)], writes=[("W2", f_ // 2)])
                    for j in range(4):
                        hk = (4 * t + j) % NHR
                        r_ = hr[hk]
                        S.dma("sp", r_[:], src[t0 + j * 128:t0 + (j + 1) * 128, :], writes=[("hr", hk)])
                        for hf in range(2):
                            po, pok = pbank(4, 6)

                            def fo_(e):
                                last = None
                                for f_ in range(NF):
                                    last = e.matmul(po[:, :], lhsT=gTt[:, f_, j * 128:(j + 1) * 128],
                                                    rhs=W2[:, f_, hf * 512:(hf + 1) * 512], start=(f_ == 0), stop=(f_ == NF - 1))
                                return last

                            S.op("pe", fo_, reads=[("gTt", f_) for f_ in range(NF)] + [("W2", fg) for fg in range(11)],
                                 writes=[pok])
                            S.op("dve", lambda e: e.tensor_tensor(out=r_[:, hf * 512:(hf + 1) * 512], in0=po[:, :],
                                                                  in1=r_[:, hf * 512:(hf + 1) * 512], op=ALU.add),
                                 reads=[pok, ("hr", hk)], writes=[("hr", hk)])
                        if final:
                            S.op("dve", lambda e: e.scalar_tensor_tensor(out=junk[:], in0=r_[:], scalar=1.0, in1=r_[:],
                                                                         op0=ALU.mult, op1=ALU.mult, accum_out=ss2[:, 0:1]),
                                 reads=[("hr", hk)], writes=["junk", "ss2"])
                            rsqrt_op(ss2[:], rs2[:], ["ss2"], ["rs2"], 1)
                            S.op("dve", lambda e: e.scalar_tensor_tensor(out=r_[:], in0=r_[:], scalar=rs2[:, 0:1], in1=gfb[:],
                                                                         op0=ALU.mult, op1=ALU.mult),
                                 reads=[("hr", hk), "rs2", "gfb"], writes=[("hr", hk)])
                        S.dma("sp", dst[t0 + j * 128:t0 + (j + 1) * 128, :], r_[:], reads=[("hr", hk)], writes=[("dst", t, j)])
                        if t + 1 < ntile:
                            pre_sub(j)
                S.barrier()

        mode = dbg or "full"
        STOP = [""]

        def chk(tag):
            if mode == tag and not S.off:
                S.barrier()
                S.off = True
        if mode in ("ffn1a", "ffn1b"):
            STOP[0] = mode[-1]
            ffn_phase(x, y, 512, w1a, w1b, 0, 0, False)
            return nc
        if mode == "ada":
            adaln(0, 0, 0.5)
            S.dma("sp", y[0:128, :], gtb[:], reads=[("gtb", 0), ("gtb", 1)], writes=["yy"])
            S.barrier()
            return nc
        if mode == "ffn1":
            ffn_phase(x, y, 512, w1a, w1b, 0, 0, False)
            return nc
        if mode == "ffn3":
            ffn_phase(x, y, 512, w3a, w3b, 6, 2, True)
            return nc
        if mode.startswith("mix"):
            h1s = x
            h2s = y
            for d_, s_src in cvt_specs:
                S.dma("pool", d_, s_src, writes=["wbf"])
            S.barrier()
        if mode == "full":
            ffn_phase(x, h1s, NT, w1a, w1b, 0, 0, False, cvt=True)

        def mix_phase():
            with ExitStack() as st:
                hb = [sb(st, "mhb%d" % i, [128, D], F32) for i in range(4)]
                xh = [sb(st, "mxh0", [128, D], BF16)] * 2
                ss = sb(st, "mss", [128, 4], F32)
                rstd = sb(st, "mrstd", [128, 4], F32)
                junk = sb(st, "mjunk", [128, D], BF16)
                for j in range(4):
                    S.dma("sp", hb[j][:], h1s[j * 128:(j + 1) * 128, :], writes=[("hb", j)])
                norm_stats([hb[j][:] for j in range(4)], [("hb", j) for j in range(4)], ss, rstd, junk, 4)
                adaln(3, 1, 1.0)
                masks = sb(st, "masks", [128, 5, 512], BF16)
                for mi_ in range(5):
                    S.dma("pool", masks[:, mi_, :], masks_in[:, mi_ * 512:(mi_ + 1) * 512], writes=["masks"])
                rf = sb(st, "rf", [128, 2], F32)
                S.dma("sp", rf[:], ropef[:, :], writes=["rf"])
                rcn = sb(st, "rcn", [128, 32], F32)
                S.dma("sp", rcn[:], rcnt_in[:, :], writes=["rcn"])
                psc = sb(st, "psc", [128, 2], F32)
                S.dma("sp", psc[:], pool_scaleT[:, :], writes=["psc"])
                wpool = sb(st, "wpool", [128, 2, 128], BF16)
                S.dma("pool", wpool[:], w_pool_bd[:, :, :], writes=["wpool"])
                uT = sb(st, "uT2", [128, 8, ST], BF16)
                QT = sb(st, "QT", [128, 4, ST], BF16)
                for h_ in range(4):
                    z0 = 64 if h_ % 2 == 0 else 0
                    S.op("pool", lambda e: e.memset(QT[z0:z0 + 64, h_, :], 0.0), writes=[("QTz", h_)])
                KT = [sb(st, "KT%d" % g, [128, 2, RING[g]], BF16) for g in range(3)]
                VT = [sb(st, "VT%d" % g, [128, NVB[g], 256], BF16) for g in range(3)]
                acc = sb(st, "acc", [128, 4, ST], F32)
                merged = acc[:].bitcast(BF16).rearrange("p a (b c) -> p (a b) c", b=2)
                yat = sb(st, "yat", [128, 2, ST], BF16)
                ypl = sb(st, "ypl", [128, 2, ST], BF16)
                PT = [sb(st, "PT%d" % i, [128, 512], BF16) for i in range(2)]
                NWS = 6
                WS = [sb(st, "WS%d" % i, [128, 8, 256], BF16) for i in range(NWS)]
                wsr = [0]
                qbuf = [sb(st, "qbuf%d" % i, [128, 512], BF16) for i in range(2)]
                permT = sb(st, "permT", [128, 128], BF16)
                S.dma("pool", permT[:], perm_in[:, :], writes=["permT"])
                Ctab = sb(st, "Ctab", [128, ST], F32)
                Stab = sb(st, "Stab", [128, ST], F32)
                posi = sb(st, "posi", [128, 512], I32)
                scr = sb(st, "scr", [128, 1024], F32)
                ang0 = ang = scr[:, 0:512]
                kf0 = kf = scr[:, 512:1024]
                rD = scr[:].rearrange("p (a b) -> p a b", a=2)
                pT = sb(st, "pT", [128, 2, 16 + 512], F32)
                w2_ = sb(st, "pw2", [128, 2, 16 + 512], F32)
                w4_ = sb(st, "pw4", [128, 2, 16 + 512], F32)
                dT = sb(st, "dT", [128, 2, 512], BF16)
                tmp = sb(st, "mtmp", [128, 512], F32)
                tmp2 = sb(st, "mtmp2", [128, 512], F32)
                sg = [sb(st, "sg%d" % i, [128, 512], F32) for i in range(2)]
                S.op("pool", lambda e: e.memset(pT[:, :, 0:16], 0.0), writes=["pThist"])

                ukeys = [("uT", dc, jj) for dc in range(8) for jj in range(8)]
                ukeys1 = [("uT1", dc, jj) for dc in range(8) for jj in range(8)] + ["acc"]
                UT = [uT]
                UK = [ukeys]
                PRE = []

                def pre_next(n=1):
                    for _ in range(n):
                        if PRE:
                            PRE.pop(0)()
                wplan, wiss, dry = [], [0], [True]
                LA = 3

                def wload(item):
                    idx = wsr[0]
                    wsr[0] += 1
                    if dry[0]:
                        wplan.append(item)
                    else:
                        while wiss[0] < min(len(wplan), idx + LA + 1):
                            k = wiss[0]
                            if IT_BR <= wplan[k] < IT_BR + 4:
                                S.dma("pool", WS[k % NWS][:, 0:4, :], wview(wplan[k])[:, 0:4, :], writes=[("WS", k % NWS)])
                            else:
                                S.dma("pool", WS[k % NWS][:], wview(wplan[k]), writes=[("WS", k % NWS)])
                            wiss[0] += 1
                    return WS[idx % NWS], ("WS", idx % NWS)

                def wsrc(mat, c0):
                    return mat[:, c0:c0 + 256].rearrange("(dc p) f -> p dc f", p=128)

                def cols(buf, ps_, c2, g, jbase, ring, qb):
                    if g == 0:
                        r0 = (jbase + 128 * qb) % ring
                        return buf[ps_, c2, r0:r0 + 128]
                    if g == 1:
                        u, rho = qb // 4, qb % 4
                        r0 = (jbase + 512 * u) % ring + rho
                        return buf[ps_, c2, r0:r0 + 4 * 127 + 1:4]
                    r0 = jbase % ring
                    if buf is QT:
                        return buf[ps_, c2, 128 * qb:128 * qb + 128]
                    return buf[ps_, c2, r0:r0 + ST].rearrange("p (m e) -> p e m", e=16)[:, 2 * qb:2 * qb + 2, :]

                def vblk(g, s, qb):
                    if g == 0:
                        return (8 * s + qb) % 12
                    if g == 1:
                        u, rho = qb // 4, qb % 4
                        return ((2 * s + u) % 3) * 4 + rho
                    return (s % 3) * 8 + qb

                def proj_fm(wt, wk, ukeys, tt):
                    ps, pk = pbank()

                    def f(e):
                        last = None
                        for dc in range(8):
                            last = e.matmul(ps[:, :], lhsT=wt[dc], rhs=UT[0][:, dc, tt * 512:(tt + 1) * 512],
                                            start=(dc == 0), stop=(dc == 7))
                        return last

                    S.op("pe", f, reads=ukeys + [wk], writes=[pk])
                    return ps, pk

                def qk_proj(g, isk, tt, j0):
                        wn, wnk = wload((IT_K if isk else IT_Q) + g)
                        pas = []
                        for c2 in range(2):
                            pa, pak = proj_fm([wn[:, dc, c2 * 128:(c2 + 1) * 128] for dc in range(8)], wnk, UK[0], tt)
                            qb_ = qbuf[c2]
                            S.op("act", lambda e: e.activation(out=qb_[:], in_=pa[:, :], func=AF.Copy), reads=[pak], writes=[("qb", c2)])
                            pas.append((pa, pak))
                        for c2 in range(2):
                            pa, pak = pas[c2]
                            qb_ = qbuf[c2]
                            pb_, pbk = pbank()
                            S.op("pe", lambda e: e.matmul(pb_[:, :], lhsT=permT[:], rhs=qb_[:], start=True, stop=True),
                                 reads=[("qb", c2), "permT"], writes=[pbk])
                            tA, tB, k1, k2 = (tmp, tmp2, "tmp", "tmp2") if c2 == 0 else (sg[0], sg[1], ("sg", 0), ("sg", 1))
                            S.op("dve", lambda e: e.tensor_tensor(out=tA[:], in0=pa[:, :], in1=Ctab[:, tt * 512:(tt + 1) * 512], op=ALU.mult),
                                 reads=[pak, ("tab", 1, tt), ("qb", c2)], writes=[k1])
                            S.op("dve", lambda e: e.tensor_tensor(out=tB[:], in0=pb_[:, :], in1=Stab[:, tt * 512:(tt + 1) * 512], op=ALU.mult),
                                 reads=[pbk, ("tab", 0, tt)], writes=[k2])
                            r0 = (j0 + tt * 512) % RING[g]
                            if isk:
                                dkey = ("KT", g, c2, r0 // 512)
                                if g == 2:
                                    b0 = (j0 % RING[2])
                                    dst_ap = KT[2][:, c2, b0:b0 + ST].rearrange("p (e m) -> p m e", e=16)[:, 32 * tt:32 * tt + 32, :]
                                else:
                                    dst_ap = KT[g][:, c2, r0:r0 + 512]
                            else:
                                dkey = ("QT", c2, tt)
                            for hh in ((None,) if isk else (0, 1)):
                                pr = slice(0, 128) if isk else slice(64 * hh, 64 * hh + 64)
                                if not isk:
                                    if g == 2:
                                        dst_ap = QT[pr, 2 * c2 + hh, :].rearrange("p (e m) -> p m e", e=16)[:, 32 * tt:32 * tt + 32, :]
                                    else:
                                        dst_ap = QT[pr, 2 * c2 + hh, tt * 512:(tt + 1) * 512]
                                if g == 2:
                                    i0 = tA[pr, :].rearrange("p (m e) -> p m e", e=16)
                                    i1 = tB[pr, :].rearrange("p (m e) -> p m e", e=16)
                                else:
                                    i0, i1 = tA[pr, :], tB[pr, :]
                                S.op("dve", lambda e: e.tensor_tensor(out=dst_ap, in0=i0, in1=i1, op=ALU.add),
                                     reads=[k1, k2, ("QTz", 0)], writes=[dkey if isk else ("QT", c2, tt, hh)])

                def rope_tables(s_, tt):
                    c0 = s_ * ST + tt * 512
                    S.dma("sp", posi[:], pos[0:1, c0:c0 + 512].partition_broadcast(128), writes=["posi"])
                    bufs = [(ang0, kf0, "ang", "kf"), (tmp[:], tmp2[:], "tmp", "tmp2")]
                    for which in (0, 1):
                        ang, kf, ka, kk = bufs[which]
                        S.op("dve", lambda e, ang=ang: e.tensor_copy(out=ang, in_=posi[:]), reads=["posi"], writes=[ka])
                        S.op("dve", lambda e, which=which, ang=ang: e.tensor_scalar(out=ang, in0=ang, scalar1=rf[:, which:which + 1],
                                                              scalar2=(np.pi / 2 if which else 0.0), op0=ALU.mult, op1=ALU.add),
                             reads=[ka, "rf"], writes=[ka])
                    for which, tab in ((0, Stab), (1, Ctab)):
                        ang, kf, ka, kk = bufs[which]
                        S.op("dve", lambda e, ang=ang, kf=kf: e.tensor_scalar(out=kf, in0=ang, scalar1=float(1.0 / (2 * np.pi)),
                                                              scalar2=None, op0=ALU.mult), reads=[ka], writes=[kk])
                        S.op("dve", lambda e, kf=kf: e.tensor_copy(out=posi[:], in_=kf), reads=[kk], writes=["posi"])
                        S.op("dve", lambda e, kf=kf: e.tensor_copy(out=kf, in_=posi[:]), reads=["posi"], writes=[kk])
                        S.op("dve", lambda e, ang=ang, kf=kf: e.scalar_tensor_tensor(out=ang, in0=kf, scalar=-TWO_PI_HI, in1=ang,
                                                                     op0=ALU.mult, op1=ALU.add),
                             reads=[kk, ka], writes=[ka])
                        S.op("dve", lambda e, ang=ang, kf=kf: e.scalar_tensor_tensor(out=ang, in0=kf, scalar=-float(TWO_PI_LO), in1=ang,
                                                                     op0=ALU.mult, op1=ALU.add),
                             reads=[kk, ka], writes=[ka])
                        S.op("dve", lambda e, ang=ang: e.tensor_scalar(out=ang, in0=ang, scalar1=-3.14159, scalar2=3.14159,
                                                              op0=ALU.max, op1=ALU.min), reads=[ka], writes=[ka])
                        S.op("act", lambda e, tab=tab, tt=tt, ang=ang: e.activation(out=tab[:, tt * 512:(tt + 1) * 512], in_=ang, func=AF.Sin),
                             reads=[ka], writes=[("tab", which, tt)])

                DT1 = [("dT", 1, a_, b_) for a_ in range(2) for b_ in range(2)]

                def pre_stats(s_, q4):
                    for j in range(4):
                        r0 = s_ * ST + q4 * 512 + j * 128
                        S.dma("sp", hb[j][:], h1s[r0:r0 + 128, :], writes=[("hb", j)])
                    norm_stats([hb[j][:] for j in range(4)], [("hb", j) for j in range(4)], ss, rstd, junk, 4, junk_keys=DT1)

                def pre_sub(q4, j, alt=False):
                    norm_to_uT(hb[j][:], ("hb", j), rstd[:, j:j + 1], ("rstd", j), xh[0], ("xh", 0),
                               merged if alt else uT, lambda dc: (("uT1" if alt else "uT"), dc, q4 * 4 + j), q4 * 512 + j * 128)

                def pre_pieces(s_, alt):
                    out = []
                    for q4 in range(2):
                        out.append(lambda q4=q4: pre_stats(s_, q4))
                        for j in range(4):
                            out.append(lambda q4=q4, j=j: pre_sub(q4, j, alt))
                    return out

                cvt3 = []
                if mode == "full":
                    s3a = w3a.rearrange("(dc p) f -> p dc f", p=128)
                    s3b = w3b.rearrange("(f p) d -> p f d", p=128)
                    for fg in range(11):
                        for ab in range(2):
                            c0 = ab * DFF + fg * 256
                            cvt3.append((w3a_bf[ab * 11 + fg].rearrange("p (dc f) -> p dc f", dc=8), s3a[:, :, c0:c0 + 256]))
                    for fg in range(11):
                        cvt3.append((w3b_bf[fg].rearrange("p (a d) -> p a d", a=2), s3b[:, 2 * fg:2 * fg + 2, :]))

                def run_tiles():
                    for s in range(NT // ST):
                        halo = s < 2
                        j0 = s * ST
                        groups = [2] if s == 0 else [0, 1, 2]

                        if s == 0:
                            for pc in pre_pieces(0, False)[1:]:
                                pc()
                        UT[0], UK[0] = (merged, ukeys1) if s == 1 else (uT, ukeys)
                        if halo:
                            PRE[:] = pre_pieces(s + 1, s + 1 == 1)
                        if s == 0 and not S.off:
                            for hf_ in range(2):
                                for q2 in range(2):
                                    S.dma("sp", WOq[hf_][0][:, :, q2 * 256:(q2 + 1) * 256], wview(IT_WO + hf_ * 2 + q2), writes=[("WO", hf_)])
                            for hf_ in range(2):
                                for dc_ in range(8):
                                    S.op("dve", lambda e: e.tensor_tensor(out=WOq[hf_][0][:, dc_, :], in0=WOq[hf_][0][:, dc_, :],
                                                                          in1=gtb[:, hf_ * 512:(hf_ + 1) * 512], op=ALU.mult),
                                         reads=[("WO", hf_), ("gtb", hf_)], writes=[("WO", hf_)])
                        chk("mixA%d" % s)
                        for tt in range(2):
                            c0 = j0 + tt * 512
                            if s < 3:
                                rope_tables(s, tt)
                            def p_and_adds():
                                if s >= 1:
                                    wt, wk = wload(IT_P)
                                    for c2 in range(2):
                                        ps, pk = proj_fm([wt[:, dc, c2 * 128:(c2 + 1) * 128] for dc in range(8)], wk, UK[0], tt)
                                        if halo:
                                            S.op("dve", lambda e: e.tensor_scalar(out=pT[:, c2, 16:528], in0=ps[:, :], scalar1=flg[:, 0:1],
                                                                                  scalar2=None, op0=ALU.mult),
                                                 reads=[pk, "flg"], writes=[("pTc", c2)])
                                        else:
                                            S.op("act", lambda e: e.activation(out=pT[:, c2, 16:528], in_=ps[:, :], func=AF.Copy),
                                                 reads=[pk], writes=[("pTc", c2)])
                                if s >= 1:
                                    if not halo:
                                        dTt = dT if tt == 0 else junk[:].rearrange("p (a b) -> p a b", a=2)
                                        pk_all = [("pTc", 0), ("pTc", 1), "pThist"]
                                        S.op("pool", lambda e: e.tensor_tensor(out=w2_[:, :, 1:528], in0=pT[:, :, 1:528], in1=pT[:, :, 0:527], op=ALU.add),
                                             reads=pk_all, writes=["w2", "w8"])
                                        S.op("pool", lambda e: e.tensor_tensor(out=w4_[:, :, 3:528], in0=w2_[:, :, 3:528], in1=w2_[:, :, 1:526], op=ALU.add),
                                             reads=["w2"], writes=["w4", "w16"])
                                        S.op("pool", lambda e: e.tensor_tensor(out=w2_[:, 1, 7:528], in0=w4_[:, 1, 7:528], in1=w4_[:, 1, 3:524], op=ALU.add),
                                             reads=["w4"], writes=["w8"])
                                        S.op("pool", lambda e: e.tensor_tensor(out=w4_[:, 1, 15:528], in0=w2_[:, 1, 15:528], in1=w2_[:, 1, 7:520], op=ALU.add),
                                             reads=["w8"], writes=["w16"])
                            def k_block():
                                for g in groups:
                                    if s == 1 and tt == 0 and g < 2:
                                        continue
                                    qk_proj(g, 1, tt, j0)
                                if halo:
                                    pre_next(2)
                            if tt == 0:
                                p_and_adds()
                                k_block()
                            else:
                                k_block()
                                p_and_adds()
                            if s >= 1:
                                if not halo:
                                    dTt = dT if tt == 0 else junk[:].rearrange("p (a b) -> p a b", a=2)
                                    pk_all = [("pTc", 0), ("pTc", 1), "pThist"]
                                    S.op("dve", lambda e: e.scalar_tensor_tensor(out=dTt[0:64, 0, :], in0=w2_[0:64, 0, 16:528], scalar=0.5,
                                                                                 in1=pT[0:64, 0, 16:528], op0=ALU.mult, op1=ALU.subtract),
                                         reads=["w2"] + pk_all, writes=[("dT", tt, 0, 0)])
                                    S.op("dve", lambda e: e.scalar_tensor_tensor(out=dTt[64:128, 0, :], in0=w4_[64:128, 0, 16:528], scalar=0.25,
                                                                                 in1=pT[64:128, 0, 16:528], op0=ALU.mult, op1=ALU.subtract),
                                         reads=["w4"] + pk_all, writes=[("dT", tt, 0, 1)])
                                    S.op("dve", lambda e: e.scalar_tensor_tensor(out=dTt[0:64, 1, :], in0=w2_[0:64, 1, 16:528], scalar=0.125,
                                                                                 in1=pT[0:64, 1, 16:528], op0=ALU.mult, op1=ALU.subtract),
                                         reads=["w8", "w16"] + pk_all, writes=[("dT", tt, 1, 0)])
                                    S.op("dve", lambda e: e.scalar_tensor_tensor(out=dTt[64:128, 1, :], in0=w4_[64:128, 1, 16:528], scalar=0.0625,
                                                                                 in1=pT[64:128, 1, 16:528], op0=ALU.mult, op1=ALU.subtract),
                                         reads=["w16"] + pk_all, writes=[("dT", tt, 1, 1)])
                                    if s == 2 and tt == 0:
                                        for c2, hfp, wb_, wkey in ((0, 0, w2_, "w2"), (0, 1, w4_, "w4"), (1, 0, w2_, "w8"), (1, 1, w4_, "w16")):
                                            pr = slice(hfp * 64, hfp * 64 + 64)
                                            S.op("dve", lambda e: e.tensor_tensor(out=tmp[pr, 0:16], in0=wb_[pr, c2, 16:32],
                                                                                  in1=rcn[pr, c2 * 16:(c2 + 1) * 16], op=ALU.mult),
                                                 reads=["w2", "w4", "w8", "w16", "rcn"], writes=["tmp"])
                                            S.op("dve", lambda e: e.tensor_tensor(out=dTt[pr, c2, 0:16], in0=tmp[pr, 0:16],
                                                                                  in1=pT[pr, c2, 16:32], op=ALU.subtract),
                                                 reads=["tmp"] + pk_all, writes=[("dT", tt, c2, hfp)])
                            if s >= 1:
                                S.op("pool", lambda e: e.tensor_copy(out=pT[:, :, 0:16], in_=pT[:, :, 512:528]),
                                     reads=[("pTc", 0), ("pTc", 1)] + [("dT", tt, a_, b_) for a_ in range(2) for b_ in range(2)] + ["w2"],
                                     writes=["pThist"])
                        chk("mixB%d" % s)
                        for g in groups:
                            wv, wvk = wload(IT_V + g)
                            for qb in range(8):
                                if s == 1 and ((g == 0 and qb < 7) or (g == 1 and qb < 4)):
                                    pre_next(1)
                                    continue
                                ps, pk = pbank(4, 6)
                                blk = vblk(g, s, qb)

                                def f(e):
                                    last = None
                                    for dc in range(8):
                                        if g == 2:
                                            for e2 in range(2):
                                                c_ = 2 * qb + e2
                                                last = e.matmul(ps[e2 * 64:e2 * 64 + 64, 0:256], lhsT=UT[0][:, dc, c_:c_ + 16 * 63 + 1:16],
                                                                rhs=wv[:, dc, :], start=(dc == 0), stop=(dc == 7))
                                        else:
                                            last = e.matmul(ps[:, 0:256], lhsT=cols(UT[0], slice(0, 128), dc, g, 0, ST, qb), rhs=wv[:, dc, :],
                                                            start=(dc == 0), stop=(dc == 7))
                                    return last

                                S.op("pe", f, reads=UK[0] + [wvk], writes=[pk])
                                S.op("act", lambda e: e.activation(out=VT[g][:, blk, :], in_=ps[:, 0:256], func=AF.Copy),
                                     reads=[pk], writes=[("VT", g, blk)])
                                if halo:
                                    pre_next(1)
                                elif cvt3 and not S.off:
                                    d_, s_src = cvt3.pop(0)
                                    S.dma("pool", d_, s_src, writes=["w3bf"])
                            chk("mixV%d%d" % (s, g))
                            if halo:
                                continue
                            for tt in range(2):
                                qk_proj(g, 0, tt, j0)
                            micro = []
                            if g == 2 and s + 1 < NT // ST and s >= 2 and not S.off:
                                micro = []
                                for tt in range(2):
                                    S.deferred = []
                                    rope_tables(s + 1, tt)
                                    m2, S.deferred = S.deferred, None
                                    pos_ = [k_ for k_, it in enumerate(m2) if it[0] == "op" and it[1][0] == "act"]
                                    for k_ in reversed(pos_):
                                        it = m2.pop(k_)
                                        m2.insert(min(len(m2), k_ + 6), it)
                                    micro += m2
                            kring = [("KT", g, c2, i) for c2 in range(2) for i in range(RING[g] // 512)]
                            qkeys = [("QT", c2, tt, hh) for c2 in range(2) for tt in range(2) for hh in range(2)] + [("QTz", h_) for h_ in range(4)]
                            units = []
                            for qb in range(8):
                                if g == 0:
                                    kbs = [(j0 + 128 * qb - 128, 0), (j0 + 128 * qb, 1)]
                                elif g == 1:
                                    kbs = [(j0 + 512 * (qb // 4) - 512, 0), (j0 + 512 * (qb // 4), 1)]
                                else:
                                    kbs = [(j0 - 2048, 2), (j0 - 1024, 3), (j0, 4)]
                                for ki_, (jb, mid) in enumerate(kbs):
                                    units.append((qb, ki_, len(kbs), jb, mid))
                            st_ = {}

                            def emit_qk(i):
                                qb, ki_, nk, jb, mid = units[i]
                                sp_, spk = pbank(0, 2)
                                if g == 0:
                                    kcol = lambda c2: KT[0][:, c2, jb % RING[0]:jb % RING[0] + 128]
                                    kb_blk = (jb // 128) % 12
                                elif g == 1:
                                    rho = qb % 4
                                    kcol = lambda c2: KT[1][:, c2, jb % RING[1] + rho:jb % RING[1] + rho + 509:4]
                                    kb_blk = ((jb // 512) % 3) * 4 + rho
                                else:
                                    kcol = lambda c2: KT[2][:, c2, jb % RING[2] + 128 * qb:jb % RING[2] + 128 * qb + 128]
                                    kb_blk = ((jb // ST) % 3) * 8 + qb

                                def fs(e):
                                    e.matmul(sp_[:, :], lhsT=ident[:], rhs=masks[:, mid, :], start=True, stop=False, skip_group_check=True)
                                    last = None
                                    for h in range(4):
                                        last = e.matmul(sp_[:, h * 128:(h + 1) * 128], lhsT=kcol(h // 2),
                                                        rhs=cols(QT, slice(0, 128), h, g, 0, ST, qb), start=False, stop=(h == 3),
                                                        skip_group_check=True)
                                    return last

                                S.op("pe", fs, reads=kring + qkeys + ["masks", "ident"], writes=[spk])
                                pt = PT[i % 2]
                                ptk = ("PT", i % 2)
                                if jb < HALO:
                                    S.op("act", lambda e: e.activation(out=pt[:], in_=sp_[:, :], func=AF.Exp, scale=0.125, bias=flg[:, 1:2]),
                                         reads=[spk, "flg"], writes=[ptk])
                                else:
                                    S.op("act", lambda e: e.activation(out=pt[:], in_=sp_[:, :], func=AF.Exp, scale=0.125),
                                         reads=[spk], writes=[ptk])
                                st_[i] = (pt, ptk, kb_blk)

                            def emit_pv(i):
                                qb, ki_, nk, jb, mid = units[i]
                                pt, ptk, kb_blk = st_.pop(i)
                                if ki_ == 0:
                                    st_["nd"] = pbank(2, 4)
                                nd, ndk = st_["nd"]
                                first, lastkb = ki_ == 0, ki_ == nk - 1

                                def fpv(e):
                                    last = None
                                    for h in range(4):
                                        po_ = slice((h % 2) * 64, (h % 2) * 64 + 64)
                                        last = e.matmul(nd[po_, (h // 2) * 128:(h // 2) * 128 + 128],
                                                        lhsT=VT[g][:, kb_blk, h * 64:(h + 1) * 64], rhs=pt[:, h * 128:(h + 1) * 128],
                                                        start=(first and h < 2), stop=False, skip_group_check=True)
                                    for h in range(4):
                                        po_ = slice((h % 2) * 64, (h % 2) * 64 + 64)
                                        last = e.matmul(nd[po_, 256 + (h // 2) * 128:256 + (h // 2) * 128 + 128], lhsT=onesb[:, 0:64],
                                                        rhs=pt[:, h * 128:(h + 1) * 128],
                                                        start=False, stop=(lastkb and h == 3), skip_group_check=True)
                                    return last

                                S.op("pe", fpv, reads=[ptk, ("VT", g, kb_blk), "onesb"], writes=[ndk])
                                if lastkb:
                                    if g < 2:
                                        if g == 0:
                                            dst_ap = acc[:, :, 128 * qb:128 * qb + 128]
                                        else:
                                            r0_ = 512 * (qb // 4) + qb % 4
                                            dst_ap = acc[:, :, r0_:r0_ + 509:4]
                                        src_ap = nd[:, :].rearrange("p (k q) -> p k q", k=4)
                                        if g == 0:
                                            S.op("dve", lambda e: e.tensor_copy(out=dst_ap, in_=src_ap), reads=[ndk], writes=["acc"])
                                        else:
                                            S.op("dve", lambda e: e.tensor_tensor(out=dst_ap, in0=dst_ap, in1=src_ap, op=ALU.add),
                                                 reads=[ndk, "acc"], writes=["acc"])
                                    else:
                                        for k4 in range(4):
                                            dst_ap = cols(acc, slice(0, 128), k4, g, 0, ST, qb)
                                            src_ap = nd[:, k4 * 128:(k4 + 1) * 128].rearrange("p (e m) -> p e m", e=2)
                                            S.op("dve", lambda e: e.tensor_tensor(out=dst_ap, in0=dst_ap, in1=src_ap, op=ALU.add),
                                                 reads=[ndk, "acc"], writes=["acc"])

                            emit_qk(0)
                            for i in range(len(units)):
                                if i + 1 < len(units):
                                    emit_qk(i + 1)
                                emit_pv(i)
                                for _ in range(2):
                                    if micro:
                                        kind, args = micro.pop(0)
                                        (S.op if kind == "op" else S.dma)(*args)
                            while micro:
                                kind, args = micro.pop(0)
                                (S.op if kind == "op" else S.dma)(*args)
                            if mode == "mixT%d%d" % (s, g) and not S.off:
                                for k_ in range(4):
                                    S.dma("sp", y[256 + k_ * 128:256 + (k_ + 1) * 128, :], acc[:, k_, :], reads=["acc"], writes=["ydbg"])
                            chk("mixT%d%d" % (s, g))
                        if halo:
                            pre_next(len(PRE))
                            continue
                        for tt in range(2):
                            dTt = dT if tt == 0 else junk[:].rearrange("p (a b) -> p a b", a=2)
                            for c2 in range(2):
                                ps, pk = pbank()
                                S.op("pe", lambda e: e.matmul(ps[:, :], lhsT=wpool[:, c2, :], rhs=dTt[:, c2, :], start=True, stop=True),
                                     reads=[("dT", tt, c2, 0), ("dT", tt, c2, 1), "wpool"], writes=[pk])
                                S.op("dve", lambda e: e.tensor_scalar(out=ypl[:, c2, tt * 512:(tt + 1) * 512], in0=ps[:, :],
                                                                      scalar1=psc[:, c2:c2 + 1], scalar2=None, op0=ALU.mult),
                                     reads=[pk, "psc"], writes=[("ypl", c2, tt)])
                        for tt in range(2):
                            tsl = slice(tt * 512, (tt + 1) * 512)
                            S.op("act", lambda e: e.activation(out=rD, in_=acc[:, 2:4, tsl], func=AF.Ln), reads=["acc"], writes=["rD", "ang", "kf"])
                            S.op("act", lambda e: e.activation(out=rD, in_=rD, func=AF.Exp, scale=-1.0), reads=["rD"], writes=["rD", "ang", "kf"])
                            S.op("dve", lambda e: e.tensor_tensor(out=yat[:, :, tsl], in0=acc[:, 0:2, tsl], in1=rD, op=ALU.mult),
                                 reads=["acc", "rD"], writes=[("yat", tt)])
                        if mode == "mixY%d" % s and not S.off:
                            for sl_ in range(2):
                                S.dma("pool", y[sl_ * 128:(sl_ + 1) * 128, :], yat[:, sl_, :], reads=[("yat", 0), ("yat", 1)], writes=["ydbg"])
                            for k_ in range(4):
                                S.dma("sp", y[256 + k_ * 128:256 + (k_ + 1) * 128, :], acc[:, k_, :], reads=["acc"], writes=["ydbg"])
                        chk("mixY%d" % s)
                        wpb, wpbk = None, None
                        for dp in range(4):
                            wgp, wgpk = wload(IT_GP + dp)
                            wga, wgak = wload(IT_GA + dp)
                            wbr, wbrk = wload(IT_BR + dp)
                            for di in range(2):
                                dc_o = dp * 2 + di
                                fs_ = slice(di * 128, (di + 1) * 128)
                                for tt in range(2):
                                    tsl = slice(tt * 512, (tt + 1) * 512)
                                    for br, (wg, wgk, src_, skey) in enumerate(((wgp, wgpk, ypl, "ypl"), (wga, wgak, yat, "yat"))):
                                        pg, pgk = proj_fm([wg[:, dc, fs_] for dc in range(8)], wgk, UK[0], tt)
                                        pbr, pbrk = pbank()

                                        def fbr(e):
                                            last = None
                                            for c2 in range(2):
                                                last = e.matmul(pbr[:, :], lhsT=wbr[:, br * 2 + c2, fs_], rhs=src_[:, c2, tsl],
                                                                start=(c2 == 0), stop=(c2 == 1))
                                            return last

                                        rk = [("ypl", c2, tt) for c2 in range(2)] if br == 0 else [("yat", tt)]
                                        S.op("pe", fbr, reads=rk + [wbrk], writes=[pbrk])
                                        S.op("act", lambda e: e.activation(out=sg[br][:], in_=pg[:, :], func=AF.Sigmoid),
                                             reads=[pgk], writes=[("sg", br)])
                                        if br == 0:
                                            S.op("dve", lambda e: e.tensor_tensor(out=tmp[:], in0=sg[0][:], in1=pbr[:, :], op=ALU.mult),
                                                 reads=[("sg", 0), pbrk], writes=["tmp"])
                                        else:
                                            S.op("dve", lambda e: e.tensor_tensor(out=tmp2[:], in0=sg[1][:], in1=pbr[:, :], op=ALU.mult),
                                                 reads=[("sg", 1), pbrk], writes=["tmp2"])
                                    S.op("dve", lambda e: e.tensor_tensor(out=merged[:, dc_o, tsl], in0=tmp[:], in1=tmp2[:], op=ALU.add),
                                         reads=["tmp", "tmp2", ("yat", 0), ("yat", 1)], writes=["acc"])
                        chk("mixF%d" % s)
                        nxt = s + 1 < NT // ST
                        if nxt:
                            pre_stats(s + 1, 0)
                        for j in range(8):
                            r0 = j0 - HALO + j * 128
                            for hf in range(2):
                                cs = slice(hf * 512, (hf + 1) * 512)
                                rb, rbk = ((sg[0], ("sg", 0)), (sg[1], ("sg", 1)), (tmp, "tmp"), (tmp2, "tmp2"))[(2 * j + hf) % 4]
                                S.dma("sp", rb[:], h1s[j0 + j * 128:j0 + (j + 1) * 128, cs], writes=[rbk])
                                wo, wok = WOq[hf]
                                ps, pk = pbank()

                                def f(e):
                                    last = None
                                    for dc in range(8):
                                        last = e.matmul(ps[:, :], lhsT=merged[:, dc, j * 128:(j + 1) * 128], rhs=wo[:, dc, :],
                                                        start=(dc == 0), stop=(dc == 7))
                                    return last

                                S.op("pe", f, reads=["acc", wok], writes=[pk])
                                S.op("dve", lambda e: e.tensor_tensor(out=rb[:], in0=ps[:, :], in1=rb[:], op=ALU.add),
                                     reads=[pk, rbk], writes=[rbk])
                                S.dma("pool", h2s[r0:r0 + 128, cs], rb[:], reads=[rbk], writes=[("dst2", j, hf)])
                            if nxt:
                                pre_sub(j // 4, j % 4)
                                if j == 3:
                                    pre_stats(s + 1, 1)

                        chk("mixG%d" % s)

                was_off = S.off
                rot_save = dict(rot)
                S.off = True
                run_tiles()
                S.off = was_off
                dry[0] = False
                wsr[0] = 0
                rot.clear()
                rot.update(rot_save)
                run_tiles()
                S.barrier()

        WOq = []
        try:
          with ExitStack() as st2:
            for hf in range(2):
                t_ = sb(st2, "WO%d" % hf, [128, 8, 512], BF16)
                WOq.append((t_, ("WO", hf)))
            mix_phase()
        except _Stop:
            return nc

        if mode == "full":
            ffn_phase(h2s, y, OWN, w3a, w3b, 6, 2, True, pre_bf=(w3a_bf, w3b_bf))
    return nc


def _host_consts():
    ident = np.eye(128, dtype=np.float32)
    c = np.arange(128)[:, None]
    a = np.arange(128)[None, :]
    m0 = np.where(c >= a, 0.0, NEG)
    m1 = np.where(c <= a, 0.0, NEG)
    ek, mk = c // 64, c % 64
    eq, mq = a // 64, a % 64
    same = ek == eq
    m2 = np.where(same & (mk >= mq), 0.0, NEG)
    m3 = np.where(same, 0.0, NEG)
    m4 = np.where(same & (mk <= mq), 0.0, NEG)
    masks = np.stack([np.tile(m, (1, 4)) for m in (m0, m1, m2, m3, m4)], axis=1).astype(np.float32)
    masks = masks.reshape(128, 5 * 512)
    inv = (np.float32(10000.0) ** (-(np.arange(0, 64, 2, dtype=np.float32)) / np.float32(64))).astype(np.float32)
    i = np.arange(128) % 64
    f = inv[i % 32]
    sgn = np.where(i < 32, -1.0, 1.0).astype(np.float32)
    ropef = np.stack([f * sgn, f], axis=1).astype(np.float32)
    return ident, masks, ropef


_NC_CACHE = {}


def kernel(**inputs):
    f32 = lambda a: np.ascontiguousarray(np.asarray(a, dtype=np.float32))
    x = f32(inputs["x"])
    c = f32(inputs["c"])
    positions = np.asarray(inputs["positions"]).astype(np.int32)
    ident, masks, ropef = _host_consts()
    w_in = f32(inputs["w_in"])[0]
    perm = np.concatenate([np.arange(h * 64 + 32, h * 64 + 64).tolist() + np.arange(h * 64, h * 64 + 32).tolist()
                           for h in range(24)]).astype(np.int64)
    pidx = np.arange(128)
    partner = (pidx // 64) * 64 + (pidx % 64 + 32) % 64
    permT = np.zeros((128, 128), np.float32)
    permT[partner, pidx] = 1.0
    w_pool = f32(inputs["w_pool"])[0]
    w_pool_bd = np.zeros((128, 2, 128), np.float32)
    for c2 in range(2):
        w_pool_bd[0:64, c2, 0:64] = w_pool[2 * c2]
        w_pool_bd[64:128, c2, 64:128] = w_pool[2 * c2 + 1]
    gTm = np.concatenate([f32(inputs[k])[0].reshape(8, 128).T for k in ("g_norm_ffn1", "g_norm_mix", "g_norm_ffn2")], axis=1)
    b_ada = f32(inputs["b_ada"])
    shared = {
        "w_ada": f32(inputs["w_ada"])[0], "b_ada": b_ada, "b_adaT": np.ascontiguousarray(b_ada[0].reshape(72, 128).T),
        "gT": np.ascontiguousarray(gTm), "g_final": f32(inputs["g_final"]).reshape(1, D),
        "w_ffn1_in": f32(inputs["w_ffn1_in"])[0], "w_ffn1_out": f32(inputs["w_ffn1_out"])[0],
        "w_ffn2_in": f32(inputs["w_ffn2_in"])[0], "w_ffn2_out": f32(inputs["w_ffn2_out"])[0],
        "w_in": w_in, "permT": permT, "w_pool_bd": w_pool_bd,
        "pool_scaleT": np.ascontiguousarray(f32(inputs["pool_scale"])[0].reshape(2, 128).T),
        "w_pb": f32(inputs["w_pool_branch"])[0], "w_ab": f32(inputs["w_attn_branch"])[0], "w_o": f32(inputs["w_out"])[0],
        "ident": ident, "masks": masks, "ropef": ropef,
    }
    wins = np.array([2, 4, 8, 16], np.float32)
    in_maps = []
    for core in range(8):
        b, half = core // 2, core % 2
        s0 = half * OWN
        xc = np.zeros((NT, D), np.float32)
        pc = np.zeros((1, NT), np.int32)
        xc[HALO:] = x[b, s0:s0 + OWN]
        pc[0, HALO:] = positions[b, s0:s0 + OWN]
        if half == 1:
            xc[:HALO] = x[b, s0 - HALO:s0]
            pc[0, :HALO] = positions[b, s0 - HALO:s0]
        flags = np.zeros((128, 2), np.float32)
        flags[:, 0] = float(half)
        flags[:, 1] = 0.0 if half else NEG
        rc = np.zeros((128, 32), np.float32)
        for p in range(128):
            for c2 in range(2):
                w = wins[2 * c2 + p // 64]
                t = np.arange(16, dtype=np.float32)
                rc[p, c2 * 16:(c2 + 1) * 16] = 1.0 / (np.minimum(t + 1, w) if half == 0 else w)
        m = dict(shared)
        m.update({"x": xc, "pos": pc, "cT": np.ascontiguousarray(c[b].reshape(8, 128).T), "flags": flags, "rcnt": rc})
        in_maps.append(m)
    dbg = os.environ.get("MK_DBG", "")
    if dbg:
        return in_maps
    if "nc" not in _NC_CACHE:
        _NC_CACHE["nc"] = build_program()
    res = run_bass_kernel_spmd(_NC_CACHE["nc"], in_maps, core_ids=list(range(8)))
    out = np.zeros((4, 2 * OWN, D), np.float32)
    for core in range(8):
        b, half = core // 2, core % 2
        out[b, half * OWN:(half + 1) * OWN] = res.results[core]["y"]
    return out
```
